# Optimizing a Trainium2 kernel written in Bass

```python
import math
import jax, jax.numpy as jnp
from jax import lax
import numpy as np

D_MODEL = 1024
BATCH = 8
SEQ = 4096
DEPTH = 2

PL_DIM = 256
N_EVEN = (DEPTH + 1) // 2
N_ODD = DEPTH // 2
CONV_A_WIDTH = D_MODEL
CONV_A_KERNEL = 3
GDN_HEADS = 8
GDN_HEAD_DIM = 128
GDN_WIDTH = GDN_HEADS * GDN_HEAD_DIM
GDN_CONV_KERNEL = 4
GDN_CHUNK = 64
HGRN_HEAD_DIM = 128
HGRN_WIDTH = 2 * D_MODEL
HGRN_HEADS = HGRN_WIDTH // HGRN_HEAD_DIM
HGRN_CHUNK = 32
EVEN_IN = 4 * CONV_A_WIDTH + 4 * GDN_WIDTH + 2 * GDN_HEADS
EVEN_MIX = CONV_A_WIDTH + GDN_WIDTH
ODD_IN = 4 * HGRN_WIDTH
ODD_MIX = HGRN_WIDTH
DEEPNORM_ALPHA = (2.0 * DEPTH) ** 0.25
DEEPNORM_BETA = (8.0 * DEPTH) ** -0.25
NORM_EPS = 1e-5

kernel_name = "hybrid_shortconv_gdn_hgrn2_deepnorm"


def layer_norm(x, g, b):
    xf = x.astype(jnp.float32)
    mu = jnp.mean(xf, axis=-1, keepdims=True)
    var = jnp.mean(jnp.square(xf - mu), axis=-1, keepdims=True)
    return ((xf - mu) * lax.rsqrt(var + NORM_EPS) * g.astype(jnp.float32) + b.astype(jnp.float32)).astype(x.dtype)


def rms_norm(x, g):
    xf = x.astype(jnp.float32)
    y = xf * lax.rsqrt(jnp.mean(jnp.square(xf), axis=-1, keepdims=True) + NORM_EPS)
    return (y * g.astype(jnp.float32)).astype(x.dtype)


def l2_normalize(x):
    xf = x.astype(jnp.float32)
    return (xf * lax.rsqrt(jnp.sum(jnp.square(xf), axis=-1, keepdims=True) + 1e-6)).astype(x.dtype)


def causal_depthwise_conv(x, w):
    k, c = w.shape
    return lax.conv_general_dilated(
        x, w[:, None, :].astype(x.dtype), window_strides=(1,), padding=[(k - 1, 0)],
        dimension_numbers=("NWC", "WIO", "NWC"), feature_group_count=c)


def _to_chunks(t, c):
    b, s, h, d = t.shape
    return t.reshape(b, s // c, c, h, d).transpose(0, 3, 1, 2, 4)


def _from_chunks(t):
    b, h, n, c, d = t.shape
    return t.transpose(0, 2, 3, 1, 4).reshape(b, n * c, h, d)


def gated_delta_rule(q, k, v, g, beta):
    dtype = v.dtype
    c = GDN_CHUNK
    dk, dv = q.shape[-1], v.shape[-1]
    q = _to_chunks(q.astype(jnp.float32) * (dk ** -0.5), c)
    k = _to_chunks(k.astype(jnp.float32), c)
    v = _to_chunks(v.astype(jnp.float32), c)
    g = _to_chunks(g.astype(jnp.float32)[..., None], c)[..., 0]
    beta = _to_chunks(beta.astype(jnp.float32)[..., None], c)
    gc = jnp.cumsum(g, axis=-1)
    incl = jnp.tril(jnp.ones((c, c), bool))
    strict = jnp.tril(jnp.ones((c, c), bool), -1)
    decay = jnp.exp(jnp.where(incl, gc[..., :, None] - gc[..., None, :], -jnp.inf))
    kb = k * beta
    low = jnp.where(strict, jnp.einsum("bhnid,bhnjd->bhnij", kb, k) * decay, 0.0)
    rhs = jnp.concatenate([v * beta, kb * jnp.exp(gc)[..., None]], axis=-1)
    sol = lax.linalg.triangular_solve(low + jnp.eye(c, dtype=jnp.float32), rhs,
                                      left_side=True, lower=True, unit_diagonal=True)
    u, w = sol[..., :dv], sol[..., dv:]
    attn = jnp.einsum("bhnid,bhnjd->bhnij", q, k) * decay
    q_dec = q * jnp.exp(gc)[..., None]
    g_last = gc[..., -1]
    k_dec = k * jnp.exp(g_last[..., None] - gc)[..., None]

    def step(state, xs):
        q_i, k_i, u_i, w_i, a_i, gl_i = xs
        v_new = u_i - jnp.einsum("bhcd,bhde->bhce", w_i, state)
        o_i = jnp.einsum("bhcd,bhde->bhce", q_i, state) + jnp.einsum("bhij,bhje->bhie", a_i, v_new)
        state = state * jnp.exp(gl_i)[..., None, None] + jnp.einsum("bhcd,bhce->bhde", k_i, v_new)
        return state, o_i

    mv = lambda t: jnp.moveaxis(t, 2, 0)
    s0 = jnp.zeros(q.shape[:2] + (dk, dv), jnp.float32)
    _, o = lax.scan(step, s0, (mv(q_dec), mv(k_dec), mv(u), mv(w), mv(attn), mv(g_last)))
    return _from_chunks(jnp.moveaxis(o, 0, 2)).astype(dtype)


def hgrn2_recurrence(q, k, v, logf):
    dtype = v.dtype
    c = HGRN_CHUNK
    q, k, v, logf = (jnp.moveaxis(_to_chunks(t.astype(jnp.float32), c), 2, 0) for t in (q, k, v, logf))
    b = jnp.cumsum(logf, axis=-2)
    incl = jnp.tril(jnp.ones((c, c), bool))[:, :, None]

    def step(state, xs):
        q_i, k_i, v_i, b_i = xs
        b_last = b_i[..., -1:, :]
        decay = jnp.exp(jnp.where(incl, b_i[..., :, None, :] - b_i[..., None, :, :], -jnp.inf))
        attn = jnp.einsum("bhtd,bhsd,bhtsd->bhts", q_i, k_i, decay)
        o_i = (jnp.einsum("bhtd,bhde->bhte", q_i * jnp.exp(b_i), state)
               + jnp.einsum("bhts,bhse->bhte", attn, v_i))
        state = (state * jnp.exp(b_last)[..., 0, :, None]
                 + jnp.einsum("bhsd,bhse->bhde", k_i * jnp.exp(b_last - b_i), v_i))
        return state, o_i

    s0 = jnp.zeros(q.shape[1:3] + (q.shape[-1], v.shape[-1]), jnp.float32)
    _, o = lax.scan(step, s0, (q, k, v, b))
    return _from_chunks(jnp.moveaxis(o, 0, 2)).astype(dtype)


def conv_gdn_mixer(x, w_in, conv_a_w, conv_b_w, a_log, dt_bias, gdn_norm_g, w_out):
    bsz, s, _ = x.shape
    wa, wg, h = CONV_A_WIDTH, GDN_WIDTH, GDN_HEADS
    proj = x @ w_in
    cuts = [wa, 2 * wa, 3 * wa, 4 * wa, 4 * wa + 3 * wg, 4 * wa + 4 * wg, 4 * wa + 4 * wg + h]
    h_a, c_a, b_a, z_a, qkv, z_b, beta_raw, a_raw = jnp.split(proj, cuts, axis=-1)
    y_a = b_a * causal_depthwise_conv(c_a * h_a, conv_a_w) * jax.nn.silu(z_a)
    qkv = jax.nn.silu(causal_depthwise_conv(qkv, conv_b_w))
    q, k, v = (t.reshape(bsz, s, h, GDN_HEAD_DIM) for t in jnp.split(qkv, 3, axis=-1))
    q, k = l2_normalize(q), l2_normalize(k)
    beta = jax.nn.sigmoid(beta_raw.astype(jnp.float32))
    g = -jnp.exp(a_log.astype(jnp.float32)) * jax.nn.softplus(a_raw.astype(jnp.float32) + dt_bias.astype(jnp.float32))
    o = gated_delta_rule(q, k, v, g, beta)
    o = rms_norm(o, gdn_norm_g) * jax.nn.silu(z_b.reshape(bsz, s, h, GDN_HEAD_DIM))
    y = jnp.concatenate([y_a, o.reshape(bsz, s, wg)], axis=-1)
    return y @ w_out


def hgrn2_mixer(x, w_in, lower_bound, hgrn_norm_g, w_out):
    bsz, s, _ = x.shape
    q_raw, f_raw, i_in, z = jnp.split(x @ w_in, 4, axis=-1)
    f = lower_bound + (1.0 - lower_bound) * jax.nn.sigmoid(f_raw.astype(jnp.float32))
    q = jax.nn.silu(q_raw)
    k = (1.0 - f).astype(x.dtype)
    logf = jnp.log(f)
    heads = lambda t: t.reshape(bsz, s, HGRN_HEADS, HGRN_HEAD_DIM)
    o = hgrn2_recurrence(heads(q), heads(k), heads(i_in), heads(logf))
    o = rms_norm(o, hgrn_norm_g) * jax.nn.silu(heads(z))
    return o.reshape(bsz, s, HGRN_WIDTH) @ w_out


def setup_inputs(seed: int = 0) -> dict:
    key = jax.random.key(seed)
    ks = jax.random.split(key, 20)
    nrm = lambda k, shape, scale: jax.random.normal(k, shape, jnp.float32) * scale
    dt = jnp.exp(jax.random.uniform(ks[6], (N_EVEN, GDN_HEADS), jnp.float32)
                 * (math.log(0.1) - math.log(0.001)) + math.log(0.001))
    return {
        "x": nrm(ks[0], (BATCH, SEQ, D_MODEL), 1.0),
        "p": nrm(ks[1], (DEPTH, BATCH, SEQ, PL_DIM), 1.0),
        "w_in_even": nrm(ks[2], (N_EVEN, D_MODEL, EVEN_IN), D_MODEL ** -0.5),
        "conv_a_w": nrm(ks[3], (N_EVEN, CONV_A_KERNEL, CONV_A_WIDTH), CONV_A_KERNEL ** -0.5),
        "conv_b_w": nrm(ks[4], (N_EVEN, GDN_CONV_KERNEL, 3 * GDN_WIDTH), GDN_CONV_KERNEL ** -0.5),
        "a_log": jnp.log(jax.random.uniform(ks[5], (N_EVEN, GDN_HEADS), jnp.float32, 1.0, 16.0)),
        "dt_bias": dt + jnp.log(-jnp.expm1(-dt)),
        "gdn_norm_g": 1.0 + nrm(ks[7], (N_EVEN, GDN_HEAD_DIM), 0.02),
        "w_out_even": nrm(ks[8], (N_EVEN, EVEN_MIX, D_MODEL), EVEN_MIX ** -0.5 * DEEPNORM_BETA),
        "w_in_odd": nrm(ks[9], (N_ODD, D_MODEL, ODD_IN), D_MODEL ** -0.5),
        "lower_bounds": nrm(ks[10], (DEPTH, HGRN_WIDTH), 0.1),
        "hgrn_norm_g": 1.0 + nrm(ks[11], (N_ODD, HGRN_HEAD_DIM), 0.02),
        "w_out_odd": nrm(ks[12], (N_ODD, ODD_MIX, D_MODEL), ODD_MIX ** -0.5 * DEEPNORM_BETA),
        "ln_g": 1.0 + nrm(ks[13], (DEPTH, D_MODEL), 0.02),
        "ln_b": nrm(ks[14], (DEPTH, D_MODEL), 0.02),
        "w_pl": nrm(ks[15], (DEPTH, PL_DIM, D_MODEL), PL_DIM ** -0.5),
        "w_pl_gate": nrm(ks[16], (DEPTH, D_MODEL, D_MODEL), D_MODEL ** -0.5),
    }


def reference(x, p, w_in_even, conv_a_w, conv_b_w, a_log, dt_bias, gdn_norm_g, w_out_even,
              w_in_odd, lower_bounds, hgrn_norm_g, w_out_odd, ln_g, ln_b, w_pl, w_pl_gate):
    lbs = jnp.cumsum(jax.nn.softmax(lower_bounds.astype(jnp.float32), axis=0), axis=0)
    lbs = lbs - lbs[0]
    for i in range(DEPTH):
        j = i // 2
        if i % 2 == 0:
            s = conv_gdn_mixer(x, w_in_even[j], conv_a_w[j], conv_b_w[j], a_log[j], dt_bias[j],
                               gdn_norm_g[j], w_out_even[j])
        else:
            s = hgrn2_mixer(x, w_in_odd[j], lbs[i], hgrn_norm_g[j], w_out_odd[j])
        x = layer_norm(DEEPNORM_ALPHA * x + s, ln_g[i], ln_b[i])
        gate = jax.nn.sigmoid((x @ w_pl_gate[i]).astype(jnp.float32))
        x = x + ((p[i] @ w_pl[i]).astype(jnp.float32) * gate).astype(x.dtype)
    return x
```

```python
import numpy as np
from contextlib import ExitStack
import concourse.bass as bass
import concourse.mybir as mybir
from concourse.bass_utils import run_bass_kernel_spmd

F32 = mybir.dt.float32
BF16 = mybir.dt.bfloat16
AF = mybir.ActivationFunctionType
ALU = mybir.AluOpType

D = 1024
TT = 512
NB = TT // 128
ALPHA = 4.0 ** 0.25
LN_EPS = 1e-5


class Op:
    __slots__ = ("eng", "fn", "reads", "writes", "chan", "ninc", "deps", "signal", "count", "idx")

    def __init__(self, eng, fn, reads, writes, chan, ninc):
        self.eng, self.fn, self.reads, self.writes, self.chan, self.ninc = eng, fn, reads, writes, chan, ninc
        self.deps = set()
        self.signal = chan is not None
        self.count = 0


class Prog:
    def __init__(self, nc):
        self.nc = nc
        self.ops = []
        self.nchan = 0
        self.lines = None

    def chan(self):
        self.nchan += 1
        return ("chan", self.nchan)

    def add(self, eng, fn, reads=(), writes=(), chan=None, ninc=1):
        op = Op(eng, fn, tuple(reads), tuple(writes), chan, ninc)
        op.idx = len(self.ops)
        if self.lines is not None:
            import sys as _s
            f = _s._getframe(1)
            while f.f_code.co_name in ("add", "MM", "TR", "ACT", "TS", "TTN", "STT", "CP", "DMA"):
                f = f.f_back
            self.lines.append(f.f_lineno)
        self.ops.append(op)
        return op

    def finish(self):
        ops = self.ops
        lastw, lastr = {}, {}

        def ch(o):
            return o.chan if o.chan is not None else o.eng

        def bank_of(k):
            if isinstance(k, tuple) and k[0] == "slot":
                return k[1]
            if isinstance(k, str) and k.startswith("bank"):
                return int(k[4:])
            return None

        bank_rd = {}
        bank_wr = {}
        for o in ops:
            deps = set()
            my = ch(o)
            for k in o.reads:
                deps.update(lastw.get(k, {}).values())
                b = bank_of(k)
                if b is not None:
                    deps.update(j for c, j in bank_rd.get(b, {}).items() if c != my)
                    bank_rd.setdefault(b, {})[my] = o.idx
                    if b in bank_wr:
                        deps.add(bank_wr[b])
            for k in o.writes:
                deps.update(lastw.get(k, {}).values())
                deps.update(lastr.get(k, {}).values())
                b = bank_of(k)
                if b is not None and o.eng == "pe":
                    deps.update(bank_rd.get(b, {}).values())
                    bank_wr[b] = o.idx
            deps.discard(o.idx)
            for j in deps:
                p = ops[j]
                if p.chan is None and o.chan is None and p.eng == o.eng:
                    if o.eng == "pe":
                        continue
                o.deps.add(j)
                p.signal = True
            for k in o.reads:
                lastr.setdefault(k, {})[my] = o.idx
            for k in o.writes:
                lastw[k] = {my: o.idx}
                lastr[k] = {}
        cnt = {}
        for o in ops:
            if o.signal:
                c = ch(o)
                cnt[c] = cnt.get(c, 0) + (16 * o.ninc if o.chan is not None else 1)
                o.count = cnt[c]
        self.cnt = cnt

    def emit(self, tail_eng="sp"):
        nc, ops, cnt = self.nc, self.ops, self.cnt
        chans = sorted(cnt.keys(), key=str)
        with ExitStack() as es:
            sems = {c: es.enter_context(nc.semaphore("s_" + (c if isinstance(c, str) else "c%d" % c[1])))
                    for c in chans}
            block = es.enter_context(nc.Block())

            def ch(o):
                return o.chan if o.chan is not None else o.eng

            def run(engname, e):
                waited = {}
                for o in ops:
                    if o.eng != engname:
                        continue
                    need = {}
                    for j in o.deps:
                        p = ops[j]
                        c = ch(p)
                        if p.count > need.get(c, 0):
                            need[c] = p.count
                    for c, v in need.items():
                        if waited.get(c, 0) >= v:
                            continue
                        e.wait_ge(sems[c], v)
                        waited[c] = v
                    ins = o.fn(e)
                    if self.lines is not None and o.chan is None:
                        ins.annotate("L%d" % self.lines[o.idx])
                    if o.signal:
                        if o.chan is not None:
                            for i_ in ins:
                                i_.then_inc(sems[o.chan], 16)
                        else:
                            ins.then_inc(sems[o.eng], 1)
                if engname == tail_eng:
                    for c, v in cnt.items():
                        if waited.get(c, 0) < v:
                            e.wait_ge(sems[c], v)

            block.sync(lambda e: run("sp", e))
            block.tensor(lambda e: run("pe", e))
            block.scalar(lambda e: run("act", e))
            block.vector(lambda e: run("dve", e))
            block.gpsimd(lambda e: run("pool", e))


def build(S=4096, n_layers=2, max_ops=None, marks=None):
    NT = S // TT
    nc = bass.Bass("TRN2", target_bir_lowering=False)

    def din(name, shape):
        return nc.dram_tensor(name, shape, F32, kind="ExternalInput").ap()

    x_d = din("x", [S, D])
    p_d = din("p", [2, S, 256])
    wie_d = din("w_in_even", [D, 8208])
    cwa_d = din("conv_a_w", [3, 1024])
    cwb_d = din("conv_b_w", [4, 3072])
    alog_d = din("a_log", [1, 8])
    dtb_d = din("dt_bias", [1, 8])
    gng_d = din("gdn_norm_g", [1, 128])
    woe_d = din("w_out_even", [2048, D])
    wio_d = din("w_in_odd", [D, 8192])
    lb_d = din("lower_bounds", [2, 2048])
    hng_d = din("hgrn_norm_g", [1, 128])
    woo_d = din("w_out_odd", [2048, D])
    lng_d = din("ln_g", [2, D])
    lnb_d = din("ln_b", [2, D])
    wpl_d = din("w_pl", [2, 256, D])
    wg_d = din("w_pl_gate", [2, D, D])
    out_d = nc.dram_tensor("out", [S, D], F32, kind="ExternalOutput").ap()

    def dscr(name, shape):
        return nc.dram_tensor(name, shape, BF16, kind="Internal").ap()

    wu_bf = [dscr("wu_even", [16, D, 512]), dscr("wu_odd", [16, D, 512])]
    wba_bf = dscr("wba_bf", [D, 16])
    wo_bf = [dscr("wo_even", [2048, D]), dscr("wo_odd", [2048, D])]
    wg_bf = dscr("wg_bf", [2, D, D])
    wpl_bf = dscr("wpl_bf", [2, 256, D])

    P = Prog(nc)
    if marks is not None:
        P.lines = []
        marks.append(P.lines)
    es = ExitStack()
    tiles = {}

    def T(name, shape, dt=F32):
        if name not in tiles:
            tiles[name] = es.enter_context(nc.sbuf_tensor(name, shape, dt))
        return tiles[name]

    psum_all = es.enter_context(nc.psum_tensor("psum_all", [128, 8, 512], F32))
    banks = [psum_all[:, i, :] for i in range(8)]

    def bk(i):
        return "bank%d" % i

    def MM(out, lhsT, rhs, r, w, start=True, stop=True):
        P.add("pe", lambda e: e.matmul(out, lhsT=lhsT, rhs=rhs, start=start, stop=stop), r, w)

    def TR(out, in_, ident, r, w):
        P.add("pe", lambda e: e.transpose(out, in_, ident), r, w)

    def ACT(out, in_, func, r, w, **kw):
        P.add("act", lambda e: e.activation(out=out, in_=in_, func=func, **kw), r, w)

    def TS(eng, out, in0, s1, s2, op0, op1, r, w):
        if op1 is None:
            P.add(eng, lambda e: e.tensor_scalar(out, in0, s1, None, op0), r, w)
        else:
            P.add(eng, lambda e: e.tensor_scalar(out, in0, s1, s2, op0, op1), r, w)

    def TTN(eng, out, in0, in1, op, r, w):
        P.add(eng, lambda e: e.tensor_tensor(out, in0, in1, op), r, w)

    def STT(out, in0, scalar, in1, op0, op1, r, w):
        P.add("dve", lambda e: e.scalar_tensor_tensor(out, in0, scalar, in1, op0, op1), r, w)

    def CP(eng, out, in_, r, w):
        if eng == "act":
            ACT(out, in_, AF.Copy, r, w)
        else:
            P.add(eng, lambda e: e.tensor_copy(out, in_), r, w)

    def DMA(eng, out, in_, chan, r, w, **kw):
        P.add(eng, lambda e: [e.dma_start(out=out, in_=in_, **kw)], r, w, chan=chan)

    ones = T("ones", [128, 128])
    ident = T("ident", [128, 128])
    ident_b = T("ident_b", [128, 128], BF16)
    m_up = T("m_up", [128, 128])
    m_bd = T("m_bd", [128, 128])
    m_bd_b = T("m_bd_b", [128, 128], BF16)
    onec_b = T("onec_b", [128, 1], BF16)
    mhalf = T("mhalf", [128, 1])
    P.add("pool", lambda e: e.memset(ones[:], 1.0), (), ["ones"])
    P.add("pool", lambda e: e.memset(onec_b[:], 1.0), (), ["onec_b"])
    P.add("pool", lambda e: e.memset(mhalf[:], -0.5), (), ["mhalf"])
    P.add("pool", lambda e: e.affine_select(out=ident[:], in_=ones[:], pattern=[[-1, 128]], compare_op=ALU.is_equal,
                                            fill=0.0, base=0, channel_multiplier=1), ["ones"], ["ident"])
    P.add("pool", lambda e: e.affine_select(out=m_up[:], in_=ones[:], pattern=[[1, 128]], compare_op=ALU.is_ge,
                                            fill=0.0, base=0, channel_multiplier=-1), ["ones"], ["m_up"])
    P.add("pool", lambda e: e.tensor_copy(ident_b[:], ident[:]), ["ident"], ["ident_b"])
    P.add("pool", lambda e: e.tensor_copy(m_bd[:], m_up[:]), ["m_up"], ["m_bd"])
    P.add("pool", lambda e: e.memset(m_bd[0:64, 64:128], 0.0), ["m_bd"], ["m_bd"])
    P.add("pool", lambda e: e.tensor_copy(m_bd_b[:], m_bd[:]), ["m_bd"], ["m_bd_b"])

    c_const = P.chan()
    lnp = T("lnp", [128, 2, D])
    c_lnp = P.chan()
    xres = T("xres", [128, NB, D])
    yT = T("yT", [128, 16, TT], BF16)
    xres_f = xres[:].rearrange("p b d -> p (b d)")
    yT_f = yT[:].rearrange("p c n -> p (c n)").bitcast(F32)
    lbb = xres_f[:, 0:4096].rearrange("p (a d) -> p a d", a=2)
    nhoml = T("nhoml", [128, 2048])
    aab = T("aab", [128, 16])
    cwr = yT_f[0:4, 0:4096]
    cw = T("cw", [128, 32, 4])
    P.add("sp", lambda e: [e.dma_start(out=xres_f[:, 0:4096],
                                       in_=lb_d.rearrange("a d -> (a d)").partition_broadcast(128)),
                           e.dma_start(out=aab[:, 0:8], in_=alog_d.rearrange("a d -> (a d)").partition_broadcast(128)),
                           e.dma_start(out=aab[:, 8:16], in_=dtb_d.rearrange("a d -> (a d)").partition_broadcast(128)),
                           e.dma_start(out=cwr[0:3, 0:1024], in_=cwa_d),
                           e.dma_start(out=cwr[0:4, 1024:4096], in_=cwb_d),
                           ], (), ["xres", "yT", "aab"], chan=c_const, ninc=5)
    nc_allow = nc.allow_non_contiguous_dma(reason="tiny per-partition vectors")
    nc_allow.__enter__()
    TTN("dve", lbb[:, 0, :], lbb[:, 1, :], lbb[:, 0, :], ALU.subtract, ["xres"], ["xres"])
    ACT(lbb[:, 1, :], lbb[:, 0, :], AF.Tanh, ["xres"], ["xres"], scale=0.5)
    TS("dve", nhoml[:], lbb[:, 1, :], 0.25, -0.25, ALU.mult, ALU.add, ["xres"], ["nhoml"])
    ACT(aab[:, 0:8], aab[:, 0:8], AF.Exp, ["aab"], ["aab"])
    TS("dve", aab[:, 0:8], aab[:, 0:8], -1.0, None, ALU.mult, None, ["aab"], ["aab"])
    cwv = banks[0][:, 0:128].rearrange("p (g k) -> p g k", k=4)
    for g in range(32):
        ntap = 3 if g < 8 else 4
        TR(cwv[:, g, 0:ntap], cwr[0:ntap, g * 128:(g + 1) * 128], ident[0:ntap, 0:ntap], ["yT", "ident"], [bk(0)])
    P.add("dve", lambda e: e.memset(cw[:], 0.0), (), ["cw"])
    CP("dve", cw[:, 0:8, 0:3], cwv[:, 0:8, 0:3], [bk(0)], ["cw"])
    CP("dve", cw[:, 8:32, :], cwv[:, 8:32, :], [bk(0)], ["cw"])

    if marks is not None:
        marks.append(('consts_end', len(P.ops)))
    c_cast = {}

    def cast_unit(layer, u):
        ch = P.chan()
        c_cast[(layer, u)] = ch
        if layer == 0:
            src = wie_d[:, 0:4096] if u < 8 else wie_d[:, 4096:8192]
            g = u % 8
            sv = src.rearrange("r (m g c) -> r m g c", m=4, g=8, c=128)[:, :, g, :]
        else:
            sv = wio_d.rearrange("r (m g c) -> r m g c", m=4, g=16, c=128)[:, :, u, :]
        dv = wu_bf[layer][u].rearrange("r (m c) -> r m c", m=4)
        P.add("pool", lambda e: [e.dma_start(out=dv[0:512], in_=sv[0:512]),
                                 e.dma_start(out=dv[512:1024], in_=sv[512:1024])],
              (), [("wu", layer, u)], chan=ch, ninc=2)

    def cast_plain(dst, src, key, nsplit):
        ch = P.chan()
        rows = src.shape[0]
        step = rows // nsplit
        P.add("pool", lambda e: [e.dma_start(out=dst[i * step:(i + 1) * step], in_=src[i * step:(i + 1) * step])
                                 for i in range(nsplit)], (), [key], chan=ch, ninc=nsplit)

    ep_big = T("ep_big", [128, 2, D])
    pst = T("pst", [128, NB, 256])
    gb4 = [T("gb4_0", [128, NB, 128]), T("gb4_1", [128, NB, 128])]
    c_gb = P.chan()
    P.add("sp", lambda e: [e.dma_start(out=gb4[l][:, b, :], in_=(gng_d, hng_d)[l][0].partition_broadcast(128))
                           for l in range(2) for b in range(NB)], (), ["gb4"], chan=c_gb, ninc=2 * NB)

    cast_unit(0, 0)
    cast_plain(wba_bf, wie_d[:, 8192:8208], "wba", 1)
    for u in range(1, 16):
        cast_unit(0, u)
    cast_plain(wo_bf[0], woe_d, ("wo", 0), 4)
    cast_plain(wg_bf[0], wg_d[0], ("wg", 0), 2)
    cast_plain(wpl_bf[0], wpl_d[0], ("wpl", 0), 1)
    if n_layers > 1:
        for u in range(16):
            cast_unit(1, u)
        cast_plain(wo_bf[1], woo_d, ("wo", 1), 4)
        cast_plain(wg_bf[1], wg_d[1], ("wg", 1), 2)
        cast_plain(wpl_bf[1], wpl_d[1], ("wpl", 1), 1)

    if marks is not None:
        marks.append(('casts_end', len(P.ops)))
    xT = T("xT", [128, 8, TT], BF16)
    Wh = [T("Wh0", [128, 8, 512], BF16), T("Wh1", [128, 8, 512], BF16)]
    wba = T("wba", [128, 8, 16], BF16)
    wpl = T("wpl", [128, 2, D], BF16)
    woutq = [T("woutq0", [128, 16, 256], BF16), T("woutq1", [128, 16, 256], BF16)]
    c_woutq = [P.chan(), P.chan()]
    pT = T("pT", [128, 2, TT], BF16)
    Sg = T("Sg", [128, 8, 128])
    Sg_b = T("Sg_b", [128, 8, 128], BF16)
    Sh = T("Sh", [128, 16, 128])
    Sh_b = T("Sh_b", [128, 16, 128], BF16)
    hist_a = T("hist_a", [128, 8, 2], BF16)
    hist_b = T("hist_b", [128, 24, 3], BF16)
    for t_, k_ in ((Sg, "Sg"), (Sg_b, "Sg_b"), (Sh, "Sh"), (Sh_b, "Sh_b"), (hist_a, "hist_a"), (hist_b, "hist_b")):
        P.add("pool", lambda e, t_=t_: e.memset(t_[:], 0.0), (), [k_])
    for h in range(8):
        pass
    c_x, c_p, c_wh, c_wout, c_wg, c_wplc, c_wbac, c_out = (P.chan(), P.chan(), [P.chan(), P.chan()], P.chan(),
                                                         P.chan(), P.chan(), P.chan(), P.chan())
    DMA("sp", wba[:], wba_bf.rearrange("(c p) n -> p c n", p=128), c_wbac, ["wba"], ["wba_sb"])

    unit_ctr = [0]

    def load_unit(layer, u):
        par = unit_ctr[0] % 2
        unit_ctr[0] += 1
        DMA("sp", Wh[par][:], wu_bf[layer][u].rearrange("(c p) n -> p c n", p=128), c_wh[par],
            [("wu", layer, u)], [("Wh", par)])
        return par

    bslot_ctr = {}

    def bslot(bank, bf=False):
        q = bslot_ctr.get(bank, 0) % 4
        bslot_ctr[bank] = bslot_ctr.get(bank, 0) + 1
        key = ("slot", bank, q)
        if bf:
            return banks[bank][:].bitcast(BF16)[:, q * 256:q * 256 + 128], key
        return banks[bank][:, q * 128:(q + 1) * 128], key

    def make_xT(rot):
        for c in range(8):
            b_ = 4 + (c + rot) % 4
            for b in range(NB):
                TR(banks[b_][:, b * 128:(b + 1) * 128], xres[:, b, c * 128:(c + 1) * 128], ident[:],
                   ["xres", "ident"], [bk(b_)])
            CP("act" if c % 2 == 0 else "dve", xT[:, c, :], banks[b_][:], [bk(b_)], ["xT"])

    def epilogue(layer, t):
        st6s = [T("ep_st6_%d" % b, [128, 2, 6]) for b in range(NB)]
        mvs = [T("ep_mv_%d" % b, [128, 2]) for b in range(NB)]
        rss = [T("ep_rs_%d" % b, [128, 2]) for b in range(NB)]
        DMA("sp", wpl[:], wpl_bf[layer].rearrange("(c p) n -> p c n", p=128), c_wplc, [("wpl", layer)], ["wpl"])
        DMA("sp", pst[:], p_d[layer, t * TT:(t + 1) * TT, :].rearrange("(b p) f -> p b f", p=128), c_p, [], ["pst"])
        wov = wo_bf[layer].rearrange("(c p) n -> p c n", p=128)
        P.add("sp", lambda e: [e.dma_start(out=lnp[:, 0, :], in_=lng_d[layer].partition_broadcast(128)),
                               e.dma_start(out=lnp[:, 1, :], in_=lnb_d[layer].partition_broadcast(128))],
              (), ["lnp"], chan=c_lnp, ninc=2)
        for q in range(4):
            wq = woutq[q % 2]
            DMA("sp", wq[:], wov[:, :, q * 256:(q + 1) * 256], c_woutq[q % 2], [("wo", layer)], [("woutq", q % 2)])
            for b in range(NB):
                bn = (q * NB + b) % 4
                for kc in range(16):
                    MM(banks[bn][:, 0:256], yT[:, kc, b * 128:(b + 1) * 128], wq[:, kc, :],
                       ["yT", ("woutq", q % 2)], [bk(bn)], start=(kc == 0), stop=(kc == 15))
                STT(xres[:, b, q * 256:(q + 1) * 256], xres[:, b, q * 256:(q + 1) * 256], ALPHA, banks[bn][:, 0:256],
                    ALU.mult, ALU.add, ["xres", ("xres_b", b), bk(bn)], [("xres_b", b)])
        for b in range(NB):
            st6, mv, rs = st6s[b], mvs[b], rss[b]
            tmp, tk = ep_big[:, b % 2, :], ("epb", b % 2)
            for n in range(2):
                P.add("dve", lambda e, n=n, b=b, st6=st6: e.bn_stats(out=st6[:, n, :], in_=xres[:, b, n * 512:(n + 1) * 512]),
                      [("xres_b", b)], [("ep_st6", b)])
            P.add("dve", lambda e, st6=st6, mv=mv: e.bn_aggr(out=mv[:], in_=st6[:].rearrange("p a s -> p (a s)")),
                  [("ep_st6", b)], [("ep_mv", b)])
            TS("dve", rs[:, 0:1], mv[:, 1:2], LN_EPS, None, ALU.add, None, [("ep_mv", b)], [("ep_rs", b)])
            TTN("pool", rs[:, 0:1], rs[:, 0:1], mhalf[:], ALU.pow, [("ep_rs", b), "mhalf"], [("ep_rs", b)])
            STT(rs[:, 1:2], mv[:, 0:1], -1.0, rs[:, 0:1], ALU.mult, ALU.mult, [("ep_mv", b), ("ep_rs", b)], [("ep_rs2", b)])
            ACT(tmp, xres[:, b, :], AF.Identity, [("xres_b", b), ("ep_rs", b), ("ep_rs2", b)], [tk], scale=rs[:, 0:1], bias=rs[:, 1:2])
            TTN("pool", tmp, tmp, lnp[:, 0, :], ALU.mult, [tk, "lnp"], [tk])
            TTN("dve", xres[:, b, :], tmp, lnp[:, 1, :], ALU.add, [tk, "lnp"], [("xres_b", b)])
        for c in range(8):
            b_ = 4 + c % 4
            for b in range(NB):
                TR(banks[b_][:, b * 128:(b + 1) * 128], xres[:, b, c * 128:(c + 1) * 128], ident[:],
                   [("xres_b", b), "ident"], [bk(b_)])
            CP("act" if c % 2 == 0 else "dve", xT[:, c, :], banks[b_][:], [bk(b_)], ["xT"])
        for c in range(2):
            b_ = 4 + c
            for b in range(NB):
                TR(banks[b_][:, b * 128:(b + 1) * 128], pst[:, b, c * 128:(c + 1) * 128], ident[:],
                   ["pst", "ident"], [bk(b_)])
            CP("act", pT[:, c, :], banks[b_][:], [bk(b_)], ["pT"])
        wgv = [Wh[0][:].rearrange("p c n -> p (c n)"), Wh[1][:].rearrange("p c n -> p (c n)")]
        P.add("sp", lambda e: [e.dma_start(out=wgv[hh].rearrange("p (c n) -> p c n", c=4),
                                           in_=wg_bf[layer][hh * 512:(hh + 1) * 512].rearrange("(c p) n -> p c n", p=128))
                               for hh in range(2)], [("wg", layer)], [("Wh", 0), ("Wh", 1)], chan=c_wg, ninc=2)

        def wgs(c, n):
            return wgv[c // 4][:, (c % 4) * 1024 + n * 512:(c % 4) * 1024 + (n + 1) * 512]

        for b in range(NB):
            tg, tgk = ep_big[:, b % 2, :], ("epb", b % 2)
            for n in range(2):
                for c in range(8):
                    MM(banks[n][:], xT[:, c, b * 128:(b + 1) * 128], wgs(c, n), ["xT", ("Wh", 0), ("Wh", 1)], [bk(n)],
                       start=(c == 0), stop=(c == 7))
                for c in range(2):
                    MM(banks[2 + n][:], pT[:, c, b * 128:(b + 1) * 128], wpl[:, c, n * 512:(n + 1) * 512],
                       ["pT", "wpl"], [bk(2 + n)], start=(c == 0), stop=(c == 1))
                ACT(tg[:, n * 512:(n + 1) * 512], banks[n][:], AF.Tanh, [bk(n)], [tgk], scale=0.5)
                STT(tg[:, n * 512:(n + 1) * 512], tg[:, n * 512:(n + 1) * 512], 1.0, banks[2 + n][:],
                    ALU.add, ALU.mult, [tgk, bk(2 + n)], [tgk])
            STT(xres[:, b, :], tg, 0.5, xres[:, b, :], ALU.mult, ALU.add, [tgk, ("xres_b", b)], [("xres_b", b)])
        P.add("dve", lambda e: e.tensor_copy(mvs[0][:, 0:1], mvs[0][:, 0:1]),
              [("xres_b", b) for b in range(NB)] + [("ep_mv", 0)], ["xres", ("ep_mv", 0)])

    def conv_diag(g, ntap, scale):
        dg = T("diag", [128, 4, 128], BF16)
        for tp in range(ntap):
            TS("pool", dg[:, tp, :], ident_b[:], cw[:, g, tp:tp + 1], scale, ALU.mult, ALU.mult,
               ["ident_b", "cw"], [("diag", tp)])
        return dg

    def proj_fm(par, col, bank):
        for c in range(8):
            MM(banks[bank][:], Wh[par][:, c, col * 128:(col + 1) * 128], xT[:, c, :], [("Wh", par), "xT"], [bk(bank)],
               start=(c == 0), stop=(c == 7))

    def unit_A(j, par):
        h_sb = T("w512_0", [128, TT])
        u_bf = T("a_u", [128, TT + 2], BF16)
        tz = T("w512_1", [128, TT])
        bz = T("w512_2", [128, TT])
        o_ = 4 * (j % 2)
        for col in range(4):
            proj_fm(par, col, o_ + col)
        CP("act", h_sb[:], banks[o_][:], [bk(o_)], ["w512_0"])
        CP("pool", u_bf[:, 0:2], hist_a[:, j, :], ["hist_a"], ["a_u"])
        TTN("dve", u_bf[:, 2:TT + 2], banks[o_ + 1][:], h_sb[:], ALU.mult, [bk(o_ + 1), "w512_0"], ["a_u"])
        CP("pool", hist_a[:, j, :], u_bf[:, TT:TT + 2], ["a_u"], ["hist_a"])
        ACT(tz[:], banks[o_ + 3][:], AF.Tanh, [bk(o_ + 3)], ["w512_1"], scale=0.5)
        STT(tz[:], tz[:], 1.0, banks[o_ + 3][:], ALU.add, ALU.mult, ["w512_1", bk(o_ + 3)], ["w512_1"])
        TTN("dve", bz[:], banks[o_ + 2][:], tz[:], ALU.mult, [bk(o_ + 2), "w512_1"], ["w512_2"])
        dg = conv_diag(j, 3, 0.5)
        for tp in range(3):
            MM(banks[o_][:], dg[:, tp, :], u_bf[:, tp:tp + TT], [("diag", tp), "a_u"], [bk(o_)], start=(tp == 0), stop=(tp == 2))
        TTN("dve", yT[:, j, :], banks[o_][:], bz[:], ALU.mult, [bk(o_), "w512_2"], ["yT"])

    def gdn_gates():
        bav = banks[3][:, 0:NB * 16].rearrange("p (b n) -> p b n", n=16)
        for b in range(NB):
            for c in range(8):
                MM(bav[:, b, :], xT[:, c, b * 128:(b + 1) * 128], wba[:, c, :], ["xT", "wba_sb"], [bk(3)],
                   start=(c == 0), stop=(c == 7))
        gt = T("g_t", [128, NB, 8])
        beta = T("g_beta", [128, NB, 8])
        nbeta = T("g_nbeta", [128, NB, 8])
        hbeta = T("g_hbeta", [128, NB, 8])
        g = T("g_g", [128, NB, 8])
        gc = T("g_gc", [128, NB, 8])
        ngc = T("g_ngc", [128, NB, 8])
        egc = T("g_egc", [128, NB, 8])
        bege = T("g_bege", [128, NB, 8])
        egl = T("g_egl", [128, NB, 8])
        el = T("g_el", [128, NB, 8])
        ACT(gt[:], bav[:, :, 0:8], AF.Tanh, [bk(3)], ["g_t"], scale=0.5)
        TS("dve", beta[:], gt[:], 0.5, 0.5, ALU.mult, ALU.add, ["g_t"], ["g_beta"])
        TS("dve", nbeta[:], gt[:], -0.5, -0.5, ALU.mult, ALU.add, ["g_t"], ["g_nbeta"])
        TS("dve", hbeta[:], gt[:], 0.25, 0.25, ALU.mult, ALU.add, ["g_t"], ["g_hbeta"])
        for b in range(NB):
            TTN("dve", g[:, b, :], bav[:, b, 8:16], aab[:, 8:16], ALU.add, [bk(3), "aab"], ["g_g"])
        ACT(g[:], g[:], AF.Exp, ["g_g"], ["g_g"])
        ACT(g[:], g[:], AF.Ln, ["g_g"], ["g_g"], bias=1.0)
        for b in range(NB):
            TTN("dve", g[:, b, :], g[:, b, :], aab[:, 0:8], ALU.mult, ["g_g", "aab"], ["g_g"])
        gcp = banks[3][:, 64:64 + NB * 8].rearrange("p (b n) -> p b n", n=8)
        glp = banks[3][:, 128:128 + NB * 8].rearrange("p (b n) -> p b n", n=8)
        for b in range(NB):
            MM(gcp[:, b, :], m_up[:], g[:, b, :], ["m_up", "g_g"], [bk(3)])
        MM(glp, ones[:], g[:], ["ones", "g_g"], [bk(3)])
        CP("dve", gc[:], gcp, [bk(3)], ["g_gc"])
        TS("dve", ngc[:], gcp, -1.0, None, ALU.mult, None, [bk(3)], ["g_ngc"])
        ACT(egc[:], gcp, AF.Exp, [bk(3)], ["g_egc"])
        TTN("dve", bege[:], egc[:], beta[:], ALU.mult, ["g_egc", "g_beta"], ["g_bege"])
        TTN("dve", egl[:], glp, gc[:], ALU.subtract, [bk(3), "g_gc"], ["g_egl"])
        ACT(egl[:], egl[:], AF.Exp, ["g_egl"], ["g_egl"])
        ACT(el[:], glp, AF.Exp, [bk(3)], ["g_el"])

    def rr(gens, fast=None):
        gens = [g_ for g_ in gens if g_ is not None]
        while gens:
            for g_ in list(gens):
                try:
                    next(g_)
                    if g_ is fast:
                        next(g_)
                except StopIteration:
                    gens.remove(g_)

    def b_bufs(h):
        pb = h % 2
        pre = T("b_pre%d" % pb, [128, 3, TT + 3], BF16)
        cT = T("b_cT%d" % pb, [128, 3, TT], BF16)
        sq = T("b_sq%d" % pb, [128, TT], BF16)
        zname = "w512_1" if pb == 0 else "zbs1"
        zfl = T(zname, [128, TT])
        return pb, pre, cT, sq, zname, zfl

    def front_B(h, par):
        pb, pre, cT, sq, zname, zfl = b_bufs(h)
        zbs = zfl[:].rearrange("p (b n) -> p b n", n=128)
        tt_ = T("w512_0", [128, TT])
        dg3 = T("diag3", [128, 3, 4, 128], BF16)
        for qi in range(3):
            gidx = qi * 8 + h
            for tp in range(4):
                TS("pool", dg3[:, qi, tp, :], ident_b[:], cw[:, 8 + gidx, tp:tp + 1], 1.0, ALU.mult, ALU.mult,
                   ["ident_b", "cw"], [("diag3", qi, tp)])
        yield
        for qi in range(3):
            proj_fm(par, qi, qi)
            yield
        zbv = banks[3][:].rearrange("p (b n) -> p b n", n=128)
        for b in range(NB):
            for c in range(8):
                MM(zbv[:, b, :], xT[:, c, b * 128:(b + 1) * 128], Wh[par][:, c, 384:512], ["xT", ("Wh", par)], [bk(3)],
                   start=(c == 0), stop=(c == 7))
            yield
        for qi in range(3):
            gidx = qi * 8 + h
            CP("pool", pre[:, qi, 0:3], hist_b[:, gidx, :], ["hist_b"], [("b_pre", pb, qi)])
            CP("act", pre[:, qi, 3:TT + 3], banks[qi][:], [bk(qi)], [("b_pre", pb, qi)])
            CP("pool", hist_b[:, gidx, :], pre[:, qi, TT:TT + 3], [("b_pre", pb, qi)], ["hist_b"])
            yield
        ACT(zbs, zbv, AF.Tanh, [bk(3)], [zname], scale=0.5)
        STT(zfl[:], zfl[:], 1.0, banks[3][:], ALU.add, ALU.mult, [zname, bk(3)], [zname])
        TTN("dve", zfl[:], zfl[:], gb4[0][:].rearrange("p b n -> p (b n)"), ALU.mult, [zname, "gb4"], [zname])
        yield
        for qi in range(3):
            for tp in range(4):
                MM(banks[qi][:], dg3[:, qi, tp, :], pre[:, qi, tp:tp + TT], [("diag3", qi, tp), ("b_pre", pb, qi)], [bk(qi)],
                   start=(tp == 0), stop=(tp == 3))
            yield
        for qi in range(3):
            ACT(tt_[:], banks[qi][:], AF.Tanh, [bk(qi)], ["w512_0"], scale=0.5)
            yield
            STT(cT[:, qi, :], tt_[:], 1.0, banks[qi][:], ALU.add, ALU.mult, ["w512_0", bk(qi)], [("b_cT", pb, qi)])
            yield
        TTN("pool", sq[:], cT[:, 0, :], cT[:, 0, :], ALU.mult, [("b_cT", pb, 0)], [("b_sq", pb)])

    def unit_B(h, nxt_front=None, tail=None):
        gt = tiles
        beta, nbeta, hbeta, g, gc, ngc, bege, egl, el = (gt["g_beta"], gt["g_nbeta"], gt["g_hbeta"], gt["g_g"], gt["g_gc"],
                                                         gt["g_ngc"], gt["g_bege"], gt["g_egl"], gt["g_el"])
        pb, pre, cT, sq, zname, zfl = b_bufs(h)
        zbs = zfl[:].rearrange("p (b n) -> p b n", n=128)
        junk = T("junk", [128, 128])

        def blk_tiles(b):
            f = lambda i: T("blk%d_f%d" % (b, i), [128, 128])
            hh = lambda i: T("blk%d_h%d" % (b, i), [128, 128], BF16)
            kf_ = lambda i: "blk%d_f%d" % (b, i)
            kh_ = lambda i: "blk%d_h%d" % (b, i)
            return f, hh, kf_, kh_

        def pre_block(b):
            blk = slice(b * 128, (b + 1) * 128)
            f, hh, kf_, kh_ = blk_tiles(b)
            sm = T("blk%d_sm" % b, [128, 16])
            smk = lambda i: ("blk_sm", b, i)
            A_, B_, Pm = [f(0), f(1)], [f(2), f(3)], [f(4), f(5)]
            kA, kB, kP = [kf_(0), kf_(1)], [kf_(2), kf_(3)], [kf_(4), kf_(5)]
            decT, dec, egr = f(6), f(7), f(8)
            ktm, kbg, kdec, vb, dgr, knT, TTb, attT, wT, qdT = (hh(i) for i in range(10))
            kps, kk = bslot(4 + b, bf=True)
            TR(kps, cT[:, 1, blk], ident_b[:], [("b_cT", pb, 1), "ident_b"], [kk])
            vps, vk = bslot(4 + b, bf=True)
            TR(vps, cT[:, 2, blk], ident_b[:], [("b_cT", pb, 2), "ident_b"], [vk])
            sqp, sqk = bslot(4 + b)
            MM(sqp[:, 0:1], sq[:, blk], onec_b[:], [("b_sq", pb), "onec_b"], [sqk])
            TS("dve", egr[:], ones[:], g[:, b, h:h + 1], None, ALU.mult, None, ["ones", "g_g"], [kf_(8)])
            grp, grk = bslot(4 + b)
            MM(grp, egr[:], m_up[:], [kf_(8), "m_up"], [grk])
            yield
            ACT(junk[:], kps, AF.Square, [kk], ["junk", smk(0)], accum_out=sm[:, 0:1])
            CP("dve", ktm[:], kps, [kk], [kh_(0)])
            TS("dve", vb[:], vps, hbeta[:, b, h:h + 1], None, ALU.mult, None, [vk, "g_hbeta"], [kh_(3)])
            TS("dve", decT[:], grp, gc[:, b, h:h + 1], 0.0, ALU.subtract, ALU.min, [grk, "g_gc"], [kf_(6)])
            TS("dve", dec[:], grp, gc[:, b, h:h + 1], 0.0, ALU.subtract, ALU.max, [grk, "g_gc"], [kf_(7)])
            ACT(egr[:], grp, AF.Exp, [grk, kf_(8)], [kf_(8)])
            ACT(decT[:], decT[:], AF.Exp, [kf_(6)], [kf_(6)])
            ACT(dec[:], dec[:], AF.Exp, [kf_(7)], [kf_(7)], scale=-1.0)
            TS("dve", sm[:, 1:2], sm[:, 0:1], 4e-6, None, ALU.add, None, [smk(0)], [smk(1)])
            TTN("pool", sm[:, 1:2], sm[:, 1:2], mhalf[:], ALU.pow, [smk(1), "mhalf"], [smk(1)])
            TTN("dve", sm[:, 2:3], sm[:, 1:2], bege[:, b, h:h + 1], ALU.mult, [smk(1), "g_bege"], [smk(2)])
            TTN("dve", sm[:, 3:4], sm[:, 1:2], egl[:, b, h:h + 1], ALU.mult, [smk(1), "g_egl"], [smk(3)])
            ACT(dgr[:], ident_b[:], AF.Copy, ["ident_b", smk(1)], [kh_(4)], scale=sm[:, 1:2])
            P.add("pool", lambda e, decT=decT: e.affine_select(out=decT[:], in_=decT[:], pattern=[[1, 128]],
                                                               compare_op=ALU.is_ge, fill=0.0, base=0, channel_multiplier=-1),
                  [kf_(6)], [kf_(6)])
            P.add("pool", lambda e, dec=dec: e.affine_select(out=dec[:], in_=dec[:], pattern=[[-1, 128]],
                                                             compare_op=ALU.is_gt, fill=0.0, base=0, channel_multiplier=1),
                  [kf_(7)], [kf_(7)])
            TS("dve", sm[:, 4:5], sqp[:, 0:1], 4 * 128e-5, 4 * 128e-5 * 4e-6, ALU.mult, ALU.add, [sqk], [smk(4)])
            yield
            TS("dve", kbg[:], kps, sm[:, 2:3], None, ALU.mult, None, [kk, smk(2)], [kh_(1)])
            ACT(kdec[:], kps, AF.Copy, [kk, smk(3)], [kh_(2)], scale=sm[:, 3:4])
            knp, knk = bslot(4 + b)
            MM(knp, ktm[:], dgr[:], [kh_(0), kh_(4)], [knk])
            TTN("dve", qdT[:], cT[:, 0, blk], egr[:], ALU.mult, [("b_cT", pb, 0), kf_(8)], [kh_(9)])
            yield
            CP("act", knT[:], knp, [knk], [kh_(5)])
            yield
            kkp, kkk = bslot(4 + b)
            MM(kkp, knT[:], knT[:], [kh_(5)], [kkk])
            qkp, qkk = bslot(4 + b)
            MM(qkp, knT[:], cT[:, 0, blk], [kh_(5), ("b_cT", pb, 0)], [qkk])
            yield
            STT(A_[0][:], kkp, nbeta[:, b, h:h + 1], dec[:], ALU.mult, ALU.mult, [kkk, "g_nbeta", kf_(7)], [kA[0]])
            TTN("dve", attT[:], qkp, decT[:], ALU.mult, [qkk, kf_(6)], [kh_(7)])
            yield
            atp, atk = bslot(4 + b)
            TR(atp, A_[0][:], ident[:], [kA[0], "ident"], [atk])
            yield
            CP("act", B_[0][:], atp, [atk], [kB[0]])
            TTN("dve", Pm[0][:], atp, ident[:], ALU.add, [atk, "ident"], [kP[0]])
            yield
            cur = 0
            for lv in range(1, 7):
                nxt = 1 - cur
                ap_, ak = bslot(4 + b)
                MM(ap_, B_[cur][:], A_[cur][:], [kB[cur], kA[cur]], [ak])
                if lv < 6:
                    bp_, bk_ = bslot(4 + b)
                    MM(bp_, A_[cur][:], B_[cur][:], [kB[cur], kA[cur]], [bk_])
                yield
                CP("act", A_[nxt][:], ap_, [ak], [kA[nxt]])
                if lv < 6:
                    CP("dve", B_[nxt][:], bp_, [bk_], [kB[nxt]])
                yield
                pp_, pk = bslot(4 + b)
                MM(pp_, A_[nxt][:], Pm[cur][:], [kA[nxt], kP[cur]], [pk])
                yield
                if lv < 6:
                    TTN("dve", Pm[nxt][:], pp_, Pm[cur][:], ALU.add, [pk, kP[cur]], [kP[nxt]])
                else:
                    TTN("dve", TTb[:], pp_, Pm[cur][:], ALU.add, [pk, kP[cur]], [kh_(6)])
                cur = nxt
            yield
            up_, uk = bslot(4 + b)
            MM(up_, TTb[:], vb[:], [kh_(6), kh_(3)], [uk])
            wp_, wk = bslot(4 + b)
            MM(wp_, kbg[:], TTb[:], [kh_(1), kh_(6)], [wk])
            yield
            CP("act", dec[:], up_, [uk], [kf_(7)])
            CP("act", wT[:], wp_, [wk], [kh_(8)])

        rr([nxt_front] + [pre_block(b) for b in range(NB)] + [tail], fast=nxt_front)

        ops_ = {}
        for b in range(NB):
            f, hh, kf_, kh_ = blk_tiles(b)
            u_sb = f(7)
            kdec, attT, wT, qdT = hh(2), hh(7), hh(8), hh(9)
            vnew = hh(10)
            wsp, wsk = bslot(4 + b)
            MM(wsp, wT[:], Sg_b[:, h, :], [kh_(8), ("Sg_b", h)], [wsk])
            TTN("dve", vnew[:], u_sb[:], wsp, ALU.subtract, [kf_(7), wsk], [kh_(10)])
            op_, ok = bslot(4 + b)
            MM(op_, qdT[:], Sg_b[:, h, :], [kh_(9), ("Sg_b", h)], [ok], start=True, stop=False)
            MM(op_, attT[:], vnew[:], [kh_(7), kh_(10)], [ok], start=False, stop=True)
            dsp, dsk = bslot(4 + b)
            MM(dsp, kdec[:], vnew[:], [kh_(2), kh_(10)], [dsk])
            STT(Sg_b[:, h, :], Sg[:, h, :], el[:, b, h:h + 1], dsp, ALU.mult, ALU.add, [("Sg", h), "g_el", dsk], [("Sg_b", h)])
            STT(Sg[:, h, :], Sg[:, h, :], el[:, b, h:h + 1], dsp, ALU.mult, ALU.add, [("Sg", h), "g_el", dsk], [("Sg", h)])
            ops_[b] = (op_, ok)
        for b in range(NB):
            blk = slice(b * 128, (b + 1) * 128)
            f, hh, kf_, kh_ = blk_tiles(b)
            sm = T("blk%d_sm" % b, [128, 16])
            smk = lambda i: ("blk_sm", b, i)
            ysb = hh(11)
            op_, ok = ops_[b]
            ACT(junk[:], op_, AF.Square, [ok], ["junk", smk(5)], accum_out=sm[:, 5:6])
            STT(sm[:, 6:7], sm[:, 5:6], 4.0 / 128.0, sm[:, 4:5], ALU.mult, ALU.add, [smk(5), smk(4)], [smk(6)])
            TTN("pool", sm[:, 6:7], sm[:, 6:7], mhalf[:], ALU.pow, [smk(6), "mhalf"], [smk(6)])
            STT(ysb[:], op_, sm[:, 6:7], zbs[:, b, :], ALU.mult, ALU.mult, [ok, smk(6), zname], [kh_(11)])

        def tail_gen():
            yield
            yield
            yield
            for b in range(NB):
                blk = slice(b * 128, (b + 1) * 128)
                f, hh, kf_, kh_ = blk_tiles(b)
                yp_, yk = bslot(4 + b, bf=True)
                TR(yp_, hh(11)[:], ident_b[:], [kh_(11), "ident_b"], [yk])
                yield
                CP("act", yT[:, 8 + h, blk], yp_, [yk], ["yT"])
                yield
        return tail_gen()

    def unit_H(h, par, tail=None):
        tf = T("w512_2", [128, TT])[:].rearrange("p (b n) -> p b n", n=128)
        kf = T("w512_3", [128, TT])[:].rearrange("p (b n) -> p b n", n=128)
        lf = T("w512_4", [128, TT])[:].rearrange("p (b n) -> p b n", n=128)
        v_b = T("h_v", [128, NB, 128], BF16)
        zs = T("w512_5", [128, TT])[:].rearrange("p (b n) -> p b n", n=128)
        tq = T("w512_0", [128, TT])
        qs = T("w512_1", [128, TT])
        proj_fm(par, 0, 0)
        for b in range(NB):
            for c in range(8):
                MM(banks[1 + b][:, 0:384], xT[:, c, b * 128:(b + 1) * 128], Wh[par][:, c, 128:512], ["xT", ("Wh", par)],
                   [bk(1 + b)], start=(c == 0), stop=(c == 7))
        ACT(tq[:], banks[0][:], AF.Tanh, [bk(0)], ["w512_0"], scale=0.5)
        STT(qs[:], tq[:], 1.0, banks[0][:], ALU.add, ALU.mult, ["w512_0", bk(0)], ["w512_1"])
        bks = [bk(1 + b) for b in range(NB)]
        k2 = [("w512_2", b) for b in range(NB)]
        k3 = [("w512_3", b) for b in range(NB)]
        k4 = [("w512_4", b) for b in range(NB)]
        k5 = [("w512_5", b) for b in range(NB)]
        kv = [("h_v", b) for b in range(NB)]
        tmv = psum_all[:, 1:1 + NB, :]
        ACT(tf, tmv[:, :, 0:128], AF.Tanh, bks, k2, scale=0.5)
        CP("act", v_b[:], tmv[:, :, 128:256], bks, kv)
        ACT(zs, tmv[:, :, 256:384], AF.Tanh, bks, k5, scale=0.5)
        STT(zs, zs, 1.0, tmv[:, :, 256:384], ALU.add, ALU.mult, k5 + bks, k5)
        TTN("dve", zs, zs, gb4[1][:], ALU.mult, k5 + ["gb4"], k5)
        for b in range(NB):
            STT(kf[:, b, :], tf[:, b, :], -1.0, nhoml[:, h * 128:(h + 1) * 128], ALU.add, ALU.mult,
                [("w512_2", b), "nhoml"], [("w512_3", b)])
        ACT(lf, kf, AF.Ln, k3, k4, scale=-1.0, bias=1.0)
        lnh = T("lnhalf", [128, 1])
        junk = T("junk", [128, 128])

        def htiles(b):
            f = lambda i: T("blk%d_f%d" % (b, i), [128, 128])
            hh = lambda i: T("blk%d_h%d" % (b, i), [128, 128], BF16)
            kf_ = lambda i: "blk%d_f%d" % (b, i)
            kh_ = lambda i: "blk%d_h%d" % (b, i)
            return f, hh, kf_, kh_

        def pre_h(b):
            blk = slice(b * 128, (b + 1) * 128)
            f, hh, kf_, kh_ = htiles(b)
            enb, ebT = f(0), f(1)
            kt, ktT, attT = hh(0), hh(1), hh(2)
            q2 = q2s[b]
            ebl = T("blk%d_ebl" % b, [128, 2])
            bp_, bk_ = bslot(HB[b])
            MM(bp_, m_bd[:], lf[:, b, :], ["m_bd", ("w512_4", b)], [bk_])
            btp, btk = bslot(HB[b])
            MM(btp, lf[:, b, :], m_bd[:], ["m_bd", ("w512_4", b)], [btk])
            yield
            ACT(enb[:], bp_, AF.Exp, [bk_], [kf_(0)], scale=-1.0)
            ACT(ebT[:], btp, AF.Exp, [btk, "lnhalf"], [kf_(1)], bias=lnh[:, 0:1])
            ACT(ebl[:, 0:2], btp[:, 63:128:64], AF.Exp, [btk], [("ebl", b)])
            yield
            TTN("dve", kt[:], kf[:, b, :], enb[:], ALU.mult, [("w512_3", b), kf_(0)], [kh_(0)])
            q2v = q2[:].rearrange("p (c x) -> p c x", x=192)[:, :, 0:64]
            TTN("dve", q2v, qs[:, blk].rearrange("p (c x) -> p c x", x=64), ebT[:].rearrange("p (c x) -> p c x", x=64),
                ALU.mult, ["w512_1", kf_(1)], [("q2", b)])
            yield
            ktp, ktk = bslot(HB[b], bf=True)
            TR(ktp, kt[:], ident_b[:], [kh_(0), "ident_b"], [ktk])
            d0p, d0k = bslot(HB[b])
            MM(d0p, kt[0:64, :], v_b[0:64, b, :], [kh_(0), ("h_v", b)], [d0k])
            d1p, d1k = banks[1 + b][:, 128:256], ("slot", 1 + b, 1)
            MM(d1p, kt[64:128, :], v_b[64:128, b, :], [kh_(0), ("h_v", b)], [d1k])
            yield
            G0, G1 = f(2), f(3)
            TS("dve", G0[:], d0p, ebl[:, 0:1], None, ALU.mult, None, [d0k, ("ebl", b)], [kf_(2)])
            TS("dve", G1[:], d1p, ebl[:, 1:2], None, ALU.mult, None, [d1k, ("ebl", b)], [kf_(3)])
            CP("act", ktT[:], ktp, [ktk], [kh_(1)])
            yield
            atp, atk = bslot(HB[b])
            MM(atp, ktT[:], q2v, [kh_(1), ("q2", b)], [atk])
            yield
            TTN("dve", attT[:], atp, m_bd[:], ALU.mult, [atk, "m_bd"], [kh_(2)])
            yield
            MM(banks[1 + b][:, 0:128], attT[:], v_b[:, b, :], [kh_(2), ("h_v", b)], [bk(1 + b)], start=True, stop=False)

        rr([pre_h(b) for b in range(NB)] + [tail])

        Pp = [T("h_P0", [128, 128]), T("h_P1", [128, 128])]
        cur, curk = Sh[:, h, :], ("Sh", h)
        idx = 0
        for b in range(NB):
            f, hh, kf_, kh_ = htiles(b)
            q2 = q2s[b]
            ebl = T("blk%d_ebl" % b, [128, 2])
            op_, ok = banks[1 + b][:, 0:128], bk(1 + b)
            for cc in range(2):
                if idx == 0:
                    sbf, sbk = Sh_b[:, h, :], ("Sh_b", h)
                else:
                    sbf, sbk = hh(4 + cc)[:], kh_(4 + cc)
                    CP("act", sbf, cur, [curk], [sbk])
                MM(op_, q2[:, cc * 128:(cc + 1) * 128], sbf, [("q2", b), sbk], [ok], start=False, stop=(cc == 1))
                if idx == 2 * NB - 1:
                    nxt, nxtk = Sh[:, h, :], ("Sh", h)
                else:
                    nxt, nxtk = Pp[idx % 2][:], "h_P%d" % (idx % 2)
                STT(nxt, cur, ebl[:, cc:cc + 1], f(2 + cc)[:], ALU.mult, ALU.add, [curk, ("ebl", b), kf_(2 + cc)], [nxtk])
                cur, curk = nxt, nxtk
                idx += 1
        CP("act", Sh_b[:, h, :], Sh[:, h, :], [("Sh", h)], [("Sh_b", h)])
        for b in range(NB):
            blk = slice(b * 128, (b + 1) * 128)
            f, hh, kf_, kh_ = htiles(b)
            ysb = hh(3)
            sm = T("blk%d_sm" % b, [128, 16])
            smk = lambda i: ("blk_sm", b, i)
            op_, ok = banks[1 + b][:, 0:128], bk(1 + b)
            ACT(junk[:], op_, AF.Square, [ok], ["junk", smk(0)], accum_out=sm[:, 0:1])
            TS("dve", sm[:, 1:2], sm[:, 0:1], 4.0 / 128.0, 4 * LN_EPS, ALU.mult, ALU.add, [smk(0)], [smk(1)])
            TTN("pool", sm[:, 1:2], sm[:, 1:2], mhalf[:], ALU.pow, [smk(1), "mhalf"], [smk(1)])
            STT(ysb[:], op_, sm[:, 1:2], zs[:, b, :], ALU.mult, ALU.mult, [ok, smk(1), ("w512_5", b)], [kh_(3)])

        def tail_gen():
            yield
            yield
            for b in range(NB):
                blk = slice(b * 128, (b + 1) * 128)
                f, hh, kf_, kh_ = htiles(b)
                yp_, yk = bslot(HB[b], bf=True)
                TR(yp_, hh(3)[:], ident_b[:], [kh_(3), "ident_b"], [yk])
                yield
                CP("act", yT[:, h, blk], yp_, [yk], ["yT"])
                yield
        return tail_gen()

    lnh = T("lnhalf", [128, 1])
    P.add("pool", lambda e: e.memset(lnh[:], float(np.log(0.5))), (), ["lnhalf"])
    HB = [5, 6, 7, 0]
    q2s = [T("h_q2_%d" % b, [128, 384], BF16) for b in range(NB)]
    for b in range(NB):
        P.add("pool", lambda e, b=b: e.memset(q2s[b][:], 0.0), (), [("q2", b)])

    for t in range(NT):
        DMA("sp", xres[:], x_d[t * TT:(t + 1) * TT, :].rearrange("(b p) f -> p b f", p=128), c_x, [], ["xres"])
        if marks is not None:
            marks.append(('tile_start', len(P.ops)))
        make_xT(0)
        if marks is not None:
            marks.append(('xT_end', len(P.ops)))
        par = load_unit(0, 0)
        gdn_gates()
        if marks is not None:
            marks.append(('gates_end', len(P.ops)))
        for u in range(8):
            npar = load_unit(0, u + 1)
            unit_A(u, par)
            par = npar
        pars = {8: par, 9: load_unit(0, 9)}
        rr([front_B(0, pars[8])])
        tail = None
        for h in range(8):
            if h + 2 < 8:
                pars[8 + h + 2] = load_unit(0, 8 + h + 2)
            nf = front_B(h + 1, pars[8 + h + 1]) if h < 7 else None
            tail = unit_B(h, nf, tail)
        rr([tail])
        epilogue(0, t)
        if n_layers > 1:
            make_xT(2)
            par = load_unit(1, 0)
            tail = None
            for u in range(16):
                npar = load_unit(1, u + 1) if u < 15 else None
                tail = unit_H(u, par, tail)
                par = npar
            rr([tail])
            epilogue(1, t)
        DMA("act", out_d[t * TT:(t + 1) * TT, :].rearrange("(b p) f -> p b f", p=128), xres[:], c_out, ["xres"], ["out"])

    if max_ops is not None:
        P.ops = P.ops[:max_ops]
    P.finish()
    P.emit()
    nc_allow.__exit__(None, None, None)
    es.close()
    return nc, len(P.ops)


_KEYS = ["x", "p", "w_in_even", "conv_a_w", "conv_b_w", "a_log", "dt_bias", "gdn_norm_g", "w_out_even", "w_in_odd",
         "lower_bounds", "hgrn_norm_g", "w_out_odd", "ln_g", "ln_b", "w_pl", "w_pl_gate"]


def make_in_maps(inputs, n_cores, S):
    f = lambda a: np.ascontiguousarray(np.asarray(a, dtype=np.float32))
    shared = {
        "w_in_even": f(inputs["w_in_even"][0]), "conv_a_w": f(inputs["conv_a_w"][0]),
        "conv_b_w": f(inputs["conv_b_w"][0]), "a_log": f(inputs["a_log"]), "dt_bias": f(inputs["dt_bias"]),
        "gdn_norm_g": f(inputs["gdn_norm_g"]), "w_out_even": f(inputs["w_out_even"][0]),
        "w_in_odd": f(inputs["w_in_odd"][0]), "lower_bounds": f(inputs["lower_bounds"]),
        "hgrn_norm_g": f(inputs["hgrn_norm_g"]), "w_out_odd": f(inputs["w_out_odd"][0]),
        "ln_g": f(inputs["ln_g"]), "ln_b": f(inputs["ln_b"]), "w_pl": f(inputs["w_pl"]),
        "w_pl_gate": f(inputs["w_pl_gate"]),
    }
    maps = []
    for b in range(n_cores):
        m = dict(shared)
        m["x"] = f(inputs["x"][b, :S])
        m["p"] = f(inputs["p"][:, b, :S])
        maps.append(m)
    return maps


def kernel(**inputs):
    S = inputs["x"].shape[1]
    nb = inputs["x"].shape[0]
    nc, _ = build(S=S)
    in_maps = make_in_maps(inputs, nb, S)
    res = run_bass_kernel_spmd(nc, in_maps, core_ids=list(range(nb)))
    return np.stack([np.asarray(r["out"], dtype=np.float32) for r in res.results], axis=0)
```

```python
import numpy as np
from contextlib import ExitStack
import concourse.bass as bass
import concourse.mybir as mybir
from concourse.bass_utils import run_bass_kernel_spmd

F32 = mybir.dt.float32
BF16 = mybir.dt.bfloat16
AF = mybir.ActivationFunctionType
ALU = mybir.AluOpType

D = 1024
TT = 512
NB = TT // 128
ALPHA = 4.0 ** 0.25
LN_EPS = 1e-5


class Op:
    __slots__ = ("eng", "fn", "reads", "writes", "chan", "ninc", "deps", "signal", "count", "idx")

    def __init__(self, eng, fn, reads, writes, chan, ninc):
        self.eng, self.fn, self.reads, self.writes, self.chan, self.ninc = eng, fn, reads, writes, chan, ninc
        self.deps = set()
        self.signal = chan is not None
        self.count = 0


class Prog:
    def __init__(self, nc):
        self.nc = nc
        self.ops = []
        self.nchan = 0
        self.lines = None

    def chan(self):
        self.nchan += 1
        return ("chan", self.nchan)

    def add(self, eng, fn, reads=(), writes=(), chan=None, ninc=1):
        op = Op(eng, fn, tuple(reads), tuple(writes), chan, ninc)
        op.idx = len(self.ops)
        if self.lines is not None:
            import sys as _s
            f = _s._getframe(1)
            while f.f_code.co_name in ("add", "MM", "TR", "ACT", "TS", "TTN", "STT", "CP", "DMA"):
                f = f.f_back
            self.lines.append(f.f_lineno)
        self.ops.append(op)
        return op

    def finish(self):
        ops = self.ops
        lastw, lastr = {}, {}

        def ch(o):
            return o.chan if o.chan is not None else o.eng

        def bank_of(k):
            if isinstance(k, tuple) and k[0] == "slot":
                return k[1]
            if isinstance(k, str) and k.startswith("bank"):
                return int(k[4:])
            return None

        bank_rd = {}
        bank_wr = {}
        for o in ops:
            deps = set()
            my = ch(o)
            for k in o.reads:
                deps.update(lastw.get(k, {}).values())
                b = bank_of(k)
                if b is not None:
                    deps.update(j for c, j in bank_rd.get(b, {}).items() if c != my)
                    bank_rd.setdefault(b, {})[my] = o.idx
                    if b in bank_wr:
                        deps.add(bank_wr[b])
            for k in o.writes:
                deps.update(lastw.get(k, {}).values())
                deps.update(lastr.get(k, {}).values())
                b = bank_of(k)
                if b is not None and o.eng == "pe":
                    deps.update(bank_rd.get(b, {}).values())
                    bank_wr[b] = o.idx
            deps.discard(o.idx)
            for j in deps:
                p = ops[j]
                if p.chan is None and o.chan is None and p.eng == o.eng:
                    if o.eng == "pe":
                        continue
                o.deps.add(j)
                p.signal = True
            for k in o.reads:
                lastr.setdefault(k, {})[my] = o.idx
            for k in o.writes:
                lastw[k] = {my: o.idx}
                lastr[k] = {}
        cnt = {}
        for o in ops:
            if o.signal:
                c = ch(o)
                cnt[c] = cnt.get(c, 0) + (16 * o.ninc if o.chan is not None else 1)
                o.count = cnt[c]
        self.cnt = cnt

    def emit(self, tail_eng="sp"):
        nc, ops, cnt = self.nc, self.ops, self.cnt
        chans = sorted(cnt.keys(), key=str)
        with ExitStack() as es:
            sems = {c: es.enter_context(nc.semaphore("s_" + (c if isinstance(c, str) else "c%d" % c[1])))
                    for c in chans}
            block = es.enter_context(nc.Block())

            def ch(o):
                return o.chan if o.chan is not None else o.eng

            def run(engname, e):
                waited = {}
                for o in ops:
                    if o.eng != engname:
                        continue
                    need = {}
                    for j in o.deps:
                        p = ops[j]
                        c = ch(p)
                        if p.count > need.get(c, 0):
                            need[c] = p.count
                    for c, v in need.items():
                        if waited.get(c, 0) >= v:
                            continue
                        e.wait_ge(sems[c], v)
                        waited[c] = v
                    ins = o.fn(e)
                    if self.lines is not None and o.chan is None:
                        ins.annotate("L%d" % self.lines[o.idx])
                    if o.signal:
                        if o.chan is not None:
                            for i_ in ins:
                                i_.then_inc(sems[o.chan], 16)
                        else:
                            ins.then_inc(sems[o.eng], 1)
                if engname == tail_eng:
                    for c, v in cnt.items():
                        if waited.get(c, 0) < v:
                            e.wait_ge(sems[c], v)

            block.sync(lambda e: run("sp", e))
            block.tensor(lambda e: run("pe", e))
            block.scalar(lambda e: run("act", e))
            block.vector(lambda e: run("dve", e))
            block.gpsimd(lambda e: run("pool", e))


def build(S=4096, n_layers=2, max_ops=None, marks=None):
    NT = S // TT
    nc = bass.Bass("TRN2", target_bir_lowering=False)

    def din(name, shape):
        return nc.dram_tensor(name, shape, F32, kind="ExternalInput").ap()

    x_d = din("x", [S, D])
    p_d = din("p", [2, S, 256])
    wie_d = din("w_in_even", [D, 8208])
    cwa_d = din("conv_a_w", [3, 1024])
    cwb_d = din("conv_b_w", [4, 3072])
    alog_d = din("a_log", [1, 8])
    dtb_d = din("dt_bias", [1, 8])
    gng_d = din("gdn_norm_g", [1, 128])
    woe_d = din("w_out_even", [2048, D])
    wio_d = din("w_in_odd", [D, 8192])
    lb_d = din("lower_bounds", [2, 2048])
    hng_d = din("hgrn_norm_g", [1, 128])
    woo_d = din("w_out_odd", [2048, D])
    lng_d = din("ln_g", [2, D])
    lnb_d = din("ln_b", [2, D])
    wpl_d = din("w_pl", [2, 256, D])
    wg_d = din("w_pl_gate", [2, D, D])
    out_d = nc.dram_tensor("out", [S, D], F32, kind="ExternalOutput").ap()

    def dscr(name, shape):
        return nc.dram_tensor(name, shape, BF16, kind="Internal").ap()

    wu_bf = [dscr("wu_even", [16, D, 512]), dscr("wu_odd", [16, D, 512])]
    wba_bf = dscr("wba_bf", [D, 16])
    wo_bf = [dscr("wo_even", [2048, D]), dscr("wo_odd", [2048, D])]
    wg_bf = dscr("wg_bf", [2, D, D])
    wpl_bf = dscr("wpl_bf", [2, 256, D])

    P = Prog(nc)
    if marks is not None:
        P.lines = []
        marks.append(P.lines)
    es = ExitStack()
    tiles = {}

    def T(name, shape, dt=F32):
        if name not in tiles:
            tiles[name] = es.enter_context(nc.sbuf_tensor(name, shape, dt))
        return tiles[name]

    banks = [es.enter_context(nc.psum_tensor("bank%d" % i, [128, 512], F32)) for i in range(8)]

    def bk(i):
        return "bank%d" % i

    def MM(out, lhsT, rhs, r, w, start=True, stop=True):
        P.add("pe", lambda e: e.matmul(out, lhsT=lhsT, rhs=rhs, start=start, stop=stop), r, w)

    def TR(out, in_, ident, r, w):
        P.add("pe", lambda e: e.transpose(out, in_, ident), r, w)

    def ACT(out, in_, func, r, w, **kw):
        P.add("act", lambda e: e.activation(out=out, in_=in_, func=func, **kw), r, w)

    def TS(eng, out, in0, s1, s2, op0, op1, r, w):
        if op1 is None:
            P.add(eng, lambda e: e.tensor_scalar(out, in0, s1, None, op0), r, w)
        else:
            P.add(eng, lambda e: e.tensor_scalar(out, in0, s1, s2, op0, op1), r, w)

    def TTN(eng, out, in0, in1, op, r, w):
        P.add(eng, lambda e: e.tensor_tensor(out, in0, in1, op), r, w)

    def STT(out, in0, scalar, in1, op0, op1, r, w):
        P.add("dve", lambda e: e.scalar_tensor_tensor(out, in0, scalar, in1, op0, op1), r, w)

    def CP(eng, out, in_, r, w):
        if eng == "act":
            ACT(out, in_, AF.Copy, r, w)
        else:
            P.add(eng, lambda e: e.tensor_copy(out, in_), r, w)

    def DMA(eng, out, in_, chan, r, w, **kw):
        P.add(eng, lambda e: [e.dma_start(out=out, in_=in_, **kw)], r, w, chan=chan)

    ones = T("ones", [128, 128])
    ident = T("ident", [128, 128])
    ident_b = T("ident_b", [128, 128], BF16)
    m_up = T("m_up", [128, 128])
    m_bd = T("m_bd", [128, 128])
    m_bd_b = T("m_bd_b", [128, 128], BF16)
    onec_b = T("onec_b", [128, 1], BF16)
    mhalf = T("mhalf", [128, 1])
    P.add("pool", lambda e: e.memset(ones[:], 1.0), (), ["ones"])
    P.add("pool", lambda e: e.memset(onec_b[:], 1.0), (), ["onec_b"])
    P.add("pool", lambda e: e.memset(mhalf[:], -0.5), (), ["mhalf"])
    P.add("pool", lambda e: e.affine_select(out=ident[:], in_=ones[:], pattern=[[-1, 128]], compare_op=ALU.is_equal,
                                            fill=0.0, base=0, channel_multiplier=1), ["ones"], ["ident"])
    P.add("pool", lambda e: e.affine_select(out=m_up[:], in_=ones[:], pattern=[[1, 128]], compare_op=ALU.is_ge,
                                            fill=0.0, base=0, channel_multiplier=-1), ["ones"], ["m_up"])
    P.add("pool", lambda e: e.tensor_copy(ident_b[:], ident[:]), ["ident"], ["ident_b"])
    P.add("pool", lambda e: e.tensor_copy(m_bd[:], m_up[:]), ["m_up"], ["m_bd"])
    P.add("pool", lambda e: e.memset(m_bd[0:64, 64:128], 0.0), ["m_bd"], ["m_bd"])
    P.add("pool", lambda e: e.tensor_copy(m_bd_b[:], m_bd[:]), ["m_bd"], ["m_bd_b"])

    c_const = P.chan()
    lnp = T("lnp", [128, 2, D])
    c_lnp = P.chan()
    xres = T("xres", [128, NB, D])
    yT = T("yT", [128, 16, TT], BF16)
    xres_f = xres[:].rearrange("p b d -> p (b d)")
    yT_f = yT[:].rearrange("p c n -> p (c n)").bitcast(F32)
    lbb = xres_f[:, 0:4096].rearrange("p (a d) -> p a d", a=2)
    nhoml = T("nhoml", [128, 2048])
    aab = T("aab", [128, 16])
    cwr = yT_f[0:4, 0:4096]
    cw = T("cw", [128, 32, 4])
    P.add("sp", lambda e: [e.dma_start(out=xres_f[:, 0:4096],
                                       in_=lb_d.rearrange("a d -> (a d)").partition_broadcast(128)),
                           e.dma_start(out=aab[:, 0:8], in_=alog_d.rearrange("a d -> (a d)").partition_broadcast(128)),
                           e.dma_start(out=aab[:, 8:16], in_=dtb_d.rearrange("a d -> (a d)").partition_broadcast(128)),
                           e.dma_start(out=cwr[0:3, 0:1024], in_=cwa_d),
                           e.dma_start(out=cwr[0:4, 1024:4096], in_=cwb_d),
                           ], (), ["xres", "yT", "aab"], chan=c_const, ninc=5)
    nc_allow = nc.allow_non_contiguous_dma(reason="tiny per-partition vectors")
    nc_allow.__enter__()
    TTN("dve", lbb[:, 0, :], lbb[:, 1, :], lbb[:, 0, :], ALU.subtract, ["xres"], ["xres"])
    ACT(lbb[:, 1, :], lbb[:, 0, :], AF.Tanh, ["xres"], ["xres"], scale=0.5)
    TS("dve", nhoml[:], lbb[:, 1, :], 0.25, -0.25, ALU.mult, ALU.add, ["xres"], ["nhoml"])
    ACT(aab[:, 0:8], aab[:, 0:8], AF.Exp, ["aab"], ["aab"])
    TS("dve", aab[:, 0:8], aab[:, 0:8], -1.0, None, ALU.mult, None, ["aab"], ["aab"])
    cwv = banks[0][:, 0:128].rearrange("p (g k) -> p g k", k=4)
    for g in range(32):
        ntap = 3 if g < 8 else 4
        TR(cwv[:, g, 0:ntap], cwr[0:ntap, g * 128:(g + 1) * 128], ident[0:ntap, 0:ntap], ["yT", "ident"], [bk(0)])
    P.add("dve", lambda e: e.memset(cw[:], 0.0), (), ["cw"])
    CP("dve", cw[:, 0:8, 0:3], cwv[:, 0:8, 0:3], [bk(0)], ["cw"])
    CP("dve", cw[:, 8:32, :], cwv[:, 8:32, :], [bk(0)], ["cw"])

    if marks is not None:
        marks.append(('consts_end', len(P.ops)))
    c_cast = {}

    def cast_unit(layer, u):
        ch = P.chan()
        c_cast[(layer, u)] = ch
        if layer == 0:
            src = wie_d[:, 0:4096] if u < 8 else wie_d[:, 4096:8192]
            g = u % 8
            sv = src.rearrange("r (m g c) -> r m g c", m=4, g=8, c=128)[:, :, g, :]
        else:
            sv = wio_d.rearrange("r (m g c) -> r m g c", m=4, g=16, c=128)[:, :, u, :]
        dv = wu_bf[layer][u].rearrange("r (m c) -> r m c", m=4)
        P.add("pool", lambda e: [e.dma_start(out=dv[0:512], in_=sv[0:512]),
                                 e.dma_start(out=dv[512:1024], in_=sv[512:1024])],
              (), [("wu", layer, u)], chan=ch, ninc=2)

    def cast_plain(dst, src, key, nsplit):
        ch = P.chan()
        rows = src.shape[0]
        step = rows // nsplit
        P.add("pool", lambda e: [e.dma_start(out=dst[i * step:(i + 1) * step], in_=src[i * step:(i + 1) * step])
                                 for i in range(nsplit)], (), [key], chan=ch, ninc=nsplit)

    ep_big = T("ep_big", [128, 2, D])
    pst = T("pst", [128, NB, 256])
    gb4 = [T("gb4_0", [128, NB, 128]), T("gb4_1", [128, NB, 128])]
    c_gb = P.chan()
    P.add("sp", lambda e: [e.dma_start(out=gb4[l][:, b, :], in_=(gng_d, hng_d)[l][0].partition_broadcast(128))
                           for l in range(2) for b in range(NB)], (), ["gb4"], chan=c_gb, ninc=2 * NB)

    cast_unit(0, 0)
    cast_plain(wba_bf, wie_d[:, 8192:8208], "wba", 1)
    for u in range(1, 16):
        cast_unit(0, u)
    cast_plain(wo_bf[0], woe_d, ("wo", 0), 4)
    cast_plain(wg_bf[0], wg_d[0], ("wg", 0), 2)
    cast_plain(wpl_bf[0], wpl_d[0], ("wpl", 0), 1)
    if n_layers > 1:
        for u in range(16):
            cast_unit(1, u)
        cast_plain(wo_bf[1], woo_d, ("wo", 1), 4)
        cast_plain(wg_bf[1], wg_d[1], ("wg", 1), 2)
        cast_plain(wpl_bf[1], wpl_d[1], ("wpl", 1), 1)

    if marks is not None:
        marks.append(('casts_end', len(P.ops)))
    xT = T("xT", [128, 8, TT], BF16)
    Wh = [T("Wh0", [128, 8, 512], BF16), T("Wh1", [128, 8, 512], BF16)]
    wba = T("wba", [128, 8, 16], BF16)
    wpl = T("wpl", [128, 2, D], BF16)
    woutq = [T("woutq0", [128, 16, 256], BF16), T("woutq1", [128, 16, 256], BF16)]
    c_woutq = [P.chan(), P.chan()]
    pT = T("pT", [128, 2, TT], BF16)
    Sg = T("Sg", [128, 8, 128])
    Sg_b = T("Sg_b", [128, 8, 128], BF16)
    Sh = T("Sh", [128, 16, 128])
    Sh_b = T("Sh_b", [128, 16, 128], BF16)
    hist_a = T("hist_a", [128, 8, 2], BF16)
    hist_b = T("hist_b", [128, 24, 3], BF16)
    for t_, k_ in ((Sg, "Sg"), (Sg_b, "Sg_b"), (Sh, "Sh"), (Sh_b, "Sh_b"), (hist_a, "hist_a"), (hist_b, "hist_b")):
        P.add("pool", lambda e, t_=t_: e.memset(t_[:], 0.0), (), [k_])
    for h in range(8):
        pass
    c_x, c_p, c_wh, c_wout, c_wg, c_wplc, c_wbac, c_out = (P.chan(), P.chan(), [P.chan(), P.chan()], P.chan(),
                                                         P.chan(), P.chan(), P.chan(), P.chan())
    DMA("sp", wba[:], wba_bf.rearrange("(c p) n -> p c n", p=128), c_wbac, ["wba"], ["wba_sb"])

    unit_ctr = [0]

    def load_unit(layer, u):
        par = unit_ctr[0] % 2
        unit_ctr[0] += 1
        DMA("sp", Wh[par][:], wu_bf[layer][u].rearrange("(c p) n -> p c n", p=128), c_wh[par],
            [("wu", layer, u)], [("Wh", par)])
        return par

    bslot_ctr = {}

    def bslot(bank, bf=False):
        q = bslot_ctr.get(bank, 0) % 4
        bslot_ctr[bank] = bslot_ctr.get(bank, 0) + 1
        key = ("slot", bank, q)
        if bf:
            return banks[bank][:].bitcast(BF16)[:, q * 256:q * 256 + 128], key
        return banks[bank][:, q * 128:(q + 1) * 128], key

    def make_xT(rot):
        for c in range(8):
            b_ = 4 + (c + rot) % 4
            for b in range(NB):
                TR(banks[b_][:, b * 128:(b + 1) * 128], xres[:, b, c * 128:(c + 1) * 128], ident[:],
                   ["xres", "ident"], [bk(b_)])
            CP("act" if c % 2 == 0 else "dve", xT[:, c, :], banks[b_][:], [bk(b_)], ["xT"])

    def epilogue(layer, t):
        st6s = [T("ep_st6_%d" % b, [128, 2, 6]) for b in range(NB)]
        mvs = [T("ep_mv_%d" % b, [128, 2]) for b in range(NB)]
        rss = [T("ep_rs_%d" % b, [128, 2]) for b in range(NB)]
        DMA("sp", wpl[:], wpl_bf[layer].rearrange("(c p) n -> p c n", p=128), c_wplc, [("wpl", layer)], ["wpl"])
        DMA("sp", pst[:], p_d[layer, t * TT:(t + 1) * TT, :].rearrange("(b p) f -> p b f", p=128), c_p, [], ["pst"])
        wov = wo_bf[layer].rearrange("(c p) n -> p c n", p=128)
        P.add("sp", lambda e: [e.dma_start(out=lnp[:, 0, :], in_=lng_d[layer].partition_broadcast(128)),
                               e.dma_start(out=lnp[:, 1, :], in_=lnb_d[layer].partition_broadcast(128))],
              (), ["lnp"], chan=c_lnp, ninc=2)
        for q in range(4):
            wq = woutq[q % 2]
            DMA("sp", wq[:], wov[:, :, q * 256:(q + 1) * 256], c_woutq[q % 2], [("wo", layer)], [("woutq", q % 2)])
            for b in range(NB):
                bn = (q * NB + b) % 4
                for kc in range(16):
                    MM(banks[bn][:, 0:256], yT[:, kc, b * 128:(b + 1) * 128], wq[:, kc, :],
                       ["yT", ("woutq", q % 2)], [bk(bn)], start=(kc == 0), stop=(kc == 15))
                STT(xres[:, b, q * 256:(q + 1) * 256], xres[:, b, q * 256:(q + 1) * 256], ALPHA, banks[bn][:, 0:256],
                    ALU.mult, ALU.add, ["xres", ("xres_b", b), bk(bn)], [("xres_b", b)])
        for c in range(2):
            b_ = 4 + c
            for b in range(NB):
                TR(banks[b_][:, b * 128:(b + 1) * 128], pst[:, b, c * 128:(c + 1) * 128], ident[:],
                   ["pst", "ident"], [bk(b_)])
            CP("act", pT[:, c, :], banks[b_][:], [bk(b_)], ["pT"])
        def ln_gen(b):
            st6, mv, rs = st6s[b], mvs[b], rss[b]
            tmp, tk = ep_big[:, b % 2, :], ("epb", b % 2)
            for n in range(2):
                P.add("dve", lambda e, n=n, b=b, st6=st6: e.bn_stats(out=st6[:, n, :], in_=xres[:, b, n * 512:(n + 1) * 512]),
                      [("xres_b", b)], [("ep_st6", b)])
            yield
            P.add("dve", lambda e, st6=st6, mv=mv: e.bn_aggr(out=mv[:], in_=st6[:].rearrange("p a s -> p (a s)")),
                  [("ep_st6", b)], [("ep_mv", b)])
            TS("dve", rs[:, 0:1], mv[:, 1:2], LN_EPS, None, ALU.add, None, [("ep_mv", b)], [("ep_rs", b)])
            yield
            TTN("pool", rs[:, 0:1], rs[:, 0:1], mhalf[:], ALU.pow, [("ep_rs", b), "mhalf"], [("ep_rs", b)])
            yield
            STT(rs[:, 1:2], mv[:, 0:1], -1.0, rs[:, 0:1], ALU.mult, ALU.mult, [("ep_mv", b), ("ep_rs", b)], [("ep_rs2", b)])
            yield
            ACT(tmp, xres[:, b, :], AF.Identity, [("xres_b", b), ("ep_rs", b), ("ep_rs2", b)], [tk], scale=rs[:, 0:1], bias=rs[:, 1:2])
            yield
            TTN("pool", tmp, tmp, lnp[:, 0, :], ALU.mult, [tk, "lnp"], [tk])
            yield
            TTN("dve", xres[:, b, :], tmp, lnp[:, 1, :], ALU.add, [tk, "lnp"], [("xres_b", b)])
            yield
            for half in range(2):
                b_ = 4 + 2 * (b % 2) + half
                for cc in range(4):
                    c = half * 4 + cc
                    TR(banks[b_][:, cc * 128:(cc + 1) * 128], xres[:, b, c * 128:(c + 1) * 128], ident[:],
                       [("xres_b", b), "ident"], [bk(b_)])
                CP("act" if half == 0 else "dve", xT[:, half * 4:half * 4 + 4, b * 128:(b + 1) * 128],
                   banks[b_][:].rearrange("p (c n) -> p c n", n=128), [bk(b_)], [("xT_b", b)])
                yield

        for b0 in range(0, NB, 2):
            rr([ln_gen(b0), ln_gen(b0 + 1)])
        wgv = [Wh[0][:].rearrange("p c n -> p (c n)"), Wh[1][:].rearrange("p c n -> p (c n)")]
        P.add("sp", lambda e: [e.dma_start(out=wgv[hh].rearrange("p (c n) -> p c n", c=4),
                                           in_=wg_bf[layer][hh * 512:(hh + 1) * 512].rearrange("(c p) n -> p c n", p=128))
                               for hh in range(2)], [("wg", layer)], [("Wh", 0), ("Wh", 1)], chan=c_wg, ninc=2)

        def wgs(c, n):
            return wgv[c // 4][:, (c % 4) * 1024 + n * 512:(c % 4) * 1024 + (n + 1) * 512]

        for b in range(NB):
            tg, tgk = ep_big[:, b % 2, :], ("epb", b % 2)
            for n in range(2):
                for c in range(8):
                    MM(banks[n][:], xT[:, c, b * 128:(b + 1) * 128], wgs(c, n), [("xT_b", b), ("Wh", 0), ("Wh", 1)], [bk(n)],
                       start=(c == 0), stop=(c == 7))
                for c in range(2):
                    MM(banks[2 + n][:], pT[:, c, b * 128:(b + 1) * 128], wpl[:, c, n * 512:(n + 1) * 512],
                       ["pT", "wpl"], [bk(2 + n)], start=(c == 0), stop=(c == 1))
                ACT(tg[:, n * 512:(n + 1) * 512], banks[n][:], AF.Tanh, [bk(n)], [tgk], scale=0.5)
                STT(tg[:, n * 512:(n + 1) * 512], tg[:, n * 512:(n + 1) * 512], 1.0, banks[2 + n][:],
                    ALU.add, ALU.mult, [tgk, bk(2 + n)], [tgk])
            STT(xres[:, b, :], tg, 0.5, xres[:, b, :], ALU.mult, ALU.add, [tgk, ("xres_b", b)], [("xres_b", b)])
        P.add("dve", lambda e: e.tensor_copy(mvs[0][:, 0:1], mvs[0][:, 0:1]),
              [("xres_b", b) for b in range(NB)] + [("ep_mv", 0)], ["xres", ("ep_mv", 0)])

    def conv_diag(g, ntap, scale):
        dg = T("diag", [128, 4, 128], BF16)
        for tp in range(ntap):
            TS("pool", dg[:, tp, :], ident_b[:], cw[:, g, tp:tp + 1], scale, ALU.mult, ALU.mult,
               ["ident_b", "cw"], [("diag", tp)])
        return dg

    def proj_fm(par, col, bank):
        for c in range(8):
            MM(banks[bank][:], Wh[par][:, c, col * 128:(col + 1) * 128], xT[:, c, :], [("Wh", par), "xT"], [bk(bank)],
               start=(c == 0), stop=(c == 7))

    def unit_A(j, par):
        h_sb = T("w512_0", [128, TT])
        u_bf = T("a_u", [128, TT + 2], BF16)
        tz = T("w512_1", [128, TT])
        bz = T("w512_2", [128, TT])
        o_ = 4 * (j % 2)
        for col in range(4):
            proj_fm(par, col, o_ + col)
        CP("act", h_sb[:], banks[o_][:], [bk(o_)], ["w512_0"])
        CP("pool", u_bf[:, 0:2], hist_a[:, j, :], ["hist_a"], ["a_u"])
        TTN("dve", u_bf[:, 2:TT + 2], banks[o_ + 1][:], h_sb[:], ALU.mult, [bk(o_ + 1), "w512_0"], ["a_u"])
        CP("pool", hist_a[:, j, :], u_bf[:, TT:TT + 2], ["a_u"], ["hist_a"])
        ACT(tz[:], banks[o_ + 3][:], AF.Tanh, [bk(o_ + 3)], ["w512_1"], scale=0.5)
        STT(tz[:], tz[:], 1.0, banks[o_ + 3][:], ALU.add, ALU.mult, ["w512_1", bk(o_ + 3)], ["w512_1"])
        TTN("dve", bz[:], banks[o_ + 2][:], tz[:], ALU.mult, [bk(o_ + 2), "w512_1"], ["w512_2"])
        dg = conv_diag(j, 3, 0.5)
        for tp in range(3):
            MM(banks[o_][:], dg[:, tp, :], u_bf[:, tp:tp + TT], [("diag", tp), "a_u"], [bk(o_)], start=(tp == 0), stop=(tp == 2))
        TTN("dve", yT[:, j, :], banks[o_][:], bz[:], ALU.mult, [bk(o_), "w512_2"], ["yT"])

    def gdn_gates():
        bav = banks[3][:, 0:NB * 16].rearrange("p (b n) -> p b n", n=16)
        for b in range(NB):
            for c in range(8):
                MM(bav[:, b, :], xT[:, c, b * 128:(b + 1) * 128], wba[:, c, :], ["xT", "wba_sb"], [bk(3)],
                   start=(c == 0), stop=(c == 7))
        gt = T("g_t", [128, NB, 8])
        beta = T("g_beta", [128, NB, 8])
        nbeta = T("g_nbeta", [128, NB, 8])
        hbeta = T("g_hbeta", [128, NB, 8])
        g = T("g_g", [128, NB, 8])
        gc = T("g_gc", [128, NB, 8])
        ngc = T("g_ngc", [128, NB, 8])
        egc = T("g_egc", [128, NB, 8])
        bege = T("g_bege", [128, NB, 8])
        egl = T("g_egl", [128, NB, 8])
        el = T("g_el", [128, NB, 8])
        ACT(gt[:], bav[:, :, 0:8], AF.Tanh, [bk(3)], ["g_t"], scale=0.5)
        TS("dve", beta[:], gt[:], 0.5, 0.5, ALU.mult, ALU.add, ["g_t"], ["g_beta"])
        TS("dve", nbeta[:], gt[:], -0.5, -0.5, ALU.mult, ALU.add, ["g_t"], ["g_nbeta"])
        TS("dve", hbeta[:], gt[:], 0.25, 0.25, ALU.mult, ALU.add, ["g_t"], ["g_hbeta"])
        for b in range(NB):
            TTN("dve", g[:, b, :], bav[:, b, 8:16], aab[:, 8:16], ALU.add, [bk(3), "aab"], ["g_g"])
        ACT(g[:], g[:], AF.Exp, ["g_g"], ["g_g"])
        ACT(g[:], g[:], AF.Ln, ["g_g"], ["g_g"], bias=1.0)
        for b in range(NB):
            TTN("dve", g[:, b, :], g[:, b, :], aab[:, 0:8], ALU.mult, ["g_g", "aab"], ["g_g"])
        gcp = banks[3][:, 64:64 + NB * 8].rearrange("p (b n) -> p b n", n=8)
        glp = banks[3][:, 128:128 + NB * 8].rearrange("p (b n) -> p b n", n=8)
        for b in range(NB):
            MM(gcp[:, b, :], m_up[:], g[:, b, :], ["m_up", "g_g"], [bk(3)])
        MM(glp, ones[:], g[:], ["ones", "g_g"], [bk(3)])
        CP("dve", gc[:], gcp, [bk(3)], ["g_gc"])
        TS("dve", ngc[:], gcp, -1.0, None, ALU.mult, None, [bk(3)], ["g_ngc"])
        ACT(egc[:], gcp, AF.Exp, [bk(3)], ["g_egc"])
        TTN("dve", bege[:], egc[:], beta[:], ALU.mult, ["g_egc", "g_beta"], ["g_bege"])
        TTN("dve", egl[:], glp, gc[:], ALU.subtract, [bk(3), "g_gc"], ["g_egl"])
        ACT(egl[:], egl[:], AF.Exp, ["g_egl"], ["g_egl"])
        ACT(el[:], glp, AF.Exp, [bk(3)], ["g_el"])

    def rr(gens, fast=None):
        gens = [g_ for g_ in gens if g_ is not None]
        while gens:
            for g_ in list(gens):
                try:
                    next(g_)
                    if g_ is fast:
                        next(g_)
                except StopIteration:
                    gens.remove(g_)

    def b_bufs(h):
        pb = h % 2
        pre = T("b_pre%d" % pb, [128, 3, TT + 3], BF16)
        cT = T("b_cT%d" % pb, [128, 3, TT], BF16)
        sq = T("b_sq%d" % pb, [128, TT], BF16)
        zname = "w512_1" if pb == 0 else "zbs1"
        zfl = T(zname, [128, TT])
        return pb, pre, cT, sq, zname, zfl

    def front_B(h, par):
        pb, pre, cT, sq, zname, zfl = b_bufs(h)
        zbs = zfl[:].rearrange("p (b n) -> p b n", n=128)
        tt_ = T("w512_0", [128, TT])
        dg3 = T("diag3", [128, 3, 4, 128], BF16)
        for qi in range(3):
            gidx = qi * 8 + h
            for tp in range(4):
                TS("pool", dg3[:, qi, tp, :], ident_b[:], cw[:, 8 + gidx, tp:tp + 1], 1.0, ALU.mult, ALU.mult,
                   ["ident_b", "cw"], [("diag3", qi, tp)])
        yield
        for qi in range(3):
            proj_fm(par, qi, qi)
            yield
        zbv = banks[3][:].rearrange("p (b n) -> p b n", n=128)
        for b in range(NB):
            for c in range(8):
                MM(zbv[:, b, :], xT[:, c, b * 128:(b + 1) * 128], Wh[par][:, c, 384:512], ["xT", ("Wh", par)], [bk(3)],
                   start=(c == 0), stop=(c == 7))
            yield
        for qi in range(3):
            gidx = qi * 8 + h
            CP("pool", pre[:, qi, 0:3], hist_b[:, gidx, :], ["hist_b"], [("b_pre", pb, qi)])
            CP("act", pre[:, qi, 3:TT + 3], banks[qi][:], [bk(qi)], [("b_pre", pb, qi)])
            CP("pool", hist_b[:, gidx, :], pre[:, qi, TT:TT + 3], [("b_pre", pb, qi)], ["hist_b"])
            yield
        ACT(zbs, zbv, AF.Tanh, [bk(3)], [zname], scale=0.5)
        STT(zfl[:], zfl[:], 1.0, banks[3][:], ALU.add, ALU.mult, [zname, bk(3)], [zname])
        TTN("dve", zfl[:], zfl[:], gb4[0][:].rearrange("p b n -> p (b n)"), ALU.mult, [zname, "gb4"], [zname])
        yield
        for qi in range(3):
            for tp in range(4):
                MM(banks[qi][:], dg3[:, qi, tp, :], pre[:, qi, tp:tp + TT], [("diag3", qi, tp), ("b_pre", pb, qi)], [bk(qi)],
                   start=(tp == 0), stop=(tp == 3))
            yield
        for qi in range(3):
            ACT(tt_[:], banks[qi][:], AF.Tanh, [bk(qi)], ["w512_0"], scale=0.5)
            yield
            STT(cT[:, qi, :], tt_[:], 1.0, banks[qi][:], ALU.add, ALU.mult, ["w512_0", bk(qi)], [("b_cT", pb, qi)])
            yield
        TTN("pool", sq[:], cT[:, 0, :], cT[:, 0, :], ALU.mult, [("b_cT", pb, 0)], [("b_sq", pb)])

    def unit_B(h, nxt_front=None, tail=None):
        gt = tiles
        beta, nbeta, hbeta, g, gc, ngc, bege, egl, el = (gt["g_beta"], gt["g_nbeta"], gt["g_hbeta"], gt["g_g"], gt["g_gc"],
                                                         gt["g_ngc"], gt["g_bege"], gt["g_egl"], gt["g_el"])
        pb, pre, cT, sq, zname, zfl = b_bufs(h)
        zbs = zfl[:].rearrange("p (b n) -> p b n", n=128)
        junk = T("junk", [128, 128])

        def blk_tiles(b):
            f = lambda i: T("blk%d_f%d" % (b, i), [128, 128])
            hh = lambda i: T("blk%d_h%d" % (b, i), [128, 128], BF16)
            kf_ = lambda i: "blk%d_f%d" % (b, i)
            kh_ = lambda i: "blk%d_h%d" % (b, i)
            return f, hh, kf_, kh_

        def pre_block(b):
            blk = slice(b * 128, (b + 1) * 128)
            f, hh, kf_, kh_ = blk_tiles(b)
            sm = T("blk%d_sm" % b, [128, 16])
            smk = lambda i: ("blk_sm", b, i)
            A_, B_, Pm = [f(0), f(1)], [f(2), f(3)], [f(4), f(5)]
            kA, kB, kP = [kf_(0), kf_(1)], [kf_(2), kf_(3)], [kf_(4), kf_(5)]
            decT, dec, egr = f(6), f(7), f(8)
            ktm, kbg, kdec, vb, dgr, knT, TTb, attT, wT, qdT = (hh(i) for i in range(10))
            kps, kk = bslot(4 + b, bf=True)
            TR(kps, cT[:, 1, blk], ident_b[:], [("b_cT", pb, 1), "ident_b"], [kk])
            vps, vk = bslot(4 + b, bf=True)
            TR(vps, cT[:, 2, blk], ident_b[:], [("b_cT", pb, 2), "ident_b"], [vk])
            sqp, sqk = bslot(4 + b)
            MM(sqp[:, 0:1], sq[:, blk], onec_b[:], [("b_sq", pb), "onec_b"], [sqk])
            TS("dve", egr[:], ones[:], g[:, b, h:h + 1], None, ALU.mult, None, ["ones", "g_g"], [kf_(8)])
            grp, grk = bslot(4 + b)
            MM(grp, egr[:], m_up[:], [kf_(8), "m_up"], [grk])
            yield
            ACT(junk[:], kps, AF.Square, [kk], ["junk", smk(0)], accum_out=sm[:, 0:1])
            CP("dve", ktm[:], kps, [kk], [kh_(0)])
            TS("dve", vb[:], vps, hbeta[:, b, h:h + 1], None, ALU.mult, None, [vk, "g_hbeta"], [kh_(3)])
            TS("dve", decT[:], grp, gc[:, b, h:h + 1], 0.0, ALU.subtract, ALU.min, [grk, "g_gc"], [kf_(6)])
            TS("dve", dec[:], grp, gc[:, b, h:h + 1], 0.0, ALU.subtract, ALU.max, [grk, "g_gc"], [kf_(7)])
            ACT(egr[:], grp, AF.Exp, [grk, kf_(8)], [kf_(8)])
            ACT(decT[:], decT[:], AF.Exp, [kf_(6)], [kf_(6)])
            ACT(dec[:], dec[:], AF.Exp, [kf_(7)], [kf_(7)], scale=-1.0)
            TS("dve", sm[:, 1:2], sm[:, 0:1], 4e-6, None, ALU.add, None, [smk(0)], [smk(1)])
            TTN("pool", sm[:, 1:2], sm[:, 1:2], mhalf[:], ALU.pow, [smk(1), "mhalf"], [smk(1)])
            TTN("dve", sm[:, 2:3], sm[:, 1:2], bege[:, b, h:h + 1], ALU.mult, [smk(1), "g_bege"], [smk(2)])
            TTN("dve", sm[:, 3:4], sm[:, 1:2], egl[:, b, h:h + 1], ALU.mult, [smk(1), "g_egl"], [smk(3)])
            ACT(dgr[:], ident_b[:], AF.Copy, ["ident_b", smk(1)], [kh_(4)], scale=sm[:, 1:2])
            P.add("pool", lambda e, decT=decT: e.affine_select(out=decT[:], in_=decT[:], pattern=[[1, 128]],
                                                               compare_op=ALU.is_ge, fill=0.0, base=0, channel_multiplier=-1),
                  [kf_(6)], [kf_(6)])
            P.add("pool", lambda e, dec=dec: e.affine_select(out=dec[:], in_=dec[:], pattern=[[-1, 128]],
                                                             compare_op=ALU.is_gt, fill=0.0, base=0, channel_multiplier=1),
                  [kf_(7)], [kf_(7)])
            TS("dve", sm[:, 4:5], sqp[:, 0:1], 4 * 128e-5, 4 * 128e-5 * 4e-6, ALU.mult, ALU.add, [sqk], [smk(4)])
            yield
            TS("dve", kbg[:], kps, sm[:, 2:3], None, ALU.mult, None, [kk, smk(2)], [kh_(1)])
            ACT(kdec[:], kps, AF.Copy, [kk, smk(3)], [kh_(2)], scale=sm[:, 3:4])
            knp, knk = bslot(4 + b)
            MM(knp, ktm[:], dgr[:], [kh_(0), kh_(4)], [knk])
            TTN("dve", qdT[:], cT[:, 0, blk], egr[:], ALU.mult, [("b_cT", pb, 0), kf_(8)], [kh_(9)])
            yield
            CP("act", knT[:], knp, [knk], [kh_(5)])
            yield
            kkp, kkk = bslot(4 + b)
            MM(kkp, knT[:], knT[:], [kh_(5)], [kkk])
            qkp, qkk = bslot(4 + b)
            MM(qkp, knT[:], cT[:, 0, blk], [kh_(5), ("b_cT", pb, 0)], [qkk])
            yield
            STT(A_[0][:], kkp, nbeta[:, b, h:h + 1], dec[:], ALU.mult, ALU.mult, [kkk, "g_nbeta", kf_(7)], [kA[0]])
            TTN("dve", attT[:], qkp, decT[:], ALU.mult, [qkk, kf_(6)], [kh_(7)])
            yield
            atp, atk = bslot(4 + b)
            TR(atp, A_[0][:], ident[:], [kA[0], "ident"], [atk])
            yield
            CP("act", B_[0][:], atp, [atk], [kB[0]])
            TTN("dve", Pm[0][:], atp, ident[:], ALU.add, [atk, "ident"], [kP[0]])
            yield
            cur = 0
            for lv in range(1, 7):
                nxt = 1 - cur
                ap_, ak = bslot(4 + b)
                MM(ap_, B_[cur][:], A_[cur][:], [kB[cur], kA[cur]], [ak])
                if lv < 6:
                    bp_, bk_ = bslot(4 + b)
                    MM(bp_, A_[cur][:], B_[cur][:], [kB[cur], kA[cur]], [bk_])
                yield
                CP("act", A_[nxt][:], ap_, [ak], [kA[nxt]])
                if lv < 6:
                    CP("dve", B_[nxt][:], bp_, [bk_], [kB[nxt]])
                yield
                pp_, pk = bslot(4 + b)
                MM(pp_, A_[nxt][:], Pm[cur][:], [kA[nxt], kP[cur]], [pk])
                yield
                if lv < 6:
                    TTN("dve", Pm[nxt][:], pp_, Pm[cur][:], ALU.add, [pk, kP[cur]], [kP[nxt]])
                else:
                    TTN("dve", TTb[:], pp_, Pm[cur][:], ALU.add, [pk, kP[cur]], [kh_(6)])
                cur = nxt
            yield
            up_, uk = bslot(4 + b)
            MM(up_, TTb[:], vb[:], [kh_(6), kh_(3)], [uk])
            wp_, wk = bslot(4 + b)
            MM(wp_, kbg[:], TTb[:], [kh_(1), kh_(6)], [wk])
            yield
            CP("act", dec[:], up_, [uk], [kf_(7)])
            CP("act", wT[:], wp_, [wk], [kh_(8)])

        rr([nxt_front] + [pre_block(b) for b in range(NB)] + [tail], fast=nxt_front)

        ops_ = {}
        for b in range(NB):
            f, hh, kf_, kh_ = blk_tiles(b)
            u_sb = f(7)
            kdec, attT, wT, qdT = hh(2), hh(7), hh(8), hh(9)
            vnew = hh(10)
            wsp, wsk = bslot(4 + b)
            MM(wsp, wT[:], Sg_b[:, h, :], [kh_(8), ("Sg_b", h)], [wsk])
            TTN("dve", vnew[:], u_sb[:], wsp, ALU.subtract, [kf_(7), wsk], [kh_(10)])
            op_, ok = bslot(4 + b)
            MM(op_, qdT[:], Sg_b[:, h, :], [kh_(9), ("Sg_b", h)], [ok], start=True, stop=False)
            MM(op_, attT[:], vnew[:], [kh_(7), kh_(10)], [ok], start=False, stop=True)
            dsp, dsk = bslot(4 + b)
            MM(dsp, kdec[:], vnew[:], [kh_(2), kh_(10)], [dsk])
            STT(Sg_b[:, h, :], Sg[:, h, :], el[:, b, h:h + 1], dsp, ALU.mult, ALU.add, [("Sg", h), "g_el", dsk], [("Sg_b", h)])
            STT(Sg[:, h, :], Sg[:, h, :], el[:, b, h:h + 1], dsp, ALU.mult, ALU.add, [("Sg", h), "g_el", dsk], [("Sg", h)])
            ops_[b] = (op_, ok)
        for b in range(NB):
            blk = slice(b * 128, (b + 1) * 128)
            f, hh, kf_, kh_ = blk_tiles(b)
            sm = T("blk%d_sm" % b, [128, 16])
            smk = lambda i: ("blk_sm", b, i)
            ysb = hh(11)
            op_, ok = ops_[b]
            ACT(junk[:], op_, AF.Square, [ok], ["junk", smk(5)], accum_out=sm[:, 5:6])
            STT(sm[:, 6:7], sm[:, 5:6], 4.0 / 128.0, sm[:, 4:5], ALU.mult, ALU.add, [smk(5), smk(4)], [smk(6)])
            TTN("pool", sm[:, 6:7], sm[:, 6:7], mhalf[:], ALU.pow, [smk(6), "mhalf"], [smk(6)])
            STT(ysb[:], op_, sm[:, 6:7], zbs[:, b, :], ALU.mult, ALU.mult, [ok, smk(6), zname], [kh_(11)])

        def tail_gen():
            yield
            yield
            yield
            for b in range(NB):
                blk = slice(b * 128, (b + 1) * 128)
                f, hh, kf_, kh_ = blk_tiles(b)
                yp_, yk = bslot(4 + b, bf=True)
                TR(yp_, hh(11)[:], ident_b[:], [kh_(11), "ident_b"], [yk])
                yield
                CP("act", yT[:, 8 + h, blk], yp_, [yk], ["yT"])
                yield
        return tail_gen()

    def unit_H(h, par, tail=None):
        tf = T("w512_2", [128, TT])[:].rearrange("p (b n) -> p b n", n=128)
        kf = T("w512_3", [128, TT])[:].rearrange("p (b n) -> p b n", n=128)
        lf = T("w512_4", [128, TT])[:].rearrange("p (b n) -> p b n", n=128)
        v_b = T("h_v", [128, NB, 128], BF16)
        zs = T("w512_5", [128, TT])[:].rearrange("p (b n) -> p b n", n=128)
        tq = T("w512_0", [128, TT])
        qs = T("w512_1", [128, TT])
        proj_fm(par, 0, 0)
        for b in range(NB):
            for c in range(8):
                MM(banks[1 + b][:, 0:384], xT[:, c, b * 128:(b + 1) * 128], Wh[par][:, c, 128:512], ["xT", ("Wh", par)],
                   [bk(1 + b)], start=(c == 0), stop=(c == 7))
        ACT(tq[:], banks[0][:], AF.Tanh, [bk(0)], ["w512_0"], scale=0.5)
        STT(qs[:], tq[:], 1.0, banks[0][:], ALU.add, ALU.mult, ["w512_0", bk(0)], ["w512_1"])
        for b in range(NB):
            ACT(tf[:, b, :], banks[1 + b][:, 0:128], AF.Tanh, [bk(1 + b)], [("w512_2", b)], scale=0.5)
            CP("act", v_b[:, b, :], banks[1 + b][:, 128:256], [bk(1 + b)], [("h_v", b)])
            ACT(zs[:, b, :], banks[1 + b][:, 256:384], AF.Tanh, [bk(1 + b)], [("w512_5", b)], scale=0.5)
            STT(zs[:, b, :], zs[:, b, :], 1.0, banks[1 + b][:, 256:384], ALU.add, ALU.mult, [("w512_5", b), bk(1 + b)],
                [("w512_5", b)])
            TTN("dve", zs[:, b, :], zs[:, b, :], gb4[1][:, b, :], ALU.mult, [("w512_5", b), "gb4"], [("w512_5", b)])
            STT(kf[:, b, :], tf[:, b, :], -1.0, nhoml[:, h * 128:(h + 1) * 128], ALU.add, ALU.mult,
                [("w512_2", b), "nhoml"], [("w512_3", b)])
        for b in range(NB):
            ACT(lf[:, b, :], kf[:, b, :], AF.Ln, [("w512_3", b)], [("w512_4", b)], scale=-1.0, bias=1.0)
        lnh = T("lnhalf", [128, 1])
        junk = T("junk", [128, 128])

        def htiles(b):
            f = lambda i: T("blk%d_f%d" % (b, i), [128, 128])
            hh = lambda i: T("blk%d_h%d" % (b, i), [128, 128], BF16)
            kf_ = lambda i: "blk%d_f%d" % (b, i)
            kh_ = lambda i: "blk%d_h%d" % (b, i)
            return f, hh, kf_, kh_

        def pre_h(b):
            blk = slice(b * 128, (b + 1) * 128)
            f, hh, kf_, kh_ = htiles(b)
            enb, ebT = f(0), f(1)
            kt, ktT, attT = hh(0), hh(1), hh(2)
            q2 = q2s[b]
            ebl = T("blk%d_ebl" % b, [128, 2])
            bp_, bk_ = bslot(HB[b])
            MM(bp_, m_bd[:], lf[:, b, :], ["m_bd", ("w512_4", b)], [bk_])
            btp, btk = bslot(HB[b])
            MM(btp, lf[:, b, :], m_bd[:], ["m_bd", ("w512_4", b)], [btk])
            yield
            ACT(enb[:], bp_, AF.Exp, [bk_], [kf_(0)], scale=-1.0)
            ACT(ebT[:], btp, AF.Exp, [btk, "lnhalf"], [kf_(1)], bias=lnh[:, 0:1])
            ACT(ebl[:, 0:1], btp[:, 63:64], AF.Exp, [btk], [("ebl", b)])
            ACT(ebl[:, 1:2], btp[:, 127:128], AF.Exp, [btk], [("ebl", b)])
            yield
            TTN("dve", kt[:], kf[:, b, :], enb[:], ALU.mult, [("w512_3", b), kf_(0)], [kh_(0)])
            q2v = q2[:].rearrange("p (c x) -> p c x", x=192)[:, :, 0:64]
            TTN("dve", q2v, qs[:, blk].rearrange("p (c x) -> p c x", x=64), ebT[:].rearrange("p (c x) -> p c x", x=64),
                ALU.mult, ["w512_1", kf_(1)], [("q2", b)])
            yield
            ktp, ktk = bslot(HB[b], bf=True)
            TR(ktp, kt[:], ident_b[:], [kh_(0), "ident_b"], [ktk])
            d0p, d0k = bslot(HB[b])
            MM(d0p, kt[0:64, :], v_b[0:64, b, :], [kh_(0), ("h_v", b)], [d0k])
            d1p, d1k = banks[1 + b][:, 128:256], ("slot", 1 + b, 1)
            MM(d1p, kt[64:128, :], v_b[64:128, b, :], [kh_(0), ("h_v", b)], [d1k])
            yield
            G0, G1 = f(2), f(3)
            TS("dve", G0[:], d0p, ebl[:, 0:1], None, ALU.mult, None, [d0k, ("ebl", b)], [kf_(2)])
            TS("dve", G1[:], d1p, ebl[:, 1:2], None, ALU.mult, None, [d1k, ("ebl", b)], [kf_(3)])
            CP("act", ktT[:], ktp, [ktk], [kh_(1)])
            yield
            atp, atk = bslot(HB[b])
            MM(atp, ktT[:], q2v, [kh_(1), ("q2", b)], [atk])
            yield
            TTN("dve", attT[:], atp, m_bd[:], ALU.mult, [atk, "m_bd"], [kh_(2)])
            yield
            MM(banks[1 + b][:, 0:128], attT[:], v_b[:, b, :], [kh_(2), ("h_v", b)], [bk(1 + b)], start=True, stop=False)

        rr([pre_h(b) for b in range(NB)] + [tail])

        Pp = [T("h_P0", [128, 128]), T("h_P1", [128, 128])]
        cur, curk = Sh[:, h, :], ("Sh", h)
        idx = 0
        for b in range(NB):
            f, hh, kf_, kh_ = htiles(b)
            q2 = q2s[b]
            ebl = T("blk%d_ebl" % b, [128, 2])
            op_, ok = banks[1 + b][:, 0:128], bk(1 + b)
            for cc in range(2):
                if idx == 0:
                    sbf, sbk = Sh_b[:, h, :], ("Sh_b", h)
                else:
                    sbf, sbk = hh(4 + cc)[:], kh_(4 + cc)
                    CP("act", sbf, cur, [curk], [sbk])
                MM(op_, q2[:, cc * 128:(cc + 1) * 128], sbf, [("q2", b), sbk], [ok], start=False, stop=(cc == 1))
                if idx == 2 * NB - 1:
                    nxt, nxtk = Sh[:, h, :], ("Sh", h)
                else:
                    nxt, nxtk = Pp[idx % 2][:], "h_P%d" % (idx % 2)
                STT(nxt, cur, ebl[:, cc:cc + 1], f(2 + cc)[:], ALU.mult, ALU.add, [curk, ("ebl", b), kf_(2 + cc)], [nxtk])
                cur, curk = nxt, nxtk
                idx += 1
        CP("act", Sh_b[:, h, :], Sh[:, h, :], [("Sh", h)], [("Sh_b", h)])
        for b in range(NB):
            blk = slice(b * 128, (b + 1) * 128)
            f, hh, kf_, kh_ = htiles(b)
            ysb = hh(3)
            sm = T("blk%d_sm" % b, [128, 16])
            smk = lambda i: ("blk_sm", b, i)
            op_, ok = banks[1 + b][:, 0:128], bk(1 + b)
            ACT(junk[:], op_, AF.Square, [ok], ["junk", smk(0)], accum_out=sm[:, 0:1])
            TS("dve", sm[:, 1:2], sm[:, 0:1], 4.0 / 128.0, 4 * LN_EPS, ALU.mult, ALU.add, [smk(0)], [smk(1)])
            TTN("pool", sm[:, 1:2], sm[:, 1:2], mhalf[:], ALU.pow, [smk(1), "mhalf"], [smk(1)])
            STT(ysb[:], op_, sm[:, 1:2], zs[:, b, :], ALU.mult, ALU.mult, [ok, smk(1), ("w512_5", b)], [kh_(3)])

        def tail_gen():
            yield
            yield
            for b in range(NB):
                blk = slice(b * 128, (b + 1) * 128)
                f, hh, kf_, kh_ = htiles(b)
                yp_, yk = bslot(HB[b], bf=True)
                TR(yp_, hh(3)[:], ident_b[:], [kh_(3), "ident_b"], [yk])
                yield
                CP("act", yT[:, h, blk], yp_, [yk], ["yT"])
                yield
        return tail_gen()

    lnh = T("lnhalf", [128, 1])
    P.add("pool", lambda e: e.memset(lnh[:], float(np.log(0.5))), (), ["lnhalf"])
    HB = [5, 6, 7, 0]
    q2s = [T("h_q2_%d" % b, [128, 384], BF16) for b in range(NB)]
    for b in range(NB):
        P.add("pool", lambda e, b=b: e.memset(q2s[b][:], 0.0), (), [("q2", b)])

    for t in range(NT):
        DMA("sp", xres[:], x_d[t * TT:(t + 1) * TT, :].rearrange("(b p) f -> p b f", p=128), c_x, [], ["xres"])
        if marks is not None:
            marks.append(('tile_start', len(P.ops)))
        make_xT(0)
        if marks is not None:
            marks.append(('xT_end', len(P.ops)))
        par = load_unit(0, 0)
        gdn_gates()
        if marks is not None:
            marks.append(('gates_end', len(P.ops)))
        for u in range(8):
            npar = load_unit(0, u + 1)
            unit_A(u, par)
            par = npar
        pars = {8: par, 9: load_unit(0, 9)}
        rr([front_B(0, pars[8])])
        tail = None
        for h in range(8):
            if h + 2 < 8:
                pars[8 + h + 2] = load_unit(0, 8 + h + 2)
            nf = front_B(h + 1, pars[8 + h + 1]) if h < 7 else None
            tail = unit_B(h, nf, tail)
        rr([tail])
        epilogue(0, t)
        if n_layers > 1:
            make_xT(2)
            par = load_unit(1, 0)
            tail = None
            for u in range(16):
                npar = load_unit(1, u + 1) if u < 15 else None
                tail = unit_H(u, par, tail)
                par = npar
            rr([tail])
            epilogue(1, t)
        DMA("act", out_d[t * TT:(t + 1) * TT, :].rearrange("(b p) f -> p b f", p=128), xres[:], c_out, ["xres"], ["out"])

    if max_ops is not None:
        P.ops = P.ops[:max_ops]
    P.finish()
    P.emit()
    nc_allow.__exit__(None, None, None)
    es.close()
    return nc, len(P.ops)


_KEYS = ["x", "p", "w_in_even", "conv_a_w", "conv_b_w", "a_log", "dt_bias", "gdn_norm_g", "w_out_even", "w_in_odd",
         "lower_bounds", "hgrn_norm_g", "w_out_odd", "ln_g", "ln_b", "w_pl", "w_pl_gate"]


def make_in_maps(inputs, n_cores, S):
    f = lambda a: np.ascontiguousarray(np.asarray(a, dtype=np.float32))
    shared = {
        "w_in_even": f(inputs["w_in_even"][0]), "conv_a_w": f(inputs["conv_a_w"][0]),
        "conv_b_w": f(inputs["conv_b_w"][0]), "a_log": f(inputs["a_log"]), "dt_bias": f(inputs["dt_bias"]),
        "gdn_norm_g": f(inputs["gdn_norm_g"]), "w_out_even": f(inputs["w_out_even"][0]),
        "w_in_odd": f(inputs["w_in_odd"][0]), "lower_bounds": f(inputs["lower_bounds"]),
        "hgrn_norm_g": f(inputs["hgrn_norm_g"]), "w_out_odd": f(inputs["w_out_odd"][0]),
        "ln_g": f(inputs["ln_g"]), "ln_b": f(inputs["ln_b"]), "w_pl": f(inputs["w_pl"]),
        "w_pl_gate": f(inputs["w_pl_gate"]),
    }
    maps = []
    for b in range(n_cores):
        m = dict(shared)
        m["x"] = f(inputs["x"][b, :S])
        m["p"] = f(inputs["p"][:, b, :S])
        maps.append(m)
    return maps


def kernel(**inputs):
    S = inputs["x"].shape[1]
    nb = inputs["x"].shape[0]
    nc, _ = build(S=S)
    in_maps = make_in_maps(inputs, nb, S)
    res = run_bass_kernel_spmd(nc, in_maps, core_ids=list(range(nb)))
    return np.stack([np.asarray(r["out"], dtype=np.float32) for r in res.results], axis=0)
```

```python
import numpy as np
from contextlib import ExitStack
import concourse.bass as bass
import concourse.mybir as mybir
from concourse.bass_utils import run_bass_kernel_spmd

F32 = mybir.dt.float32
BF16 = mybir.dt.bfloat16
AF = mybir.ActivationFunctionType
ALU = mybir.AluOpType

D = 1024
TT = 512
NB = TT // 128
ALPHA = 4.0 ** 0.25
LN_EPS = 1e-5


class Op:
    __slots__ = ("eng", "fn", "reads", "writes", "chan", "ninc", "deps", "signal", "count", "idx")

    def __init__(self, eng, fn, reads, writes, chan, ninc):
        self.eng, self.fn, self.reads, self.writes, self.chan, self.ninc = eng, fn, reads, writes, chan, ninc
        self.deps = set()
        self.signal = chan is not None
        self.count = 0


class Prog:
    def __init__(self, nc):
        self.nc = nc
        self.ops = []
        self.nchan = 0
        self.lines = None

    def chan(self):
        self.nchan += 1
        return ("chan", self.nchan)

    def add(self, eng, fn, reads=(), writes=(), chan=None, ninc=1):
        op = Op(eng, fn, tuple(reads), tuple(writes), chan, ninc)
        op.idx = len(self.ops)
        if self.lines is not None:
            import sys as _s
            f = _s._getframe(1)
            while f.f_code.co_name in ("add", "MM", "TR", "ACT", "TS", "TTN", "STT", "CP", "DMA"):
                f = f.f_back
            self.lines.append(f.f_lineno)
        self.ops.append(op)
        return op

    def finish(self):
        ops = self.ops
        lastw, lastr = {}, {}

        def ch(o):
            return o.chan if o.chan is not None else o.eng

        def bank_of(k):
            if isinstance(k, tuple) and k[0] == "slot":
                return k[1]
            if isinstance(k, str) and k.startswith("bank"):
                return int(k[4:])
            return None

        bank_rd = {}
        bank_wr = {}
        for o in ops:
            deps = set()
            my = ch(o)
            for k in o.reads:
                deps.update(lastw.get(k, {}).values())
                b = bank_of(k)
                if b is not None:
                    deps.update(j for c, j in bank_rd.get(b, {}).items() if c != my)
                    bank_rd.setdefault(b, {})[my] = o.idx
                    if b in bank_wr:
                        deps.add(bank_wr[b])
            for k in o.writes:
                deps.update(lastw.get(k, {}).values())
                deps.update(lastr.get(k, {}).values())
                b = bank_of(k)
                if b is not None and o.eng == "pe":
                    deps.update(bank_rd.get(b, {}).values())
                    bank_wr[b] = o.idx
            deps.discard(o.idx)
            for j in deps:
                p = ops[j]
                if p.chan is None and o.chan is None and p.eng == o.eng:
                    if o.eng == "pe":
                        continue
                o.deps.add(j)
                p.signal = True
            for k in o.reads:
                lastr.setdefault(k, {})[my] = o.idx
            for k in o.writes:
                lastw[k] = {my: o.idx}
                lastr[k] = {}
        cnt = {}
        for o in ops:
            if o.signal:
                c = ch(o)
                cnt[c] = cnt.get(c, 0) + (16 * o.ninc if o.chan is not None else 1)
                o.count = cnt[c]
        self.cnt = cnt

    def emit(self, tail_eng="sp"):
        nc, ops, cnt = self.nc, self.ops, self.cnt
        chans = sorted(cnt.keys(), key=str)
        with ExitStack() as es:
            sems = {c: es.enter_context(nc.semaphore("s_" + (c if isinstance(c, str) else "c%d" % c[1])))
                    for c in chans}
            block = es.enter_context(nc.Block())

            def ch(o):
                return o.chan if o.chan is not None else o.eng

            def run(engname, e):
                waited = {}
                for o in ops:
                    if o.eng != engname:
                        continue
                    need = {}
                    for j in o.deps:
                        p = ops[j]
                        c = ch(p)
                        if p.count > need.get(c, 0):
                            need[c] = p.count
                    for c, v in need.items():
                        if waited.get(c, 0) >= v:
                            continue
                        e.wait_ge(sems[c], v)
                        waited[c] = v
                    ins = o.fn(e)
                    if self.lines is not None and o.chan is None:
                        ins.annotate("L%d" % self.lines[o.idx])
                    if o.signal:
                        if o.chan is not None:
                            for i_ in ins:
                                i_.then_inc(sems[o.chan], 16)
                        else:
                            ins.then_inc(sems[o.eng], 1)
                if engname == tail_eng:
                    for c, v in cnt.items():
                        if waited.get(c, 0) < v:
                            e.wait_ge(sems[c], v)

            block.sync(lambda e: run("sp", e))
            block.tensor(lambda e: run("pe", e))
            block.scalar(lambda e: run("act", e))
            block.vector(lambda e: run("dve", e))
            block.gpsimd(lambda e: run("pool", e))


def build(S=4096, n_layers=2, max_ops=None, marks=None):
    NT = S // TT
    nc = bass.Bass("TRN2", target_bir_lowering=False)

    def din(name, shape):
        return nc.dram_tensor(name, shape, F32, kind="ExternalInput").ap()

    x_d = din("x", [S, D])
    p_d = din("p", [2, S, 256])
    wie_d = din("w_in_even", [D, 8208])
    cwa_d = din("conv_a_w", [3, 1024])
    cwb_d = din("conv_b_w", [4, 3072])
    alog_d = din("a_log", [1, 8])
    dtb_d = din("dt_bias", [1, 8])
    gng_d = din("gdn_norm_g", [1, 128])
    woe_d = din("w_out_even", [2048, D])
    wio_d = din("w_in_odd", [D, 8192])
    lb_d = din("lower_bounds", [2, 2048])
    hng_d = din("hgrn_norm_g", [1, 128])
    woo_d = din("w_out_odd", [2048, D])
    lng_d = din("ln_g", [2, D])
    lnb_d = din("ln_b", [2, D])
    wpl_d = din("w_pl", [2, 256, D])
    wg_d = din("w_pl_gate", [2, D, D])
    out_d = nc.dram_tensor("out", [S, D], F32, kind="ExternalOutput").ap()

    def dscr(name, shape):
        return nc.dram_tensor(name, shape, BF16, kind="Internal").ap()

    wrm = [dscr("wrm_even", [D, 8192]), dscr("wrm_odd", [D, 8192])]
    wba_bf = dscr("wba_bf", [D, 16])
    wo_bf = [dscr("wo_even", [2048, D]), dscr("wo_odd", [2048, D])]
    wg_bf = dscr("wg_bf", [2, D, D])
    wpl_bf = dscr("wpl_bf", [2, 256, D])

    P = Prog(nc)
    if marks is not None:
        P.lines = []
        marks.append(P.lines)
    es = ExitStack()
    tiles = {}

    def T(name, shape, dt=F32):
        if name not in tiles:
            tiles[name] = es.enter_context(nc.sbuf_tensor(name, shape, dt))
        return tiles[name]

    banks = [es.enter_context(nc.psum_tensor("bank%d" % i, [128, 512], F32)) for i in range(8)]

    def bk(i):
        return "bank%d" % i

    def MM(out, lhsT, rhs, r, w, start=True, stop=True):
        P.add("pe", lambda e: e.matmul(out, lhsT=lhsT, rhs=rhs, start=start, stop=stop), r, w)

    def TR(out, in_, ident, r, w):
        P.add("pe", lambda e: e.transpose(out, in_, ident), r, w)

    def ACT(out, in_, func, r, w, **kw):
        P.add("act", lambda e: e.activation(out=out, in_=in_, func=func, **kw), r, w)

    def TS(eng, out, in0, s1, s2, op0, op1, r, w):
        if op1 is None:
            P.add(eng, lambda e: e.tensor_scalar(out, in0, s1, None, op0), r, w)
        else:
            P.add(eng, lambda e: e.tensor_scalar(out, in0, s1, s2, op0, op1), r, w)

    def TTN(eng, out, in0, in1, op, r, w):
        P.add(eng, lambda e: e.tensor_tensor(out, in0, in1, op), r, w)

    def STT(out, in0, scalar, in1, op0, op1, r, w):
        P.add("dve", lambda e: e.scalar_tensor_tensor(out, in0, scalar, in1, op0, op1), r, w)

    def CP(eng, out, in_, r, w):
        if eng == "act":
            ACT(out, in_, AF.Copy, r, w)
        else:
            P.add(eng, lambda e: e.tensor_copy(out, in_), r, w)

    def DMA(eng, out, in_, chan, r, w, **kw):
        P.add(eng, lambda e: [e.dma_start(out=out, in_=in_, **kw)], r, w, chan=chan)

    ones = T("ones", [128, 128])
    ident = T("ident", [128, 128])
    ident_b = T("ident_b", [128, 128], BF16)
    m_up = T("m_up", [128, 128])
    m_bd = T("m_bd", [128, 128])
    m_bd_b = T("m_bd_b", [128, 128], BF16)
    onec_b = T("onec_b", [128, 1], BF16)
    mhalf = T("mhalf", [128, 1])
    P.add("pool", lambda e: e.memset(ones[:], 1.0), (), ["ones"])
    P.add("pool", lambda e: e.memset(onec_b[:], 1.0), (), ["onec_b"])
    P.add("pool", lambda e: e.memset(mhalf[:], -0.5), (), ["mhalf"])
    P.add("pool", lambda e: e.affine_select(out=ident[:], in_=ones[:], pattern=[[-1, 128]], compare_op=ALU.is_equal,
                                            fill=0.0, base=0, channel_multiplier=1), ["ones"], ["ident"])
    P.add("pool", lambda e: e.affine_select(out=m_up[:], in_=ones[:], pattern=[[1, 128]], compare_op=ALU.is_ge,
                                            fill=0.0, base=0, channel_multiplier=-1), ["ones"], ["m_up"])
    P.add("pool", lambda e: e.tensor_copy(ident_b[:], ident[:]), ["ident"], ["ident_b"])
    P.add("pool", lambda e: e.tensor_copy(m_bd[:], m_up[:]), ["m_up"], ["m_bd"])
    P.add("pool", lambda e: e.memset(m_bd[0:64, 64:128], 0.0), ["m_bd"], ["m_bd"])
    P.add("pool", lambda e: e.tensor_copy(m_bd_b[:], m_bd[:]), ["m_bd"], ["m_bd_b"])

    c_const = P.chan()
    lnp = T("lnp", [128, 2, D])
    c_lnp = P.chan()
    xres = T("xres", [128, NB, D])
    yT = T("yT", [128, 16, TT], BF16)
    xres_f = xres[:].rearrange("p b d -> p (b d)")
    yT_f = yT[:].rearrange("p c n -> p (c n)").bitcast(F32)
    lbb = xres_f[:, 0:4096].rearrange("p (a d) -> p a d", a=2)
    nhoml = T("nhoml", [128, 2048])
    aab = T("aab", [128, 16])
    cwr = yT_f[0:4, 0:4096]
    cw = T("cw", [128, 32, 4])
    P.add("sp", lambda e: [e.dma_start(out=xres_f[:, 0:4096],
                                       in_=lb_d.rearrange("a d -> (a d)").partition_broadcast(128)),
                           e.dma_start(out=aab[:, 0:8], in_=alog_d.rearrange("a d -> (a d)").partition_broadcast(128)),
                           e.dma_start(out=aab[:, 8:16], in_=dtb_d.rearrange("a d -> (a d)").partition_broadcast(128)),
                           e.dma_start(out=cwr[0:3, 0:1024], in_=cwa_d),
                           e.dma_start(out=cwr[0:4, 1024:4096], in_=cwb_d),
                           ], (), ["xres", "yT", "aab"], chan=c_const, ninc=5)
    nc_allow = nc.allow_non_contiguous_dma(reason="tiny per-partition vectors")
    nc_allow.__enter__()
    TTN("dve", lbb[:, 0, :], lbb[:, 1, :], lbb[:, 0, :], ALU.subtract, ["xres"], ["xres"])
    ACT(lbb[:, 1, :], lbb[:, 0, :], AF.Tanh, ["xres"], ["xres"], scale=0.5)
    TS("dve", nhoml[:], lbb[:, 1, :], 0.25, -0.25, ALU.mult, ALU.add, ["xres"], ["nhoml"])
    ACT(aab[:, 0:8], aab[:, 0:8], AF.Exp, ["aab"], ["aab"])
    TS("dve", aab[:, 0:8], aab[:, 0:8], -1.0, None, ALU.mult, None, ["aab"], ["aab"])
    cwv = banks[0][:, 0:128].rearrange("p (g k) -> p g k", k=4)
    for g in range(32):
        ntap = 3 if g < 8 else 4
        TR(cwv[:, g, 0:ntap], cwr[0:ntap, g * 128:(g + 1) * 128], ident[0:ntap, 0:ntap], ["yT", "ident"], [bk(0)])
    P.add("dve", lambda e: e.memset(cw[:], 0.0), (), ["cw"])
    CP("dve", cw[:, 0:8, 0:3], cwv[:, 0:8, 0:3], [bk(0)], ["cw"])
    CP("dve", cw[:, 8:32, :], cwv[:, 8:32, :], [bk(0)], ["cw"])

    if marks is not None:
        marks.append(('consts_end', len(P.ops)))
    c_cast = {}

    def cast_piece(layer, region, pc):
        ch = P.chan()
        src_d = wie_d if layer == 0 else wio_d
        base = region * 4096
        W = 1024 if layer == 0 else 2048
        cols = [base + m * W + pc * 512 for m in range(4)]
        P.add("pool", lambda e: [e.dma_start(out=wrm[layer][:, c0:c0 + 512], in_=src_d[:, c0:c0 + 512]) for c0 in cols],
              (), [("wp", layer, region, pc)], chan=ch, ninc=4)

    def cast_plain(dst, src, key, nsplit):
        ch = P.chan()
        rows = src.shape[0]
        step = rows // nsplit
        P.add("pool", lambda e: [e.dma_start(out=dst[i * step:(i + 1) * step], in_=src[i * step:(i + 1) * step])
                                 for i in range(nsplit)], (), [key], chan=ch, ninc=nsplit)

    ep_big = T("ep_big", [128, 2, D])
    pst = T("pst", [128, NB, 256])
    gb4 = [T("gb4_0", [128, NB, 128]), T("gb4_1", [128, NB, 128])]
    c_gb = P.chan()
    P.add("sp", lambda e: [e.dma_start(out=gb4[l][:, b, :], in_=(gng_d, hng_d)[l][0].partition_broadcast(128))
                           for l in range(2) for b in range(NB)], (), ["gb4"], chan=c_gb, ninc=2 * NB)

    cast_piece(0, 0, 0)
    cast_plain(wba_bf, wie_d[:, 8192:8208], "wba", 1)
    cast_piece(0, 0, 1)
    cast_piece(0, 1, 0)
    cast_piece(0, 1, 1)
    cast_plain(wo_bf[0], woe_d, ("wo", 0), 4)
    cast_plain(wg_bf[0], wg_d[0], ("wg", 0), 2)
    cast_plain(wpl_bf[0], wpl_d[0], ("wpl", 0), 1)
    if n_layers > 1:
        for pc in range(4):
            cast_piece(1, 0, pc)
        cast_plain(wo_bf[1], woo_d, ("wo", 1), 4)
        cast_plain(wg_bf[1], wg_d[1], ("wg", 1), 2)
        cast_plain(wpl_bf[1], wpl_d[1], ("wpl", 1), 1)

    if marks is not None:
        marks.append(('casts_end', len(P.ops)))
    xT = T("xT", [128, 8, TT], BF16)
    Wh = [T("Wh0", [128, 8, 512], BF16), T("Wh1", [128, 8, 512], BF16)]
    wba = T("wba", [128, 8, 16], BF16)
    wpl = T("wpl", [128, 2, D], BF16)
    woutq = [T("woutq0", [128, 16, 256], BF16), T("woutq1", [128, 16, 256], BF16)]
    c_woutq = [P.chan(), P.chan()]
    pT = T("pT", [128, 2, TT], BF16)
    Sg = T("Sg", [128, 8, 128])
    Sg_b = T("Sg_b", [128, 8, 128], BF16)
    Sh = T("Sh", [128, 16, 128])
    Sh_b = T("Sh_b", [128, 16, 128], BF16)
    hist_a = T("hist_a", [128, 8, 2], BF16)
    hist_b = T("hist_b", [128, 24, 3], BF16)
    for t_, k_ in ((Sg, "Sg"), (Sg_b, "Sg_b"), (Sh, "Sh"), (Sh_b, "Sh_b"), (hist_a, "hist_a"), (hist_b, "hist_b")):
        P.add("pool", lambda e, t_=t_: e.memset(t_[:], 0.0), (), [k_])
    for h in range(8):
        pass
    c_x, c_p, c_wh, c_wout, c_wg, c_wplc, c_wbac, c_out = (P.chan(), P.chan(), [P.chan(), P.chan()], P.chan(),
                                                         P.chan(), P.chan(), P.chan(), P.chan())
    DMA("sp", wba[:], wba_bf.rearrange("(c p) n -> p c n", p=128), c_wbac, ["wba"], ["wba_sb"])

    unit_ctr = [0]

    def load_unit(layer, u):
        par = unit_ctr[0] % 2
        unit_ctr[0] += 1
        if layer == 0:
            region, g, ng = u // 8, u % 8, 8
        else:
            region, g, ng = 0, u, 16
        cols = [region * 4096 + m * 128 * ng + g * 128 for m in range(4)]
        P.add("sp", lambda e: [e.dma_start(out=Wh[par][:, :, m * 128:(m + 1) * 128],
                                           in_=wrm[layer][:, cols[m]:cols[m] + 128].rearrange("(c p) n -> p c n", p=128))
                               for m in range(4)],
              [("wp", layer, region, g // 4)], [("Wh", par)], chan=c_wh[par], ninc=4)
        return par

    bslot_ctr = {}

    def bslot(bank, bf=False):
        q = bslot_ctr.get(bank, 0) % 4
        bslot_ctr[bank] = bslot_ctr.get(bank, 0) + 1
        key = ("slot", bank, q)
        if bf:
            return banks[bank][:].bitcast(BF16)[:, q * 256:q * 256 + 128], key
        return banks[bank][:, q * 128:(q + 1) * 128], key

    def make_xT(rot):
        for c in range(8):
            b_ = 4 + (c + rot) % 4
            for b in range(NB):
                TR(banks[b_][:, b * 128:(b + 1) * 128], xres[:, b, c * 128:(c + 1) * 128], ident[:],
                   ["xres", "ident"], [bk(b_)])
            CP("act" if c % 2 == 0 else "dve", xT[:, c, :], banks[b_][:], [bk(b_)], ["xT"])

    def epilogue(layer, t):
        st6s = [T("ep_st6_%d" % b, [128, 2, 6]) for b in range(NB)]
        mvs = [T("ep_mv_%d" % b, [128, 2]) for b in range(NB)]
        rss = [T("ep_rs_%d" % b, [128, 2]) for b in range(NB)]
        DMA("sp", wpl[:], wpl_bf[layer].rearrange("(c p) n -> p c n", p=128), c_wplc, [("wpl", layer)], ["wpl"])
        DMA("sp", pst[:], p_d[layer, t * TT:(t + 1) * TT, :].rearrange("(b p) f -> p b f", p=128), c_p, [], ["pst"])
        wov = wo_bf[layer].rearrange("(c p) n -> p c n", p=128)
        P.add("sp", lambda e: [e.dma_start(out=lnp[:, 0, :], in_=lng_d[layer].partition_broadcast(128)),
                               e.dma_start(out=lnp[:, 1, :], in_=lnb_d[layer].partition_broadcast(128))],
              (), ["lnp"], chan=c_lnp, ninc=2)
        for q in range(4):
            wq = woutq[q % 2]
            DMA("sp", wq[:], wov[:, :, q * 256:(q + 1) * 256], c_woutq[q % 2], [("wo", layer)], [("woutq", q % 2)])
            for b in range(NB):
                bn = (q * NB + b) % 4
                for kc in range(16):
                    MM(banks[bn][:, 0:256], yT[:, kc, b * 128:(b + 1) * 128], wq[:, kc, :],
                       ["yT", ("woutq", q % 2)], [bk(bn)], start=(kc == 0), stop=(kc == 15))
                STT(xres[:, b, q * 256:(q + 1) * 256], xres[:, b, q * 256:(q + 1) * 256], ALPHA, banks[bn][:, 0:256],
                    ALU.mult, ALU.add, ["xres", ("xres_b", b), bk(bn)], [("xres_b", b)])
        for c in range(2):
            b_ = 4 + c
            for b in range(NB):
                TR(banks[b_][:, b * 128:(b + 1) * 128], pst[:, b, c * 128:(c + 1) * 128], ident[:],
                   ["pst", "ident"], [bk(b_)])
            CP("act", pT[:, c, :], banks[b_][:], [bk(b_)], ["pT"])
        def ln_gen(b):
            st6, mv, rs = st6s[b], mvs[b], rss[b]
            tmp, tk = ep_big[:, b % 2, :], ("epb", b % 2)
            for n in range(2):
                P.add("dve", lambda e, n=n, b=b, st6=st6: e.bn_stats(out=st6[:, n, :], in_=xres[:, b, n * 512:(n + 1) * 512]),
                      [("xres_b", b)], [("ep_st6", b)])
            yield
            P.add("dve", lambda e, st6=st6, mv=mv: e.bn_aggr(out=mv[:], in_=st6[:].rearrange("p a s -> p (a s)")),
                  [("ep_st6", b)], [("ep_mv", b)])
            TS("dve", rs[:, 0:1], mv[:, 1:2], LN_EPS, None, ALU.add, None, [("ep_mv", b)], [("ep_rs", b)])
            yield
            TTN("pool", rs[:, 0:1], rs[:, 0:1], mhalf[:], ALU.pow, [("ep_rs", b), "mhalf"], [("ep_rs", b)])
            yield
            STT(rs[:, 1:2], mv[:, 0:1], -1.0, rs[:, 0:1], ALU.mult, ALU.mult, [("ep_mv", b), ("ep_rs", b)], [("ep_rs2", b)])
            yield
            ACT(tmp, xres[:, b, :], AF.Identity, [("xres_b", b), ("ep_rs", b), ("ep_rs2", b)], [tk], scale=rs[:, 0:1], bias=rs[:, 1:2])
            yield
            TTN("pool", tmp, tmp, lnp[:, 0, :], ALU.mult, [tk, "lnp"], [tk])
            yield
            TTN("dve", xres[:, b, :], tmp, lnp[:, 1, :], ALU.add, [tk, "lnp"], [("xres_b", b)])
            yield
            for half in range(2):
                b_ = 4 + 2 * (b % 2) + half
                for cc in range(4):
                    c = half * 4 + cc
                    TR(banks[b_][:, cc * 128:(cc + 1) * 128], xres[:, b, c * 128:(c + 1) * 128], ident[:],
                       [("xres_b", b), "ident"], [bk(b_)])
                CP("act" if half == 0 else "dve", xT[:, half * 4:half * 4 + 4, b * 128:(b + 1) * 128],
                   banks[b_][:].rearrange("p (c n) -> p c n", n=128), [bk(b_)], [("xT_b", b)])
                yield

        for b0 in range(0, NB, 2):
            rr([ln_gen(b0), ln_gen(b0 + 1)])
        wgv = [Wh[0][:].rearrange("p c n -> p (c n)"), Wh[1][:].rearrange("p c n -> p (c n)")]
        P.add("sp", lambda e: [e.dma_start(out=wgv[hh].rearrange("p (c n) -> p c n", c=4),
                                           in_=wg_bf[layer][hh * 512:(hh + 1) * 512].rearrange("(c p) n -> p c n", p=128))
                               for hh in range(2)], [("wg", layer)], [("Wh", 0), ("Wh", 1)], chan=c_wg, ninc=2)

        def wgs(c, n):
            return wgv[c // 4][:, (c % 4) * 1024 + n * 512:(c % 4) * 1024 + (n + 1) * 512]

        for b in range(NB):
            tg, tgk = ep_big[:, b % 2, :], ("epb", b % 2)
            for n in range(2):
                for c in range(8):
                    MM(banks[n][:], xT[:, c, b * 128:(b + 1) * 128], wgs(c, n), [("xT_b", b), ("Wh", 0), ("Wh", 1)], [bk(n)],
                       start=(c == 0), stop=(c == 7))
                for c in range(2):
                    MM(banks[2 + n][:], pT[:, c, b * 128:(b + 1) * 128], wpl[:, c, n * 512:(n + 1) * 512],
                       ["pT", "wpl"], [bk(2 + n)], start=(c == 0), stop=(c == 1))
                ACT(tg[:, n * 512:(n + 1) * 512], banks[n][:], AF.Tanh, [bk(n)], [tgk], scale=0.5)
                STT(tg[:, n * 512:(n + 1) * 512], tg[:, n * 512:(n + 1) * 512], 1.0, banks[2 + n][:],
                    ALU.add, ALU.mult, [tgk, bk(2 + n)], [tgk])
            STT(xres[:, b, :], tg, 0.5, xres[:, b, :], ALU.mult, ALU.add, [tgk, ("xres_b", b)], [("xres_b", b)])
        P.add("dve", lambda e: e.tensor_copy(mvs[0][:, 0:1], mvs[0][:, 0:1]),
              [("xres_b", b) for b in range(NB)] + [("ep_mv", 0)], ["xres", ("ep_mv", 0)])

    def conv_diag(g, ntap, scale):
        dg = T("diag", [128, 4, 128], BF16)
        for tp in range(ntap):
            TS("pool", dg[:, tp, :], ident_b[:], cw[:, g, tp:tp + 1], scale, ALU.mult, ALU.mult,
               ["ident_b", "cw"], [("diag", tp)])
        return dg

    def proj_fm(par, col, bank):
        for c in range(8):
            MM(banks[bank][:], Wh[par][:, c, col * 128:(col + 1) * 128], xT[:, c, :], [("Wh", par), "xT"], [bk(bank)],
               start=(c == 0), stop=(c == 7))

    def unit_A(j, par):
        h_sb = T("w512_0", [128, TT])
        u_bf = T("a_u", [128, TT + 2], BF16)
        tz = T("w512_1", [128, TT])
        bz = T("w512_2", [128, TT])
        o_ = 4 * (j % 2)
        for col in range(4):
            proj_fm(par, col, o_ + col)
        CP("act", h_sb[:], banks[o_][:], [bk(o_)], ["w512_0"])
        CP("pool", u_bf[:, 0:2], hist_a[:, j, :], ["hist_a"], ["a_u"])
        TTN("dve", u_bf[:, 2:TT + 2], banks[o_ + 1][:], h_sb[:], ALU.mult, [bk(o_ + 1), "w512_0"], ["a_u"])
        CP("pool", hist_a[:, j, :], u_bf[:, TT:TT + 2], ["a_u"], ["hist_a"])
        ACT(tz[:], banks[o_ + 3][:], AF.Tanh, [bk(o_ + 3)], ["w512_1"], scale=0.5)
        STT(tz[:], tz[:], 1.0, banks[o_ + 3][:], ALU.add, ALU.mult, ["w512_1", bk(o_ + 3)], ["w512_1"])
        TTN("dve", bz[:], banks[o_ + 2][:], tz[:], ALU.mult, [bk(o_ + 2), "w512_1"], ["w512_2"])
        dg = conv_diag(j, 3, 0.5)
        for tp in range(3):
            MM(banks[o_][:], dg[:, tp, :], u_bf[:, tp:tp + TT], [("diag", tp), "a_u"], [bk(o_)], start=(tp == 0), stop=(tp == 2))
        TTN("dve", yT[:, j, :], banks[o_][:], bz[:], ALU.mult, [bk(o_), "w512_2"], ["yT"])

    def gdn_gates():
        bav = banks[3][:, 0:NB * 16].rearrange("p (b n) -> p b n", n=16)
        for b in range(NB):
            for c in range(8):
                MM(bav[:, b, :], xT[:, c, b * 128:(b + 1) * 128], wba[:, c, :], ["xT", "wba_sb"], [bk(3)],
                   start=(c == 0), stop=(c == 7))
        gt = T("g_t", [128, NB, 8])
        beta = T("g_beta", [128, NB, 8])
        nbeta = T("g_nbeta", [128, NB, 8])
        hbeta = T("g_hbeta", [128, NB, 8])
        g = T("g_g", [128, NB, 8])
        gc = T("g_gc", [128, NB, 8])
        ngc = T("g_ngc", [128, NB, 8])
        egc = T("g_egc", [128, NB, 8])
        bege = T("g_bege", [128, NB, 8])
        egl = T("g_egl", [128, NB, 8])
        el = T("g_el", [128, NB, 8])
        ACT(gt[:], bav[:, :, 0:8], AF.Tanh, [bk(3)], ["g_t"], scale=0.5)
        TS("dve", beta[:], gt[:], 0.5, 0.5, ALU.mult, ALU.add, ["g_t"], ["g_beta"])
        TS("dve", nbeta[:], gt[:], -0.5, -0.5, ALU.mult, ALU.add, ["g_t"], ["g_nbeta"])
        TS("dve", hbeta[:], gt[:], 0.25, 0.25, ALU.mult, ALU.add, ["g_t"], ["g_hbeta"])
        for b in range(NB):
            TTN("dve", g[:, b, :], bav[:, b, 8:16], aab[:, 8:16], ALU.add, [bk(3), "aab"], ["g_g"])
        ACT(g[:], g[:], AF.Exp, ["g_g"], ["g_g"])
        ACT(g[:], g[:], AF.Ln, ["g_g"], ["g_g"], bias=1.0)
        for b in range(NB):
            TTN("dve", g[:, b, :], g[:, b, :], aab[:, 0:8], ALU.mult, ["g_g", "aab"], ["g_g"])
        gcp = banks[3][:, 64:64 + NB * 8].rearrange("p (b n) -> p b n", n=8)
        glp = banks[3][:, 128:128 + NB * 8].rearrange("p (b n) -> p b n", n=8)
        for b in range(NB):
            MM(gcp[:, b, :], m_up[:], g[:, b, :], ["m_up", "g_g"], [bk(3)])
        MM(glp, ones[:], g[:], ["ones", "g_g"], [bk(3)])
        CP("dve", gc[:], gcp, [bk(3)], ["g_gc"])
        TS("dve", ngc[:], gcp, -1.0, None, ALU.mult, None, [bk(3)], ["g_ngc"])
        ACT(egc[:], gcp, AF.Exp, [bk(3)], ["g_egc"])
        TTN("dve", bege[:], egc[:], beta[:], ALU.mult, ["g_egc", "g_beta"], ["g_bege"])
        TTN("dve", egl[:], glp, gc[:], ALU.subtract, [bk(3), "g_gc"], ["g_egl"])
        ACT(egl[:], egl[:], AF.Exp, ["g_egl"], ["g_egl"])
        ACT(el[:], glp, AF.Exp, [bk(3)], ["g_el"])

    def rr(gens, fast=None):
        gens = [g_ for g_ in gens if g_ is not None]
        while gens:
            for g_ in list(gens):
                try:
                    next(g_)
                    if g_ is fast:
                        next(g_)
                except StopIteration:
                    gens.remove(g_)

    def b_bufs(h):
        pb = h % 2
        pre = T("b_pre%d" % pb, [128, 3, TT + 3], BF16)
        cT = T("b_cT%d" % pb, [128, 3, TT], BF16)
        sq = T("b_sq%d" % pb, [128, TT], BF16)
        zname = "w512_1" if pb == 0 else "zbs1"
        zfl = T(zname, [128, TT])
        return pb, pre, cT, sq, zname, zfl

    def front_B(h, par):
        pb, pre, cT, sq, zname, zfl = b_bufs(h)
        zbs = zfl[:].rearrange("p (b n) -> p b n", n=128)
        tt_ = T("w512_0", [128, TT])
        dg3 = T("diag3", [128, 3, 4, 128], BF16)
        for qi in range(3):
            gidx = qi * 8 + h
            for tp in range(4):
                TS("pool", dg3[:, qi, tp, :], ident_b[:], cw[:, 8 + gidx, tp:tp + 1], 1.0, ALU.mult, ALU.mult,
                   ["ident_b", "cw"], [("diag3", qi, tp)])
        yield
        for qi in range(3):
            proj_fm(par, qi, qi)
            yield
        zbv = banks[3][:].rearrange("p (b n) -> p b n", n=128)
        for b in range(NB):
            for c in range(8):
                MM(zbv[:, b, :], xT[:, c, b * 128:(b + 1) * 128], Wh[par][:, c, 384:512], ["xT", ("Wh", par)], [bk(3)],
                   start=(c == 0), stop=(c == 7))
            yield
        for qi in range(3):
            gidx = qi * 8 + h
            CP("pool", pre[:, qi, 0:3], hist_b[:, gidx, :], ["hist_b"], [("b_pre", pb, qi)])
            CP("act", pre[:, qi, 3:TT + 3], banks[qi][:], [bk(qi)], [("b_pre", pb, qi)])
            CP("pool", hist_b[:, gidx, :], pre[:, qi, TT:TT + 3], [("b_pre", pb, qi)], ["hist_b"])
            yield
        ACT(zbs, zbv, AF.Tanh, [bk(3)], [zname], scale=0.5)
        STT(zfl[:], zfl[:], 1.0, banks[3][:], ALU.add, ALU.mult, [zname, bk(3)], [zname])
        TTN("dve", zfl[:], zfl[:], gb4[0][:].rearrange("p b n -> p (b n)"), ALU.mult, [zname, "gb4"], [zname])
        yield
        for qi in range(3):
            for tp in range(4):
                MM(banks[qi][:], dg3[:, qi, tp, :], pre[:, qi, tp:tp + TT], [("diag3", qi, tp), ("b_pre", pb, qi)], [bk(qi)],
                   start=(tp == 0), stop=(tp == 3))
            yield
        for qi in range(3):
            ACT(tt_[:], banks[qi][:], AF.Tanh, [bk(qi)], ["w512_0"], scale=0.5)
            yield
            STT(cT[:, qi, :], tt_[:], 1.0, banks[qi][:], ALU.add, ALU.mult, ["w512_0", bk(qi)], [("b_cT", pb, qi)])
            yield
        TTN("pool", sq[:], cT[:, 0, :], cT[:, 0, :], ALU.mult, [("b_cT", pb, 0)], [("b_sq", pb)])

    def unit_B(h, nxt_front=None, tail=None):
        gt = tiles
        beta, nbeta, hbeta, g, gc, ngc, bege, egl, el = (gt["g_beta"], gt["g_nbeta"], gt["g_hbeta"], gt["g_g"], gt["g_gc"],
                                                         gt["g_ngc"], gt["g_bege"], gt["g_egl"], gt["g_el"])
        pb, pre, cT, sq, zname, zfl = b_bufs(h)
        zbs = zfl[:].rearrange("p (b n) -> p b n", n=128)
        junk = T("junk", [128, 128])

        def blk_tiles(b):
            f = lambda i: T("blk%d_f%d" % (b, i), [128, 128])
            hh = lambda i: T("blk%d_h%d" % (b, i), [128, 128], BF16)
            kf_ = lambda i: "blk%d_f%d" % (b, i)
            kh_ = lambda i: "blk%d_h%d" % (b, i)
            return f, hh, kf_, kh_

        def pre_block(b):
            blk = slice(b * 128, (b + 1) * 128)
            f, hh, kf_, kh_ = blk_tiles(b)
            sm = T("blk%d_sm" % b, [128, 16])
            smk = lambda i: ("blk_sm", b, i)
            A_, B_, Pm = [f(0), f(1)], [f(2), f(3)], [f(4), f(5)]
            kA, kB, kP = [kf_(0), kf_(1)], [kf_(2), kf_(3)], [kf_(4), kf_(5)]
            decT, dec, egr = f(6), f(7), f(8)
            ktm, kbg, kdec, vb, dgr, knT, TTb, attT, wT, qdT = (hh(i) for i in range(10))
            kps, kk = bslot(4 + b, bf=True)
            TR(kps, cT[:, 1, blk], ident_b[:], [("b_cT", pb, 1), "ident_b"], [kk])
            vps, vk = bslot(4 + b, bf=True)
            TR(vps, cT[:, 2, blk], ident_b[:], [("b_cT", pb, 2), "ident_b"], [vk])
            sqp, sqk = bslot(4 + b)
            MM(sqp[:, 0:1], sq[:, blk], onec_b[:], [("b_sq", pb), "onec_b"], [sqk])
            TS("dve", egr[:], ones[:], g[:, b, h:h + 1], None, ALU.mult, None, ["ones", "g_g"], [kf_(8)])
            grp, grk = bslot(4 + b)
            MM(grp, egr[:], m_up[:], [kf_(8), "m_up"], [grk])
            yield
            ACT(junk[:], kps, AF.Square, [kk], ["junk", smk(0)], accum_out=sm[:, 0:1])
            CP("dve", ktm[:], kps, [kk], [kh_(0)])
            TS("dve", vb[:], vps, hbeta[:, b, h:h + 1], None, ALU.mult, None, [vk, "g_hbeta"], [kh_(3)])
            TS("dve", decT[:], grp, gc[:, b, h:h + 1], 0.0, ALU.subtract, ALU.min, [grk, "g_gc"], [kf_(6)])
            TS("dve", dec[:], grp, gc[:, b, h:h + 1], 0.0, ALU.subtract, ALU.max, [grk, "g_gc"], [kf_(7)])
            ACT(egr[:], grp, AF.Exp, [grk, kf_(8)], [kf_(8)])
            ACT(decT[:], decT[:], AF.Exp, [kf_(6)], [kf_(6)])
            ACT(dec[:], dec[:], AF.Exp, [kf_(7)], [kf_(7)], scale=-1.0)
            TS("dve", sm[:, 1:2], sm[:, 0:1], 4e-6, None, ALU.add, None, [smk(0)], [smk(1)])
            TTN("pool", sm[:, 1:2], sm[:, 1:2], mhalf[:], ALU.pow, [smk(1), "mhalf"], [smk(1)])
            TTN("dve", sm[:, 2:3], sm[:, 1:2], bege[:, b, h:h + 1], ALU.mult, [smk(1), "g_bege"], [smk(2)])
            TTN("dve", sm[:, 3:4], sm[:, 1:2], egl[:, b, h:h + 1], ALU.mult, [smk(1), "g_egl"], [smk(3)])
            ACT(dgr[:], ident_b[:], AF.Copy, ["ident_b", smk(1)], [kh_(4)], scale=sm[:, 1:2])
            P.add("pool", lambda e, decT=decT: e.affine_select(out=decT[:], in_=decT[:], pattern=[[1, 128]],
                                                               compare_op=ALU.is_ge, fill=0.0, base=0, channel_multiplier=-1),
                  [kf_(6)], [kf_(6)])
            P.add("pool", lambda e, dec=dec: e.affine_select(out=dec[:], in_=dec[:], pattern=[[-1, 128]],
                                                             compare_op=ALU.is_gt, fill=0.0, base=0, channel_multiplier=1),
                  [kf_(7)], [kf_(7)])
            TS("dve", sm[:, 4:5], sqp[:, 0:1], 4 * 128e-5, 4 * 128e-5 * 4e-6, ALU.mult, ALU.add, [sqk], [smk(4)])
            yield
            TS("dve", kbg[:], kps, sm[:, 2:3], None, ALU.mult, None, [kk, smk(2)], [kh_(1)])
            ACT(kdec[:], kps, AF.Copy, [kk, smk(3)], [kh_(2)], scale=sm[:, 3:4])
            knp, knk = bslot(4 + b)
            MM(knp, ktm[:], dgr[:], [kh_(0), kh_(4)], [knk])
            TTN("dve", qdT[:], cT[:, 0, blk], egr[:], ALU.mult, [("b_cT", pb, 0), kf_(8)], [kh_(9)])
            yield
            CP("act", knT[:], knp, [knk], [kh_(5)])
            yield
            kkp, kkk = bslot(4 + b)
            MM(kkp, knT[:], knT[:], [kh_(5)], [kkk])
            qkp, qkk = bslot(4 + b)
            MM(qkp, knT[:], cT[:, 0, blk], [kh_(5), ("b_cT", pb, 0)], [qkk])
            yield
            STT(A_[0][:], kkp, nbeta[:, b, h:h + 1], dec[:], ALU.mult, ALU.mult, [kkk, "g_nbeta", kf_(7)], [kA[0]])
            TTN("dve", attT[:], qkp, decT[:], ALU.mult, [qkk, kf_(6)], [kh_(7)])
            yield
            atp, atk = bslot(4 + b)
            TR(atp, A_[0][:], ident[:], [kA[0], "ident"], [atk])
            yield
            CP("act", B_[0][:], atp, [atk], [kB[0]])
            TTN("dve", Pm[0][:], atp, ident[:], ALU.add, [atk, "ident"], [kP[0]])
            yield
            cur = 0
            for lv in range(1, 7):
                nxt = 1 - cur
                ap_, ak = bslot(4 + b)
                MM(ap_, B_[cur][:], A_[cur][:], [kB[cur], kA[cur]], [ak])
                if lv < 6:
                    bp_, bk_ = bslot(4 + b)
                    MM(bp_, A_[cur][:], B_[cur][:], [kB[cur], kA[cur]], [bk_])
                yield
                CP("act", A_[nxt][:], ap_, [ak], [kA[nxt]])
                if lv < 6:
                    CP("dve", B_[nxt][:], bp_, [bk_], [kB[nxt]])
                yield
                pp_, pk = bslot(4 + b)
                MM(pp_, A_[nxt][:], Pm[cur][:], [kA[nxt], kP[cur]], [pk])
                yield
                if lv < 6:
                    TTN("dve", Pm[nxt][:], pp_, Pm[cur][:], ALU.add, [pk, kP[cur]], [kP[nxt]])
                else:
                    TTN("dve", TTb[:], pp_, Pm[cur][:], ALU.add, [pk, kP[cur]], [kh_(6)])
                cur = nxt
            yield
            up_, uk = bslot(4 + b)
            MM(up_, TTb[:], vb[:], [kh_(6), kh_(3)], [uk])
            wp_, wk = bslot(4 + b)
            MM(wp_, kbg[:], TTb[:], [kh_(1), kh_(6)], [wk])
            yield
            CP("act", dec[:], up_, [uk], [kf_(7)])
            CP("act", wT[:], wp_, [wk], [kh_(8)])

        rr([nxt_front] + [pre_block(b) for b in range(NB)] + [tail], fast=nxt_front)

        ops_ = {}
        for b in range(NB):
            f, hh, kf_, kh_ = blk_tiles(b)
            u_sb = f(7)
            kdec, attT, wT, qdT = hh(2), hh(7), hh(8), hh(9)
            vnew = hh(10)
            wsp, wsk = bslot(4 + b)
            MM(wsp, wT[:], Sg_b[:, h, :], [kh_(8), ("Sg_b", h)], [wsk])
            TTN("dve", vnew[:], u_sb[:], wsp, ALU.subtract, [kf_(7), wsk], [kh_(10)])
            op_, ok = bslot(4 + b)
            MM(op_, qdT[:], Sg_b[:, h, :], [kh_(9), ("Sg_b", h)], [ok], start=True, stop=False)
            MM(op_, attT[:], vnew[:], [kh_(7), kh_(10)], [ok], start=False, stop=True)
            dsp, dsk = bslot(4 + b)
            MM(dsp, kdec[:], vnew[:], [kh_(2), kh_(10)], [dsk])
            STT(Sg_b[:, h, :], Sg[:, h, :], el[:, b, h:h + 1], dsp, ALU.mult, ALU.add, [("Sg", h), "g_el", dsk], [("Sg_b", h)])
            STT(Sg[:, h, :], Sg[:, h, :], el[:, b, h:h + 1], dsp, ALU.mult, ALU.add, [("Sg", h), "g_el", dsk], [("Sg", h)])
            ops_[b] = (op_, ok)
        for b in range(NB):
            blk = slice(b * 128, (b + 1) * 128)
            f, hh, kf_, kh_ = blk_tiles(b)
            sm = T("blk%d_sm" % b, [128, 16])
            smk = lambda i: ("blk_sm", b, i)
            ysb = hh(11)
            op_, ok = ops_[b]
            ACT(junk[:], op_, AF.Square, [ok], ["junk", smk(5)], accum_out=sm[:, 5:6])
            STT(sm[:, 6:7], sm[:, 5:6], 4.0 / 128.0, sm[:, 4:5], ALU.mult, ALU.add, [smk(5), smk(4)], [smk(6)])
            TTN("pool", sm[:, 6:7], sm[:, 6:7], mhalf[:], ALU.pow, [smk(6), "mhalf"], [smk(6)])
            STT(ysb[:], op_, sm[:, 6:7], zbs[:, b, :], ALU.mult, ALU.mult, [ok, smk(6), zname], [kh_(11)])

        def tail_gen():
            yield
            yield
            yield
            for b in range(NB):
                blk = slice(b * 128, (b + 1) * 128)
                f, hh, kf_, kh_ = blk_tiles(b)
                yp_, yk = bslot(4 + b, bf=True)
                TR(yp_, hh(11)[:], ident_b[:], [kh_(11), "ident_b"], [yk])
                yield
                CP("act", yT[:, 8 + h, blk], yp_, [yk], ["yT"])
                yield
        return tail_gen()

    def unit_H(h, par, tail=None):
        tf = T("w512_2", [128, TT])[:].rearrange("p (b n) -> p b n", n=128)
        kf = T("w512_3", [128, TT])[:].rearrange("p (b n) -> p b n", n=128)
        lf = T("w512_4", [128, TT])[:].rearrange("p (b n) -> p b n", n=128)
        v_b = T("h_v", [128, NB, 128], BF16)
        zs = T("w512_5", [128, TT])[:].rearrange("p (b n) -> p b n", n=128)
        tq = T("w512_0", [128, TT])
        qs = T("w512_1", [128, TT])
        proj_fm(par, 0, 0)
        for b in range(NB):
            for c in range(8):
                MM(banks[1 + b][:, 0:384], xT[:, c, b * 128:(b + 1) * 128], Wh[par][:, c, 128:512], ["xT", ("Wh", par)],
                   [bk(1 + b)], start=(c == 0), stop=(c == 7))
        ACT(tq[:], banks[0][:], AF.Tanh, [bk(0)], ["w512_0"], scale=0.5)
        STT(qs[:], tq[:], 1.0, banks[0][:], ALU.add, ALU.mult, ["w512_0", bk(0)], ["w512_1"])
        for b in range(NB):
            ACT(tf[:, b, :], banks[1 + b][:, 0:128], AF.Tanh, [bk(1 + b)], [("w512_2", b)], scale=0.5)
            CP("act", v_b[:, b, :], banks[1 + b][:, 128:256], [bk(1 + b)], [("h_v", b)])
            ACT(zs[:, b, :], banks[1 + b][:, 256:384], AF.Tanh, [bk(1 + b)], [("w512_5", b)], scale=0.5)
            STT(zs[:, b, :], zs[:, b, :], 1.0, banks[1 + b][:, 256:384], ALU.add, ALU.mult, [("w512_5", b), bk(1 + b)],
                [("w512_5", b)])
            TTN("dve", zs[:, b, :], zs[:, b, :], gb4[1][:, b, :], ALU.mult, [("w512_5", b), "gb4"], [("w512_5", b)])
            STT(kf[:, b, :], tf[:, b, :], -1.0, nhoml[:, h * 128:(h + 1) * 128], ALU.add, ALU.mult,
                [("w512_2", b), "nhoml"], [("w512_3", b)])
        for b in range(NB):
            ACT(lf[:, b, :], kf[:, b, :], AF.Ln, [("w512_3", b)], [("w512_4", b)], scale=-1.0, bias=1.0)
        lnh = T("lnhalf", [128, 1])
        junk = T("junk", [128, 128])

        def htiles(b):
            f = lambda i: T("blk%d_f%d" % (b, i), [128, 128])
            hh = lambda i: T("blk%d_h%d" % (b, i), [128, 128], BF16)
            kf_ = lambda i: "blk%d_f%d" % (b, i)
            kh_ = lambda i: "blk%d_h%d" % (b, i)
            return f, hh, kf_, kh_

        def pre_h(b):
            blk = slice(b * 128, (b + 1) * 128)
            f, hh, kf_, kh_ = htiles(b)
            enb, ebT = f(0), f(1)
            kt, ktT, attT = hh(0), hh(1), hh(2)
            q2 = q2s[b]
            ebl = T("blk%d_ebl" % b, [128, 2])
            bp_, bk_ = bslot(HB[b])
            MM(bp_, m_bd[:], lf[:, b, :], ["m_bd", ("w512_4", b)], [bk_])
            btp, btk = bslot(HB[b])
            MM(btp, lf[:, b, :], m_bd[:], ["m_bd", ("w512_4", b)], [btk])
            yield
            ACT(enb[:], bp_, AF.Exp, [bk_], [kf_(0)], scale=-1.0)
            ACT(ebT[:], btp, AF.Exp, [btk, "lnhalf"], [kf_(1)], bias=lnh[:, 0:1])
            ACT(ebl[:, 0:1], btp[:, 63:64], AF.Exp, [btk], [("ebl", b)])
            ACT(ebl[:, 1:2], btp[:, 127:128], AF.Exp, [btk], [("ebl", b)])
            yield
            TTN("dve", kt[:], kf[:, b, :], enb[:], ALU.mult, [("w512_3", b), kf_(0)], [kh_(0)])
            q2v = q2[:].rearrange("p (c x) -> p c x", x=192)[:, :, 0:64]
            TTN("dve", q2v, qs[:, blk].rearrange("p (c x) -> p c x", x=64), ebT[:].rearrange("p (c x) -> p c x", x=64),
                ALU.mult, ["w512_1", kf_(1)], [("q2", b)])
            yield
            ktp, ktk = bslot(HB[b], bf=True)
            TR(ktp, kt[:], ident_b[:], [kh_(0), "ident_b"], [ktk])
            d0p, d0k = bslot(HB[b])
            MM(d0p, kt[0:64, :], v_b[0:64, b, :], [kh_(0), ("h_v", b)], [d0k])
            d1p, d1k = banks[1 + b][:, 128:256], ("slot", 1 + b, 1)
            MM(d1p, kt[64:128, :], v_b[64:128, b, :], [kh_(0), ("h_v", b)], [d1k])
            yield
            G0, G1 = f(2), f(3)
            TS("dve", G0[:], d0p, ebl[:, 0:1], None, ALU.mult, None, [d0k, ("ebl", b)], [kf_(2)])
            TS("dve", G1[:], d1p, ebl[:, 1:2], None, ALU.mult, None, [d1k, ("ebl", b)], [kf_(3)])
            CP("act", ktT[:], ktp, [ktk], [kh_(1)])
            yield
            atp, atk = bslot(HB[b])
            MM(atp, ktT[:], q2v, [kh_(1), ("q2", b)], [atk])
            yield
            TTN("dve", attT[:], atp, m_bd[:], ALU.mult, [atk, "m_bd"], [kh_(2)])
            yield
            MM(banks[1 + b][:, 0:128], attT[:], v_b[:, b, :], [kh_(2), ("h_v", b)], [bk(1 + b)], start=True, stop=False)

        rr([pre_h(b) for b in range(NB)] + [tail])

        Pp = [T("h_P0", [128, 128]), T("h_P1", [128, 128])]
        cur, curk = Sh[:, h, :], ("Sh", h)
        idx = 0
        for b in range(NB):
            f, hh, kf_, kh_ = htiles(b)
            q2 = q2s[b]
            ebl = T("blk%d_ebl" % b, [128, 2])
            op_, ok = banks[1 + b][:, 0:128], bk(1 + b)
            for cc in range(2):
                if idx == 0:
                    sbf, sbk = Sh_b[:, h, :], ("Sh_b", h)
                else:
                    sbf, sbk = hh(4 + cc)[:], kh_(4 + cc)
                    CP("act", sbf, cur, [curk], [sbk])
                MM(op_, q2[:, cc * 128:(cc + 1) * 128], sbf, [("q2", b), sbk], [ok], start=False, stop=(cc == 1))
                if idx == 2 * NB - 1:
                    nxt, nxtk = Sh[:, h, :], ("Sh", h)
                else:
                    nxt, nxtk = Pp[idx % 2][:], "h_P%d" % (idx % 2)
                STT(nxt, cur, ebl[:, cc:cc + 1], f(2 + cc)[:], ALU.mult, ALU.add, [curk, ("ebl", b), kf_(2 + cc)], [nxtk])
                cur, curk = nxt, nxtk
                idx += 1
        CP("act", Sh_b[:, h, :], Sh[:, h, :], [("Sh", h)], [("Sh_b", h)])
        for b in range(NB):
            blk = slice(b * 128, (b + 1) * 128)
            f, hh, kf_, kh_ = htiles(b)
            ysb = hh(3)
            sm = T("blk%d_sm" % b, [128, 16])
            smk = lambda i: ("blk_sm", b, i)
            op_, ok = banks[1 + b][:, 0:128], bk(1 + b)
            ACT(junk[:], op_, AF.Square, [ok], ["junk", smk(0)], accum_out=sm[:, 0:1])
            TS("dve", sm[:, 1:2], sm[:, 0:1], 4.0 / 128.0, 4 * LN_EPS, ALU.mult, ALU.add, [smk(0)], [smk(1)])
            TTN("pool", sm[:, 1:2], sm[:, 1:2], mhalf[:], ALU.pow, [smk(1), "mhalf"], [smk(1)])
            STT(ysb[:], op_, sm[:, 1:2], zs[:, b, :], ALU.mult, ALU.mult, [ok, smk(1), ("w512_5", b)], [kh_(3)])

        def tail_gen():
            yield
            yield
            for b in range(NB):
                blk = slice(b * 128, (b + 1) * 128)
                f, hh, kf_, kh_ = htiles(b)
                yp_, yk = bslot(HB[b], bf=True)
                TR(yp_, hh(3)[:], ident_b[:], [kh_(3), "ident_b"], [yk])
                yield
                CP("act", yT[:, h, blk], yp_, [yk], ["yT"])
                yield
        return tail_gen()

    lnh = T("lnhalf", [128, 1])
    P.add("pool", lambda e: e.memset(lnh[:], float(np.log(0.5))), (), ["lnhalf"])
    HB = [5, 6, 7, 0]
    q2s = [T("h_q2_%d" % b, [128, 384], BF16) for b in range(NB)]
    for b in range(NB):
        P.add("pool", lambda e, b=b: e.memset(q2s[b][:], 0.0), (), [("q2", b)])

    for t in range(NT):
        DMA("sp", xres[:], x_d[t * TT:(t + 1) * TT, :].rearrange("(b p) f -> p b f", p=128), c_x, [], ["xres"])
        if marks is not None:
            marks.append(('tile_start', len(P.ops)))
        make_xT(0)
        if marks is not None:
            marks.append(('xT_end', len(P.ops)))
        par = load_unit(0, 0)
        gdn_gates()
        if marks is not None:
            marks.append(('gates_end', len(P.ops)))
        for u in range(8):
            npar = load_unit(0, u + 1)
            unit_A(u, par)
            par = npar
        pars = {8: par, 9: load_unit(0, 9)}
        rr([front_B(0, pars[8])])
        tail = None
        for h in range(8):
            if h + 2 < 8:
                pars[8 + h + 2] = load_unit(0, 8 + h + 2)
            nf = front_B(h + 1, pars[8 + h + 1]) if h < 7 else None
            tail = unit_B(h, nf, tail)
        rr([tail])
        epilogue(0, t)
        if n_layers > 1:
            make_xT(2)
            par = load_unit(1, 0)
            tail = None
            for u in range(16):
                npar = load_unit(1, u + 1) if u < 15 else None
                tail = unit_H(u, par, tail)
                par = npar
            rr([tail])
            epilogue(1, t)
        DMA("act", out_d[t * TT:(t + 1) * TT, :].rearrange("(b p) f -> p b f", p=128), xres[:], c_out, ["xres"], ["out"])

    if max_ops is not None:
        P.ops = P.ops[:max_ops]
    P.finish()
    P.emit()
    nc_allow.__exit__(None, None, None)
    es.close()
    return nc, len(P.ops)


_KEYS = ["x", "p", "w_in_even", "conv_a_w", "conv_b_w", "a_log", "dt_bias", "gdn_norm_g", "w_out_even", "w_in_odd",
         "lower_bounds", "hgrn_norm_g", "w_out_odd", "ln_g", "ln_b", "w_pl", "w_pl_gate"]


def make_in_maps(inputs, n_cores, S):
    f = lambda a: np.ascontiguousarray(np.asarray(a, dtype=np.float32))
    shared = {
        "w_in_even": f(inputs["w_in_even"][0]), "conv_a_w": f(inputs["conv_a_w"][0]),
        "conv_b_w": f(inputs["conv_b_w"][0]), "a_log": f(inputs["a_log"]), "dt_bias": f(inputs["dt_bias"]),
        "gdn_norm_g": f(inputs["gdn_norm_g"]), "w_out_even": f(inputs["w_out_even"][0]),
        "w_in_odd": f(inputs["w_in_odd"][0]), "lower_bounds": f(inputs["lower_bounds"]),
        "hgrn_norm_g": f(inputs["hgrn_norm_g"]), "w_out_odd": f(inputs["w_out_odd"][0]),
        "ln_g": f(inputs["ln_g"]), "ln_b": f(inputs["ln_b"]), "w_pl": f(inputs["w_pl"]),
        "w_pl_gate": f(inputs["w_pl_gate"]),
    }
    maps = []
    for b in range(n_cores):
        m = dict(shared)
        m["x"] = f(inputs["x"][b, :S])
        m["p"] = f(inputs["p"][:, b, :S])
        maps.append(m)
    return maps


def kernel(**inputs):
    S = inputs["x"].shape[1]
    nb = inputs["x"].shape[0]
    nc, _ = build(S=S)
    in_maps = make_in_maps(inputs, nb, S)
    res = run_bass_kernel_spmd(nc, in_maps, core_ids=list(range(nb)))
    return np.stack([np.asarray(r["out"], dtype=np.float32) for r in res.results], axis=0)
```

```python
import numpy as np
from contextlib import ExitStack
import concourse.bass as bass
import concourse.mybir as mybir
from concourse.bass_utils import run_bass_kernel_spmd

F32 = mybir.dt.float32
BF16 = mybir.dt.bfloat16
AF = mybir.ActivationFunctionType
ALU = mybir.AluOpType

D = 1024
TT = 512
NB = TT // 128
ALPHA = 4.0 ** 0.25
LN_EPS = 1e-5


class Op:
    __slots__ = ("eng", "fn", "reads", "writes", "chan", "ninc", "deps", "signal", "count", "idx")

    def __init__(self, eng, fn, reads, writes, chan, ninc):
        self.eng, self.fn, self.reads, self.writes, self.chan, self.ninc = eng, fn, reads, writes, chan, ninc
        self.deps = set()
        self.signal = chan is not None
        self.count = 0


class Prog:
    def __init__(self, nc):
        self.nc = nc
        self.ops = []
        self.nchan = 0
        self.lines = None

    def chan(self):
        self.nchan += 1
        return ("chan", self.nchan)

    def add(self, eng, fn, reads=(), writes=(), chan=None, ninc=1):
        op = Op(eng, fn, tuple(reads), tuple(writes), chan, ninc)
        op.idx = len(self.ops)
        if self.lines is not None:
            import sys as _s
            f = _s._getframe(1)
            while f.f_code.co_name in ("add", "MM", "TR", "ACT", "TS", "TTN", "STT", "CP", "DMA"):
                f = f.f_back
            self.lines.append(f.f_lineno)
        self.ops.append(op)
        return op

    def finish(self):
        ops = self.ops
        lastw, lastr = {}, {}

        def ch(o):
            return o.chan if o.chan is not None else o.eng

        def bank_of(k):
            if isinstance(k, tuple) and k[0] == "slot":
                return k[1]
            if isinstance(k, str) and k.startswith("bank"):
                return int(k[4:])
            return None

        bank_rd = {}
        bank_wr = {}
        for o in ops:
            deps = set()
            my = ch(o)
            for k in o.reads:
                deps.update(lastw.get(k, {}).values())
                b = bank_of(k)
                if b is not None:
                    deps.update(j for c, j in bank_rd.get(b, {}).items() if c != my)
                    bank_rd.setdefault(b, {})[my] = o.idx
                    if b in bank_wr:
                        deps.add(bank_wr[b])
            for k in o.writes:
                deps.update(lastw.get(k, {}).values())
                deps.update(lastr.get(k, {}).values())
                b = bank_of(k)
                if b is not None and o.eng == "pe":
                    deps.update(bank_rd.get(b, {}).values())
                    bank_wr[b] = o.idx
            deps.discard(o.idx)
            for j in deps:
                p = ops[j]
                if p.chan is None and o.chan is None and p.eng == o.eng:
                    if o.eng == "pe":
                        continue
                o.deps.add(j)
                p.signal = True
            for k in o.reads:
                lastr.setdefault(k, {})[my] = o.idx
            for k in o.writes:
                lastw[k] = {my: o.idx}
                lastr[k] = {}
        cnt = {}
        for o in ops:
            if o.signal:
                c = ch(o)
                cnt[c] = cnt.get(c, 0) + (16 * o.ninc if o.chan is not None else 1)
                o.count = cnt[c]
        self.cnt = cnt

    def emit(self, tail_eng="sp"):
        nc, ops, cnt = self.nc, self.ops, self.cnt
        chans = sorted(cnt.keys(), key=str)
        with ExitStack() as es:
            sems = {c: es.enter_context(nc.semaphore("s_" + (c if isinstance(c, str) else "c%d" % c[1])))
                    for c in chans}
            block = es.enter_context(nc.Block())

            def ch(o):
                return o.chan if o.chan is not None else o.eng

            def run(engname, e):
                waited = {}
                for o in ops:
                    if o.eng != engname:
                        continue
                    need = {}
                    for j in o.deps:
                        p = ops[j]
                        c = ch(p)
                        if p.count > need.get(c, 0):
                            need[c] = p.count
                    for c, v in need.items():
                        if waited.get(c, 0) >= v:
                            continue
                        e.wait_ge(sems[c], v)
                        waited[c] = v
                    ins = o.fn(e)
                    if self.lines is not None and o.chan is None:
                        ins.annotate("L%d" % self.lines[o.idx])
                    if o.signal:
                        if o.chan is not None:
                            for i_ in ins:
                                i_.then_inc(sems[o.chan], 16)
                        else:
                            ins.then_inc(sems[o.eng], 1)
                if engname == tail_eng:
                    for c, v in cnt.items():
                        if waited.get(c, 0) < v:
                            e.wait_ge(sems[c], v)

            block.sync(lambda e: run("sp", e))
            block.tensor(lambda e: run("pe", e))
            block.scalar(lambda e: run("act", e))
            block.vector(lambda e: run("dve", e))
            block.gpsimd(lambda e: run("pool", e))


def build(S=4096, n_layers=2, max_ops=None, marks=None):
    NT = S // TT
    nc = bass.Bass("TRN2", target_bir_lowering=False)

    def din(name, shape):
        return nc.dram_tensor(name, shape, F32, kind="ExternalInput").ap()

    x_d = din("x", [S, D])
    p_d = din("p", [2, S, 256])
    wie_d = din("w_in_even", [D, 8208])
    cwa_d = din("conv_a_w", [3, 1024])
    cwb_d = din("conv_b_w", [4, 3072])
    alog_d = din("a_log", [1, 8])
    dtb_d = din("dt_bias", [1, 8])
    gng_d = din("gdn_norm_g", [1, 128])
    woe_d = din("w_out_even", [2048, D])
    wio_d = din("w_in_odd", [D, 8192])
    lb_d = din("lower_bounds", [2, 2048])
    hng_d = din("hgrn_norm_g", [1, 128])
    woo_d = din("w_out_odd", [2048, D])
    lng_d = din("ln_g", [2, D])
    lnb_d = din("ln_b", [2, D])
    wpl_d = din("w_pl", [2, 256, D])
    wg_d = din("w_pl_gate", [2, D, D])
    out_d = nc.dram_tensor("out", [S, D], F32, kind="ExternalOutput").ap()

    def dscr(name, shape):
        return nc.dram_tensor(name, shape, BF16, kind="Internal").ap()

    wrm = [dscr("wrm_even", [D, 8192]), dscr("wrm_odd", [D, 8192])]
    wba_bf = dscr("wba_bf", [D, 16])
    wo_bf = [dscr("wo_even", [2048, D]), dscr("wo_odd", [2048, D])]
    wg_bf = dscr("wg_bf", [2, D, D])
    wpl_bf = dscr("wpl_bf", [2, 256, D])

    P = Prog(nc)
    if marks is not None:
        P.lines = []
        marks.append(P.lines)
    es = ExitStack()
    tiles = {}

    def T(name, shape, dt=F32):
        if name not in tiles:
            tiles[name] = es.enter_context(nc.sbuf_tensor(name, shape, dt))
        return tiles[name]

    banks = [es.enter_context(nc.psum_tensor("bank%d" % i, [128, 512], F32)) for i in range(8)]

    def bk(i):
        return "bank%d" % i

    def MM(out, lhsT, rhs, r, w, start=True, stop=True):
        P.add("pe", lambda e: e.matmul(out, lhsT=lhsT, rhs=rhs, start=start, stop=stop), r, w)

    def TR(out, in_, ident, r, w):
        P.add("pe", lambda e: e.transpose(out, in_, ident), r, w)

    def ACT(out, in_, func, r, w, **kw):
        P.add("act", lambda e: e.activation(out=out, in_=in_, func=func, **kw), r, w)

    def TS(eng, out, in0, s1, s2, op0, op1, r, w):
        if op1 is None:
            P.add(eng, lambda e: e.tensor_scalar(out, in0, s1, None, op0), r, w)
        else:
            P.add(eng, lambda e: e.tensor_scalar(out, in0, s1, s2, op0, op1), r, w)

    def TTN(eng, out, in0, in1, op, r, w):
        P.add(eng, lambda e: e.tensor_tensor(out, in0, in1, op), r, w)

    def STT(out, in0, scalar, in1, op0, op1, r, w):
        P.add("dve", lambda e: e.scalar_tensor_tensor(out, in0, scalar, in1, op0, op1), r, w)

    def CP(eng, out, in_, r, w):
        if eng == "act":
            ACT(out, in_, AF.Copy, r, w)
        else:
            P.add(eng, lambda e: e.tensor_copy(out, in_), r, w)

    def DMA(eng, out, in_, chan, r, w, **kw):
        P.add(eng, lambda e: [e.dma_start(out=out, in_=in_, **kw)], r, w, chan=chan)

    ones = T("ones", [128, 128])
    ident = T("ident", [128, 128])
    ident_b = T("ident_b", [128, 128], BF16)
    m_up = T("m_up", [128, 128])
    m_bd = T("m_bd", [128, 128])
    m_bd_b = T("m_bd_b", [128, 128], BF16)
    onec_b = T("onec_b", [128, 1], BF16)
    mhalf = T("mhalf", [128, 1])
    P.add("pool", lambda e: e.memset(ones[:], 1.0), (), ["ones"])
    P.add("pool", lambda e: e.memset(onec_b[:], 1.0), (), ["onec_b"])
    P.add("pool", lambda e: e.memset(mhalf[:], -0.5), (), ["mhalf"])
    P.add("pool", lambda e: e.affine_select(out=ident[:], in_=ones[:], pattern=[[-1, 128]], compare_op=ALU.is_equal,
                                            fill=0.0, base=0, channel_multiplier=1), ["ones"], ["ident"])
    P.add("pool", lambda e: e.affine_select(out=m_up[:], in_=ones[:], pattern=[[1, 128]], compare_op=ALU.is_ge,
                                            fill=0.0, base=0, channel_multiplier=-1), ["ones"], ["m_up"])
    P.add("pool", lambda e: e.tensor_copy(ident_b[:], ident[:]), ["ident"], ["ident_b"])
    P.add("pool", lambda e: e.tensor_copy(m_bd[:], m_up[:]), ["m_up"], ["m_bd"])
    P.add("pool", lambda e: e.memset(m_bd[0:64, 64:128], 0.0), ["m_bd"], ["m_bd"])
    P.add("pool", lambda e: e.tensor_copy(m_bd_b[:], m_bd[:]), ["m_bd"], ["m_bd_b"])

    c_const = P.chan()
    lnp = T("lnp", [128, 2, D])
    c_lnp = P.chan()
    xres = T("xres", [128, NB, D])
    yT = T("yT", [128, 16, TT], BF16)
    xres_f = xres[:].rearrange("p b d -> p (b d)")
    yT_f = yT[:].rearrange("p c n -> p (c n)").bitcast(F32)
    lbb = xres_f[:, 0:4096].rearrange("p (a d) -> p a d", a=2)
    nhoml = T("nhoml", [128, 2048])
    aab = T("aab", [128, 16])
    cwr = yT_f[0:4, 0:4096]
    cw = T("cw", [128, 32, 4])
    P.add("sp", lambda e: [e.dma_start(out=xres_f[:, 0:4096],
                                       in_=lb_d.rearrange("a d -> (a d)").partition_broadcast(128)),
                           e.dma_start(out=aab[:, 0:8], in_=alog_d.rearrange("a d -> (a d)").partition_broadcast(128)),
                           e.dma_start(out=aab[:, 8:16], in_=dtb_d.rearrange("a d -> (a d)").partition_broadcast(128)),
                           e.dma_start(out=cwr[0:3, 0:1024], in_=cwa_d),
                           e.dma_start(out=cwr[0:4, 1024:4096], in_=cwb_d),
                           ], (), ["xres", "yT", "aab"], chan=c_const, ninc=5)
    nc_allow = nc.allow_non_contiguous_dma(reason="tiny per-partition vectors")
    nc_allow.__enter__()
    TTN("dve", lbb[:, 0, :], lbb[:, 1, :], lbb[:, 0, :], ALU.subtract, ["xres"], ["xres"])
    ACT(lbb[:, 1, :], lbb[:, 0, :], AF.Tanh, ["xres"], ["xres"], scale=0.5)
    TS("dve", nhoml[:], lbb[:, 1, :], 0.25, -0.25, ALU.mult, ALU.add, ["xres"], ["nhoml"])
    ACT(aab[:, 0:8], aab[:, 0:8], AF.Exp, ["aab"], ["aab"])
    TS("dve", aab[:, 0:8], aab[:, 0:8], -1.0, None, ALU.mult, None, ["aab"], ["aab"])
    cwv = banks[0][:, 0:128].rearrange("p (g k) -> p g k", k=4)
    for g in range(32):
        ntap = 3 if g < 8 else 4
        TR(cwv[:, g, 0:ntap], cwr[0:ntap, g * 128:(g + 1) * 128], ident[0:ntap, 0:ntap], ["yT", "ident"], [bk(0)])
    P.add("dve", lambda e: e.memset(cw[:], 0.0), (), ["cw"])
    CP("dve", cw[:, 0:8, 0:3], cwv[:, 0:8, 0:3], [bk(0)], ["cw"])
    CP("dve", cw[:, 8:32, :], cwv[:, 8:32, :], [bk(0)], ["cw"])

    if marks is not None:
        marks.append(('consts_end', len(P.ops)))
    c_cast = {}

    def cast_piece(layer, region, pc):
        ch = P.chan()
        src_d = wie_d if layer == 0 else wio_d
        base = region * 4096
        W = 1024 if layer == 0 else 2048
        cols = [base + m * W + pc * 512 for m in range(4)]
        P.add("pool", lambda e: [e.dma_start(out=wrm[layer][:, c0:c0 + 512], in_=src_d[:, c0:c0 + 512]) for c0 in cols],
              (), [("wp", layer, region, pc)], chan=ch, ninc=4)

    def cast_plain(dst, src, key, nsplit):
        ch = P.chan()
        rows = src.shape[0]
        step = rows // nsplit
        P.add("pool", lambda e: [e.dma_start(out=dst[i * step:(i + 1) * step], in_=src[i * step:(i + 1) * step])
                                 for i in range(nsplit)], (), [key], chan=ch, ninc=nsplit)

    ep_big = T("ep_big", [128, 2, D])
    pst = T("pst", [128, NB, 256])
    gb4 = [T("gb4_0", [128, NB, 128]), T("gb4_1", [128, NB, 128])]
    c_gb = P.chan()
    P.add("sp", lambda e: [e.dma_start(out=gb4[l][:, b, :], in_=(gng_d, hng_d)[l][0].partition_broadcast(128))
                           for l in range(2) for b in range(NB)], (), ["gb4"], chan=c_gb, ninc=2 * NB)

    cast_piece(0, 0, 0)
    cast_plain(wba_bf, wie_d[:, 8192:8208], "wba", 1)
    cast_piece(0, 0, 1)
    cast_piece(0, 1, 0)
    cast_piece(0, 1, 1)
    cast_plain(wo_bf[0], woe_d, ("wo", 0), 4)
    cast_plain(wg_bf[0], wg_d[0], ("wg", 0), 2)
    cast_plain(wpl_bf[0], wpl_d[0], ("wpl", 0), 1)
    if n_layers > 1:
        for pc in range(4):
            cast_piece(1, 0, pc)
        cast_plain(wo_bf[1], woo_d, ("wo", 1), 4)
        cast_plain(wg_bf[1], wg_d[1], ("wg", 1), 2)
        cast_plain(wpl_bf[1], wpl_d[1], ("wpl", 1), 1)

    if marks is not None:
        marks.append(('casts_end', len(P.ops)))
    xT = T("xT", [128, 8, TT], BF16)
    Wh = [T("Wh0", [128, 8, 512], BF16), T("Wh1", [128, 8, 512], BF16)]
    wba = T("wba", [128, 8, 16], BF16)
    wpl = T("wpl", [128, 2, D], BF16)
    woutq = [T("woutq0", [128, 16, 256], BF16), T("woutq1", [128, 16, 256], BF16)]
    c_woutq = [P.chan(), P.chan()]
    pT = T("pT", [128, 2, TT], BF16)
    Sg = T("Sg", [128, 8, 128])
    Sg_b = T("Sg_b", [128, 8, 128], BF16)
    Sh = T("Sh", [128, 16, 128])
    Sh_b = T("Sh_b", [128, 16, 128], BF16)
    hist_a = T("hist_a", [128, 8, 2], BF16)
    hist_b = T("hist_b", [128, 24, 3], BF16)
    for t_, k_ in ((Sg, "Sg"), (Sg_b, "Sg_b"), (Sh, "Sh"), (Sh_b, "Sh_b"), (hist_a, "hist_a"), (hist_b, "hist_b")):
        P.add("pool", lambda e, t_=t_: e.memset(t_[:], 0.0), (), [k_])
    for h in range(8):
        pass
    c_x, c_p, c_wh, c_wout, c_wg, c_wplc, c_wbac, c_out = (P.chan(), P.chan(), [P.chan(), P.chan()], P.chan(),
                                                         P.chan(), P.chan(), P.chan(), P.chan())
    DMA("sp", wba[:], wba_bf.rearrange("(c p) n -> p c n", p=128), c_wbac, ["wba"], ["wba_sb"])

    unit_ctr = [0]

    def load_unit(layer, u):
        par = unit_ctr[0] % 2
        unit_ctr[0] += 1
        if layer == 0:
            region, g, ng = u // 8, u % 8, 8
        else:
            region, g, ng = 0, u, 16
        cols = [region * 4096 + m * 128 * ng + g * 128 for m in range(4)]
        P.add("sp", lambda e: [e.dma_start(out=Wh[par][:, :, m * 128:(m + 1) * 128],
                                           in_=wrm[layer][:, cols[m]:cols[m] + 128].rearrange("(c p) n -> p c n", p=128))
                               for m in range(4)],
              [("wp", layer, region, g // 4)], [("Wh", par)], chan=c_wh[par], ninc=4)
        return par

    bslot_ctr = {}

    def bslot(bank, bf=False):
        q = bslot_ctr.get(bank, 0) % 4
        bslot_ctr[bank] = bslot_ctr.get(bank, 0) + 1
        key = ("slot", bank, q)
        if bf:
            return banks[bank][:].bitcast(BF16)[:, q * 256:q * 256 + 128], key
        return banks[bank][:, q * 128:(q + 1) * 128], key

    def make_xT(rot):
        for c in range(8):
            b_ = 4 + (c + rot) % 4
            for b in range(NB):
                TR(banks[b_][:, b * 128:(b + 1) * 128], xres[:, b, c * 128:(c + 1) * 128], ident[:],
                   ["xres", "ident"], [bk(b_)])
            CP("act" if c % 2 == 0 else "dve", xT[:, c, :], banks[b_][:], [bk(b_)], ["xT"])

    def epilogue(layer, t):
        st6s = [T("ep_st6_%d" % b, [128, 2, 6]) for b in range(NB)]
        mvs = [T("ep_mv_%d" % b, [128, 2]) for b in range(NB)]
        rss = [T("ep_rs_%d" % b, [128, 2]) for b in range(NB)]
        DMA("sp", wpl[:], wpl_bf[layer].rearrange("(c p) n -> p c n", p=128), c_wplc, [("wpl", layer)], ["wpl"])
        DMA("sp", pst[:], p_d[layer, t * TT:(t + 1) * TT, :].rearrange("(b p) f -> p b f", p=128), c_p, [], ["pst"])
        wov = wo_bf[layer].rearrange("(c p) n -> p c n", p=128)
        P.add("sp", lambda e: [e.dma_start(out=lnp[:, 0, :], in_=lng_d[layer].partition_broadcast(128)),
                               e.dma_start(out=lnp[:, 1, :], in_=lnb_d[layer].partition_broadcast(128))],
              (), ["lnp"], chan=c_lnp, ninc=2)
        for q in range(4):
            wq = woutq[q % 2]
            DMA("sp", wq[:], wov[:, :, q * 256:(q + 1) * 256], c_woutq[q % 2], [("wo", layer)], [("woutq", q % 2)])
            for b in range(NB):
                bn = (q * NB + b) % 4
                for kc in range(16):
                    MM(banks[bn][:, 0:256], yT[:, kc, b * 128:(b + 1) * 128], wq[:, kc, :],
                       ["yT", ("woutq", q % 2)], [bk(bn)], start=(kc == 0), stop=(kc == 15))
                STT(xres[:, b, q * 256:(q + 1) * 256], xres[:, b, q * 256:(q + 1) * 256], ALPHA, banks[bn][:, 0:256],
                    ALU.mult, ALU.add, ["xres", ("xres_b", b), bk(bn)], [("xres_b", b)])
        for c in range(2):
            b_ = 4 + c
            for b in range(NB):
                TR(banks[b_][:, b * 128:(b + 1) * 128], pst[:, b, c * 128:(c + 1) * 128], ident[:],
                   ["pst", "ident"], [bk(b_)])
            CP("act", pT[:, c, :], banks[b_][:], [bk(b_)], ["pT"])
        def ln_gen(b):
            st6, mv, rs = st6s[b], mvs[b], rss[b]
            tmp, tk = ep_big[:, b % 2, :], ("epb", b % 2)
            for n in range(2):
                P.add("dve", lambda e, n=n, b=b, st6=st6: e.bn_stats(out=st6[:, n, :], in_=xres[:, b, n * 512:(n + 1) * 512]),
                      [("xres_b", b)], [("ep_st6", b)])
            yield
            P.add("dve", lambda e, st6=st6, mv=mv: e.bn_aggr(out=mv[:], in_=st6[:].rearrange("p a s -> p (a s)")),
                  [("ep_st6", b)], [("ep_mv", b)])
            TS("dve", rs[:, 0:1], mv[:, 1:2], LN_EPS, None, ALU.add, None, [("ep_mv", b)], [("ep_rs", b)])
            yield
            TTN("pool", rs[:, 0:1], rs[:, 0:1], mhalf[:], ALU.pow, [("ep_rs", b), "mhalf"], [("ep_rs", b)])
            yield
            STT(rs[:, 1:2], mv[:, 0:1], -1.0, rs[:, 0:1], ALU.mult, ALU.mult, [("ep_mv", b), ("ep_rs", b)], [("ep_rs2", b)])
            yield
            ACT(tmp, xres[:, b, :], AF.Identity, [("xres_b", b), ("ep_rs", b), ("ep_rs2", b)], [tk], scale=rs[:, 0:1], bias=rs[:, 1:2])
            yield
            TTN("pool", tmp, tmp, lnp[:, 0, :], ALU.mult, [tk, "lnp"], [tk])
            yield
            TTN("dve", xres[:, b, :], tmp, lnp[:, 1, :], ALU.add, [tk, "lnp"], [("xres_b", b)])
            yield
            for half in range(2):
                b_ = 4 + 2 * (b % 2) + half
                for cc in range(4):
                    c = half * 4 + cc
                    TR(banks[b_][:, cc * 128:(cc + 1) * 128], xres[:, b, c * 128:(c + 1) * 128], ident[:],
                       [("xres_b", b), "ident"], [bk(b_)])
                CP("act" if half == 0 else "dve", xT[:, half * 4:half * 4 + 4, b * 128:(b + 1) * 128],
                   banks[b_][:].rearrange("p (c n) -> p c n", n=128), [bk(b_)], [("xT_b", b)])
                yield

        for b0 in range(0, NB, 2):
            rr([ln_gen(b0), ln_gen(b0 + 1)])
        wgv = [Wh[0][:].rearrange("p c n -> p (c n)"), Wh[1][:].rearrange("p c n -> p (c n)")]
        P.add("sp", lambda e: [e.dma_start(out=wgv[hh].rearrange("p (c n) -> p c n", c=4),
                                           in_=wg_bf[layer][hh * 512:(hh + 1) * 512].rearrange("(c p) n -> p c n", p=128))
                               for hh in range(2)], [("wg", layer)], [("Wh", 0), ("Wh", 1)], chan=c_wg, ninc=2)

        def wgs(c, n):
            return wgv[c // 4][:, (c % 4) * 1024 + n * 512:(c % 4) * 1024 + (n + 1) * 512]

        for b in range(NB):
            tg, tgk = ep_big[:, b % 2, :], ("epb", b % 2)
            for n in range(2):
                for c in range(8):
                    MM(banks[n][:], xT[:, c, b * 128:(b + 1) * 128], wgs(c, n), [("xT_b", b), ("Wh", 0), ("Wh", 1)], [bk(n)],
                       start=(c == 0), stop=(c == 7))
                for c in range(2):
                    MM(banks[2 + n][:], pT[:, c, b * 128:(b + 1) * 128], wpl[:, c, n * 512:(n + 1) * 512],
                       ["pT", "wpl"], [bk(2 + n)], start=(c == 0), stop=(c == 1))
                ACT(tg[:, n * 512:(n + 1) * 512], banks[n][:], AF.Tanh, [bk(n)], [tgk], scale=0.5)
                STT(tg[:, n * 512:(n + 1) * 512], tg[:, n * 512:(n + 1) * 512], 1.0, banks[2 + n][:],
                    ALU.add, ALU.mult, [tgk, bk(2 + n)], [tgk])
            STT(xres[:, b, :], tg, 0.5, xres[:, b, :], ALU.mult, ALU.add, [tgk, ("xres_b", b)], [("xres_b", b)])
        P.add("dve", lambda e: e.tensor_copy(mvs[0][:, 0:1], mvs[0][:, 0:1]),
              [("xres_b", b) for b in range(NB)] + [("ep_mv", 0)], ["xres", ("ep_mv", 0)])

    def conv_diag(g, ntap, scale):
        dg = T("diag", [128, 4, 128], BF16)
        for tp in range(ntap):
            TS("pool", dg[:, tp, :], ident_b[:], cw[:, g, tp:tp + 1], scale, ALU.mult, ALU.mult,
               ["ident_b", "cw"], [("diag", tp)])
        return dg

    def proj_fm(par, col, bank):
        for c in range(8):
            MM(banks[bank][:], Wh[par][:, c, col * 128:(col + 1) * 128], xT[:, c, :], [("Wh", par), "xT"], [bk(bank)],
               start=(c == 0), stop=(c == 7))

    def unit_A(j, par):
        h_sb = T("w512_0", [128, TT])
        u_bf = T("a_u", [128, TT + 2], BF16)
        tz = T("w512_1", [128, TT])
        bz = T("w512_2", [128, TT])
        o_ = 4 * (j % 2)
        for col in range(4):
            proj_fm(par, col, o_ + col)
        CP("act", h_sb[:], banks[o_][:], [bk(o_)], ["w512_0"])
        CP("pool", u_bf[:, 0:2], hist_a[:, j, :], ["hist_a"], ["a_u"])
        TTN("dve", u_bf[:, 2:TT + 2], banks[o_ + 1][:], h_sb[:], ALU.mult, [bk(o_ + 1), "w512_0"], ["a_u"])
        CP("pool", hist_a[:, j, :], u_bf[:, TT:TT + 2], ["a_u"], ["hist_a"])
        ACT(tz[:], banks[o_ + 3][:], AF.Tanh, [bk(o_ + 3)], ["w512_1"], scale=0.5)
        STT(tz[:], tz[:], 1.0, banks[o_ + 3][:], ALU.add, ALU.mult, ["w512_1", bk(o_ + 3)], ["w512_1"])
        TTN("dve", bz[:], banks[o_ + 2][:], tz[:], ALU.mult, [bk(o_ + 2), "w512_1"], ["w512_2"])
        dg = conv_diag(j, 3, 0.5)
        for tp in range(3):
            MM(banks[o_][:], dg[:, tp, :], u_bf[:, tp:tp + TT], [("diag", tp), "a_u"], [bk(o_)], start=(tp == 0), stop=(tp == 2))
        TTN("dve", yT[:, j, :], banks[o_][:], bz[:], ALU.mult, [bk(o_), "w512_2"], ["yT"])

    def gdn_gates():
        bav = banks[3][:, 0:NB * 16].rearrange("p (b n) -> p b n", n=16)
        for b in range(NB):
            for c in range(8):
                MM(bav[:, b, :], xT[:, c, b * 128:(b + 1) * 128], wba[:, c, :], ["xT", "wba_sb"], [bk(3)],
                   start=(c == 0), stop=(c == 7))
        gt = T("g_t", [128, NB, 8])
        beta = T("g_beta", [128, NB, 8])
        nbeta = T("g_nbeta", [128, NB, 8])
        hbeta = T("g_hbeta", [128, NB, 8])
        g = T("g_g", [128, NB, 8])
        gc = T("g_gc", [128, NB, 8])
        ngc = T("g_ngc", [128, NB, 8])
        egc = T("g_egc", [128, NB, 8])
        bege = T("g_bege", [128, NB, 8])
        egl = T("g_egl", [128, NB, 8])
        el = T("g_el", [128, NB, 8])
        ACT(gt[:], bav[:, :, 0:8], AF.Tanh, [bk(3)], ["g_t"], scale=0.5)
        TS("dve", beta[:], gt[:], 0.5, 0.5, ALU.mult, ALU.add, ["g_t"], ["g_beta"])
        TS("dve", nbeta[:], gt[:], -0.5, -0.5, ALU.mult, ALU.add, ["g_t"], ["g_nbeta"])
        TS("dve", hbeta[:], gt[:], 0.25, 0.25, ALU.mult, ALU.add, ["g_t"], ["g_hbeta"])
        for b in range(NB):
            TTN("dve", g[:, b, :], bav[:, b, 8:16], aab[:, 8:16], ALU.add, [bk(3), "aab"], ["g_g"])
        ACT(g[:], g[:], AF.Exp, ["g_g"], ["g_g"])
        ACT(g[:], g[:], AF.Ln, ["g_g"], ["g_g"], bias=1.0)
        for b in range(NB):
            TTN("dve", g[:, b, :], g[:, b, :], aab[:, 0:8], ALU.mult, ["g_g", "aab"], ["g_g"])
        gcp = banks[3][:, 64:64 + NB * 8].rearrange("p (b n) -> p b n", n=8)
        glp = banks[3][:, 128:128 + NB * 8].rearrange("p (b n) -> p b n", n=8)
        for b in range(NB):
            MM(gcp[:, b, :], m_up[:], g[:, b, :], ["m_up", "g_g"], [bk(3)])
        MM(glp, ones[:], g[:], ["ones", "g_g"], [bk(3)])
        CP("dve", gc[:], gcp, [bk(3)], ["g_gc"])
        TS("dve", ngc[:], gcp, -1.0, None, ALU.mult, None, [bk(3)], ["g_ngc"])
        ACT(egc[:], gcp, AF.Exp, [bk(3)], ["g_egc"])
        TTN("dve", bege[:], egc[:], beta[:], ALU.mult, ["g_egc", "g_beta"], ["g_bege"])
        TTN("dve", egl[:], glp, gc[:], ALU.subtract, [bk(3), "g_gc"], ["g_egl"])
        ACT(egl[:], egl[:], AF.Exp, ["g_egl"], ["g_egl"])
        ACT(el[:], glp, AF.Exp, [bk(3)], ["g_el"])

    def rr(gens, fast=None):
        gens = [g_ for g_ in gens if g_ is not None]
        while gens:
            for g_ in list(gens):
                try:
                    next(g_)
                    if g_ is fast:
                        next(g_)
                except StopIteration:
                    gens.remove(g_)

    def b_bufs(h):
        pb = h % 2
        pre = T("b_pre%d" % pb, [128, 3, TT + 3], BF16)
        cT = T("b_cT%d" % pb, [128, 3, TT], BF16)
        sq = T("b_sq%d" % pb, [128, TT], BF16)
        zname = "w512_1" if pb == 0 else "zbs1"
        zfl = T(zname, [128, TT])
        return pb, pre, cT, sq, zname, zfl

    def front_B(h, par):
        pb, pre, cT, sq, zname, zfl = b_bufs(h)
        zbs = zfl[:].rearrange("p (b n) -> p b n", n=128)
        tt_ = T("w512_0", [128, TT])
        dg3 = T("diag3", [128, 3, 4, 128], BF16)
        for qi in range(3):
            gidx = qi * 8 + h
            for tp in range(4):
                TS("pool", dg3[:, qi, tp, :], ident_b[:], cw[:, 8 + gidx, tp:tp + 1], 1.0, ALU.mult, ALU.mult,
                   ["ident_b", "cw"], [("diag3", qi, tp)])
        yield
        for qi in range(3):
            proj_fm(par, qi, qi)
            yield
        zbv = banks[3][:].rearrange("p (b n) -> p b n", n=128)
        for b in range(NB):
            for c in range(8):
                MM(zbv[:, b, :], xT[:, c, b * 128:(b + 1) * 128], Wh[par][:, c, 384:512], ["xT", ("Wh", par)], [bk(3)],
                   start=(c == 0), stop=(c == 7))
            yield
        for qi in range(3):
            gidx = qi * 8 + h
            CP("pool", pre[:, qi, 0:3], hist_b[:, gidx, :], ["hist_b"], [("b_pre", pb, qi)])
            CP("act", pre[:, qi, 3:TT + 3], banks[qi][:], [bk(qi)], [("b_pre", pb, qi)])
            CP("pool", hist_b[:, gidx, :], pre[:, qi, TT:TT + 3], [("b_pre", pb, qi)], ["hist_b"])
            yield
        ACT(zbs, zbv, AF.Tanh, [bk(3)], [zname], scale=0.5)
        STT(zfl[:], zfl[:], 1.0, banks[3][:], ALU.add, ALU.mult, [zname, bk(3)], [zname])
        TTN("dve", zfl[:], zfl[:], gb4[0][:].rearrange("p b n -> p (b n)"), ALU.mult, [zname, "gb4"], [zname])
        yield
        for qi in range(3):
            for tp in range(4):
                MM(banks[qi][:], dg3[:, qi, tp, :], pre[:, qi, tp:tp + TT], [("diag3", qi, tp), ("b_pre", pb, qi)], [bk(qi)],
                   start=(tp == 0), stop=(tp == 3))
            yield
        for qi in range(3):
            ACT(tt_[:], banks[qi][:], AF.Tanh, [bk(qi)], ["w512_0"], scale=0.5)
            yield
            STT(cT[:, qi, :], tt_[:], 1.0, banks[qi][:], ALU.add, ALU.mult, ["w512_0", bk(qi)], [("b_cT", pb, qi)])
            yield
        TTN("pool", sq[:], cT[:, 0, :], cT[:, 0, :], ALU.mult, [("b_cT", pb, 0)], [("b_sq", pb)])

    def unit_B(h, nxt_front=None, tail=None):
        gt = tiles
        beta, nbeta, hbeta, g, gc, ngc, bege, egl, el = (gt["g_beta"], gt["g_nbeta"], gt["g_hbeta"], gt["g_g"], gt["g_gc"],
                                                         gt["g_ngc"], gt["g_bege"], gt["g_egl"], gt["g_el"])
        pb, pre, cT, sq, zname, zfl = b_bufs(h)
        zbs = zfl[:].rearrange("p (b n) -> p b n", n=128)
        junk = T("junk", [128, 128])

        def blk_tiles(b):
            f = lambda i: T("blk%d_f%d" % (b, i), [128, 128])
            hh = lambda i: T("blk%d_h%d" % (b, i), [128, 128], BF16)
            kf_ = lambda i: "blk%d_f%d" % (b, i)
            kh_ = lambda i: "blk%d_h%d" % (b, i)
            return f, hh, kf_, kh_

        def pre_block(b):
            blk = slice(b * 128, (b + 1) * 128)
            f, hh, kf_, kh_ = blk_tiles(b)
            sm = T("blk%d_sm" % b, [128, 16])
            smk = lambda i: ("blk_sm", b, i)
            A_, B_, Pm = [f(0), f(1)], [f(2), f(3)], [f(4), f(5)]
            kA, kB, kP = [kf_(0), kf_(1)], [kf_(2), kf_(3)], [kf_(4), kf_(5)]
            decT, dec, egr = f(6), f(7), f(8)
            ktm, kbg, kdec, vb, dgr, knT, TTb, attT, wT, qdT = (hh(i) for i in range(10))
            kps, kk = bslot(4 + b, bf=True)
            TR(kps, cT[:, 1, blk], ident_b[:], [("b_cT", pb, 1), "ident_b"], [kk])
            vps, vk = bslot(4 + b, bf=True)
            TR(vps, cT[:, 2, blk], ident_b[:], [("b_cT", pb, 2), "ident_b"], [vk])
            sqp, sqk = bslot(4 + b)
            MM(sqp[:, 0:1], sq[:, blk], onec_b[:], [("b_sq", pb), "onec_b"], [sqk])
            TS("dve", egr[:], ones[:], g[:, b, h:h + 1], None, ALU.mult, None, ["ones", "g_g"], [kf_(8)])
            grp, grk = bslot(4 + b)
            MM(grp, egr[:], m_up[:], [kf_(8), "m_up"], [grk])
            yield
            ACT(junk[:], kps, AF.Square, [kk], ["junk", smk(0)], accum_out=sm[:, 0:1])
            CP("dve", ktm[:], kps, [kk], [kh_(0)])
            TS("dve", vb[:], vps, hbeta[:, b, h:h + 1], None, ALU.mult, None, [vk, "g_hbeta"], [kh_(3)])
            TS("dve", decT[:], grp, gc[:, b, h:h + 1], 0.0, ALU.subtract, ALU.min, [grk, "g_gc"], [kf_(6)])
            TS("dve", dec[:], grp, gc[:, b, h:h + 1], 0.0, ALU.subtract, ALU.max, [grk, "g_gc"], [kf_(7)])
            ACT(egr[:], grp, AF.Exp, [grk, kf_(8)], [kf_(8)])
            ACT(decT[:], decT[:], AF.Exp, [kf_(6)], [kf_(6)])
            ACT(dec[:], dec[:], AF.Exp, [kf_(7)], [kf_(7)], scale=-1.0)
            TS("dve", sm[:, 1:2], sm[:, 0:1], 4e-6, None, ALU.add, None, [smk(0)], [smk(1)])
            TTN("pool", sm[:, 1:2], sm[:, 1:2], mhalf[:], ALU.pow, [smk(1), "mhalf"], [smk(1)])
            TTN("dve", sm[:, 2:3], sm[:, 1:2], bege[:, b, h:h + 1], ALU.mult, [smk(1), "g_bege"], [smk(2)])
            TTN("dve", sm[:, 3:4], sm[:, 1:2], egl[:, b, h:h + 1], ALU.mult, [smk(1), "g_egl"], [smk(3)])
            ACT(dgr[:], ident_b[:], AF.Copy, ["ident_b", smk(1)], [kh_(4)], scale=sm[:, 1:2])
            P.add("pool", lambda e, decT=decT: e.affine_select(out=decT[:], in_=decT[:], pattern=[[1, 128]],
                                                               compare_op=ALU.is_ge, fill=0.0, base=0, channel_multiplier=-1),
                  [kf_(6)], [kf_(6)])
            P.add("pool", lambda e, dec=dec: e.affine_select(out=dec[:], in_=dec[:], pattern=[[-1, 128]],
                                                             compare_op=ALU.is_gt, fill=0.0, base=0, channel_multiplier=1),
                  [kf_(7)], [kf_(7)])
            TS("dve", sm[:, 4:5], sqp[:, 0:1], 4 * 128e-5, 4 * 128e-5 * 4e-6, ALU.mult, ALU.add, [sqk], [smk(4)])
            yield
            TS("dve", kbg[:], kps, sm[:, 2:3], None, ALU.mult, None, [kk, smk(2)], [kh_(1)])
            ACT(kdec[:], kps, AF.Copy, [kk, smk(3)], [kh_(2)], scale=sm[:, 3:4])
            knp, knk = bslot(4 + b)
            MM(knp, ktm[:], dgr[:], [kh_(0), kh_(4)], [knk])
            TTN("dve", qdT[:], cT[:, 0, blk], egr[:], ALU.mult, [("b_cT", pb, 0), kf_(8)], [kh_(9)])
            yield
            CP("act", knT[:], knp, [knk], [kh_(5)])
            yield
            kkp, kkk = bslot(4 + b)
            MM(kkp, knT[:], knT[:], [kh_(5)], [kkk])
            qkp, qkk = bslot(4 + b)
            MM(qkp, knT[:], cT[:, 0, blk], [kh_(5), ("b_cT", pb, 0)], [qkk])
            yield
            STT(A_[0][:], kkp, nbeta[:, b, h:h + 1], dec[:], ALU.mult, ALU.mult, [kkk, "g_nbeta", kf_(7)], [kA[0]])
            TTN("dve", attT[:], qkp, decT[:], ALU.mult, [qkk, kf_(6)], [kh_(7)])
            yield
            atp, atk = bslot(4 + b)
            TR(atp, A_[0][:], ident[:], [kA[0], "ident"], [atk])
            yield
            CP("act", B_[0][:], atp, [atk], [kB[0]])
            TTN("dve", Pm[0][:], atp, ident[:], ALU.add, [atk, "ident"], [kP[0]])
            yield
            cur = 0
            for lv in range(1, 7):
                nxt = 1 - cur
                ap_, ak = bslot(4 + b)
                MM(ap_, B_[cur][:], A_[cur][:], [kB[cur], kA[cur]], [ak])
                if lv < 6:
                    bp_, bk_ = bslot(4 + b)
                    MM(bp_, A_[cur][:], B_[cur][:], [kB[cur], kA[cur]], [bk_])
                yield
                CP("act", A_[nxt][:], ap_, [ak], [kA[nxt]])
                if lv < 6:
                    CP("dve", B_[nxt][:], bp_, [bk_], [kB[nxt]])
                yield
                pp_, pk = bslot(4 + b)
                MM(pp_, A_[nxt][:], Pm[cur][:], [kA[nxt], kP[cur]], [pk])
                yield
                if lv < 6:
                    TTN("dve", Pm[nxt][:], pp_, Pm[cur][:], ALU.add, [pk, kP[cur]], [kP[nxt]])
                else:
                    TTN("dve", TTb[:], pp_, Pm[cur][:], ALU.add, [pk, kP[cur]], [kh_(6)])
                cur = nxt
            yield
            up_, uk = bslot(4 + b)
            MM(up_, TTb[:], vb[:], [kh_(6), kh_(3)], [uk])
            wp_, wk = bslot(4 + b)
            MM(wp_, kbg[:], TTb[:], [kh_(1), kh_(6)], [wk])
            yield
            CP("act", dec[:], up_, [uk], [kf_(7)])
            CP("act", wT[:], wp_, [wk], [kh_(8)])

        rr([pre_block(b) for b in range(NB)] + [nxt_front, tail])

        ops_ = {}
        for b in range(NB):
            f, hh, kf_, kh_ = blk_tiles(b)
            u_sb = f(7)
            kdec, attT, wT, qdT = hh(2), hh(7), hh(8), hh(9)
            vnew = hh(10)
            wsp, wsk = bslot(4 + b)
            MM(wsp, wT[:], Sg_b[:, h, :], [kh_(8), ("Sg_b", h)], [wsk])
            TTN("dve", vnew[:], u_sb[:], wsp, ALU.subtract, [kf_(7), wsk], [kh_(10)])
            op_, ok = bslot(4 + b)
            MM(op_, qdT[:], Sg_b[:, h, :], [kh_(9), ("Sg_b", h)], [ok], start=True, stop=False)
            MM(op_, attT[:], vnew[:], [kh_(7), kh_(10)], [ok], start=False, stop=True)
            dsp, dsk = bslot(4 + b)
            MM(dsp, kdec[:], vnew[:], [kh_(2), kh_(10)], [dsk])
            STT(Sg_b[:, h, :], Sg[:, h, :], el[:, b, h:h + 1], dsp, ALU.mult, ALU.add, [("Sg", h), "g_el", dsk], [("Sg_b", h)])
            STT(Sg[:, h, :], Sg[:, h, :], el[:, b, h:h + 1], dsp, ALU.mult, ALU.add, [("Sg", h), "g_el", dsk], [("Sg", h)])
            ops_[b] = (op_, ok)
        for b in range(NB):
            blk = slice(b * 128, (b + 1) * 128)
            f, hh, kf_, kh_ = blk_tiles(b)
            sm = T("blk%d_sm" % b, [128, 16])
            smk = lambda i: ("blk_sm", b, i)
            ysb = hh(11)
            op_, ok = ops_[b]
            ACT(junk[:], op_, AF.Square, [ok], ["junk", smk(5)], accum_out=sm[:, 5:6])
            STT(sm[:, 6:7], sm[:, 5:6], 4.0 / 128.0, sm[:, 4:5], ALU.mult, ALU.add, [smk(5), smk(4)], [smk(6)])
            TTN("pool", sm[:, 6:7], sm[:, 6:7], mhalf[:], ALU.pow, [smk(6), "mhalf"], [smk(6)])
            STT(ysb[:], op_, sm[:, 6:7], zbs[:, b, :], ALU.mult, ALU.mult, [ok, smk(6), zname], [kh_(11)])

        def tail_gen():
            yield
            yield
            yield
            for b in range(NB):
                blk = slice(b * 128, (b + 1) * 128)
                f, hh, kf_, kh_ = blk_tiles(b)
                yp_, yk = bslot(4 + b, bf=True)
                TR(yp_, hh(11)[:], ident_b[:], [kh_(11), "ident_b"], [yk])
                yield
                CP("act", yT[:, 8 + h, blk], yp_, [yk], ["yT"])
                yield
        return tail_gen()

    def unit_H(h, par, tail=None):
        tf = T("w512_2", [128, TT])[:].rearrange("p (b n) -> p b n", n=128)
        kf = T("w512_3", [128, TT])[:].rearrange("p (b n) -> p b n", n=128)
        lf = T("w512_4", [128, TT])[:].rearrange("p (b n) -> p b n", n=128)
        v_b = T("h_v", [128, NB, 128], BF16)
        zs = T("w512_5", [128, TT])[:].rearrange("p (b n) -> p b n", n=128)
        tq = T("w512_0", [128, TT])
        qs = T("w512_1", [128, TT])
        proj_fm(par, 0, 0)
        for b in range(NB):
            for c in range(8):
                MM(banks[1 + b][:, 0:384], xT[:, c, b * 128:(b + 1) * 128], Wh[par][:, c, 128:512], ["xT", ("Wh", par)],
                   [bk(1 + b)], start=(c == 0), stop=(c == 7))
        ACT(tq[:], banks[0][:], AF.Tanh, [bk(0)], ["w512_0"], scale=0.5)
        STT(qs[:], tq[:], 1.0, banks[0][:], ALU.add, ALU.mult, ["w512_0", bk(0)], ["w512_1"])
        for b in range(NB):
            ACT(tf[:, b, :], banks[1 + b][:, 0:128], AF.Tanh, [bk(1 + b)], [("w512_2", b)], scale=0.5)
            CP("act", v_b[:, b, :], banks[1 + b][:, 128:256], [bk(1 + b)], [("h_v", b)])
            ACT(zs[:, b, :], banks[1 + b][:, 256:384], AF.Tanh, [bk(1 + b)], [("w512_5", b)], scale=0.5)
            STT(zs[:, b, :], zs[:, b, :], 1.0, banks[1 + b][:, 256:384], ALU.add, ALU.mult, [("w512_5", b), bk(1 + b)],
                [("w512_5", b)])
            TTN("dve", zs[:, b, :], zs[:, b, :], gb4[1][:, b, :], ALU.mult, [("w512_5", b), "gb4"], [("w512_5", b)])
            STT(kf[:, b, :], tf[:, b, :], -1.0, nhoml[:, h * 128:(h + 1) * 128], ALU.add, ALU.mult,
                [("w512_2", b), "nhoml"], [("w512_3", b)])
        for b in range(NB):
            ACT(lf[:, b, :], kf[:, b, :], AF.Ln, [("w512_3", b)], [("w512_4", b)], scale=-1.0, bias=1.0)
        lnh = T("lnhalf", [128, 1])
        junk = T("junk", [128, 128])

        def htiles(b):
            f = lambda i: T("blk%d_f%d" % (b, i), [128, 128])
            hh = lambda i: T("blk%d_h%d" % (b, i), [128, 128], BF16)
            kf_ = lambda i: "blk%d_f%d" % (b, i)
            kh_ = lambda i: "blk%d_h%d" % (b, i)
            return f, hh, kf_, kh_

        def pre_h(b):
            blk = slice(b * 128, (b + 1) * 128)
            f, hh, kf_, kh_ = htiles(b)
            enb, ebT = f(0), f(1)
            kt, ktT, attT = hh(0), hh(1), hh(2)
            q2 = q2s[b]
            ebl = T("blk%d_ebl" % b, [128, 2])
            bp_, bk_ = bslot(HB[b])
            MM(bp_, m_bd[:], lf[:, b, :], ["m_bd", ("w512_4", b)], [bk_])
            btp, btk = bslot(HB[b])
            MM(btp, lf[:, b, :], m_bd[:], ["m_bd", ("w512_4", b)], [btk])
            yield
            ACT(enb[:], bp_, AF.Exp, [bk_], [kf_(0)], scale=-1.0)
            ACT(ebT[:], btp, AF.Exp, [btk, "lnhalf"], [kf_(1)], bias=lnh[:, 0:1])
            ACT(ebl[:, 0:1], btp[:, 63:64], AF.Exp, [btk], [("ebl", b)])
            ACT(ebl[:, 1:2], btp[:, 127:128], AF.Exp, [btk], [("ebl", b)])
            yield
            TTN("dve", kt[:], kf[:, b, :], enb[:], ALU.mult, [("w512_3", b), kf_(0)], [kh_(0)])
            q2v = q2[:].rearrange("p (c x) -> p c x", x=192)[:, :, 0:64]
            TTN("dve", q2v, qs[:, blk].rearrange("p (c x) -> p c x", x=64), ebT[:].rearrange("p (c x) -> p c x", x=64),
                ALU.mult, ["w512_1", kf_(1)], [("q2", b)])
            yield
            ktp, ktk = bslot(HB[b], bf=True)
            TR(ktp, kt[:], ident_b[:], [kh_(0), "ident_b"], [ktk])
            d0p, d0k = bslot(HB[b])
            MM(d0p, kt[0:64, :], v_b[0:64, b, :], [kh_(0), ("h_v", b)], [d0k])
            d1p, d1k = banks[1 + b][:, 128:256], ("slot", 1 + b, 1)
            MM(d1p, kt[64:128, :], v_b[64:128, b, :], [kh_(0), ("h_v", b)], [d1k])
            yield
            G0, G1 = f(2), f(3)
            TS("dve", G0[:], d0p, ebl[:, 0:1], None, ALU.mult, None, [d0k, ("ebl", b)], [kf_(2)])
            TS("dve", G1[:], d1p, ebl[:, 1:2], None, ALU.mult, None, [d1k, ("ebl", b)], [kf_(3)])
            CP("act", ktT[:], ktp, [ktk], [kh_(1)])
            yield
            atp, atk = bslot(HB[b])
            MM(atp, ktT[:], q2v, [kh_(1), ("q2", b)], [atk])
            yield
            TTN("dve", attT[:], atp, m_bd[:], ALU.mult, [atk, "m_bd"], [kh_(2)])
            yield
            MM(banks[1 + b][:, 0:128], attT[:], v_b[:, b, :], [kh_(2), ("h_v", b)], [bk(1 + b)], start=True, stop=False)

        rr([pre_h(b) for b in range(NB)] + [tail])

        Pp = [T("h_P0", [128, 128]), T("h_P1", [128, 128])]
        cur, curk = Sh[:, h, :], ("Sh", h)
        idx = 0
        for b in range(NB):
            f, hh, kf_, kh_ = htiles(b)
            q2 = q2s[b]
            ebl = T("blk%d_ebl" % b, [128, 2])
            op_, ok = banks[1 + b][:, 0:128], bk(1 + b)
            for cc in range(2):
                if idx == 0:
                    sbf, sbk = Sh_b[:, h, :], ("Sh_b", h)
                else:
                    sbf, sbk = hh(4 + cc)[:], kh_(4 + cc)
                    CP("act", sbf, cur, [curk], [sbk])
                MM(op_, q2[:, cc * 128:(cc + 1) * 128], sbf, [("q2", b), sbk], [ok], start=False, stop=(cc == 1))
                if idx == 2 * NB - 1:
                    nxt, nxtk = Sh[:, h, :], ("Sh", h)
                else:
                    nxt, nxtk = Pp[idx % 2][:], "h_P%d" % (idx % 2)
                STT(nxt, cur, ebl[:, cc:cc + 1], f(2 + cc)[:], ALU.mult, ALU.add, [curk, ("ebl", b), kf_(2 + cc)], [nxtk])
                cur, curk = nxt, nxtk
                idx += 1
        CP("act", Sh_b[:, h, :], Sh[:, h, :], [("Sh", h)], [("Sh_b", h)])
        for b in range(NB):
            blk = slice(b * 128, (b + 1) * 128)
            f, hh, kf_, kh_ = htiles(b)
            ysb = hh(3)
            sm = T("blk%d_sm" % b, [128, 16])
            smk = lambda i: ("blk_sm", b, i)
            op_, ok = banks[1 + b][:, 0:128], bk(1 + b)
            ACT(junk[:], op_, AF.Square, [ok], ["junk", smk(0)], accum_out=sm[:, 0:1])
            TS("dve", sm[:, 1:2], sm[:, 0:1], 4.0 / 128.0, 4 * LN_EPS, ALU.mult, ALU.add, [smk(0)], [smk(1)])
            TTN("pool", sm[:, 1:2], sm[:, 1:2], mhalf[:], ALU.pow, [smk(1), "mhalf"], [smk(1)])
            STT(ysb[:], op_, sm[:, 1:2], zs[:, b, :], ALU.mult, ALU.mult, [ok, smk(1), ("w512_5", b)], [kh_(3)])

        def tail_gen():
            yield
            yield
            for b in range(NB):
                blk = slice(b * 128, (b + 1) * 128)
                f, hh, kf_, kh_ = htiles(b)
                yp_, yk = bslot(HB[b], bf=True)
                TR(yp_, hh(3)[:], ident_b[:], [kh_(3), "ident_b"], [yk])
                yield
                CP("act", yT[:, h, blk], yp_, [yk], ["yT"])
                yield
        return tail_gen()

    lnh = T("lnhalf", [128, 1])
    P.add("pool", lambda e: e.memset(lnh[:], float(np.log(0.5))), (), ["lnhalf"])
    HB = [5, 6, 7, 0]
    q2s = [T("h_q2_%d" % b, [128, 384], BF16) for b in range(NB)]
    for b in range(NB):
        P.add("pool", lambda e, b=b: e.memset(q2s[b][:], 0.0), (), [("q2", b)])

    for t in range(NT):
        DMA("sp", xres[:], x_d[t * TT:(t + 1) * TT, :].rearrange("(b p) f -> p b f", p=128), c_x, [], ["xres"])
        if marks is not None:
            marks.append(('tile_start', len(P.ops)))
        make_xT(0)
        if marks is not None:
            marks.append(('xT_end', len(P.ops)))
        par = load_unit(0, 0)
        gdn_gates()
        if marks is not None:
            marks.append(('gates_end', len(P.ops)))
        for u in range(8):
            npar = load_unit(0, u + 1)
            unit_A(u, par)
            par = npar
        pars = {8: par, 9: load_unit(0, 9)}
        rr([front_B(0, pars[8])])
        tail = None
        for h in range(8):
            if h + 2 < 8:
                pars[8 + h + 2] = load_unit(0, 8 + h + 2)
            nf = front_B(h + 1, pars[8 + h + 1]) if h < 7 else None
            tail = unit_B(h, nf, tail)
        rr([tail])
        epilogue(0, t)
        if n_layers > 1:
            make_xT(2)
            par = load_unit(1, 0)
            tail = None
            for u in range(16):
                npar = load_unit(1, u + 1) if u < 15 else None
                tail = unit_H(u, par, tail)
                par = npar
            rr([tail])
            epilogue(1, t)
        DMA("act", out_d[t * TT:(t + 1) * TT, :].rearrange("(b p) f -> p b f", p=128), xres[:], c_out, ["xres"], ["out"])

    if max_ops is not None:
        P.ops = P.ops[:max_ops]
    P.finish()
    P.emit()
    nc_allow.__exit__(None, None, None)
    es.close()
    return nc, len(P.ops)


_KEYS = ["x", "p", "w_in_even", "conv_a_w", "conv_b_w", "a_log", "dt_bias", "gdn_norm_g", "w_out_even", "w_in_odd",
         "lower_bounds", "hgrn_norm_g", "w_out_odd", "ln_g", "ln_b", "w_pl", "w_pl_gate"]


def make_in_maps(inputs, n_cores, S):
    f = lambda a: np.ascontiguousarray(np.asarray(a, dtype=np.float32))
    shared = {
        "w_in_even": f(inputs["w_in_even"][0]), "conv_a_w": f(inputs["conv_a_w"][0]),
        "conv_b_w": f(inputs["conv_b_w"][0]), "a_log": f(inputs["a_log"]), "dt_bias": f(inputs["dt_bias"]),
        "gdn_norm_g": f(inputs["gdn_norm_g"]), "w_out_even": f(inputs["w_out_even"][0]),
        "w_in_odd": f(inputs["w_in_odd"][0]), "lower_bounds": f(inputs["lower_bounds"]),
        "hgrn_norm_g": f(inputs["hgrn_norm_g"]), "w_out_odd": f(inputs["w_out_odd"][0]),
        "ln_g": f(inputs["ln_g"]), "ln_b": f(inputs["ln_b"]), "w_pl": f(inputs["w_pl"]),
        "w_pl_gate": f(inputs["w_pl_gate"]),
    }
    maps = []
    for b in range(n_cores):
        m = dict(shared)
        m["x"] = f(inputs["x"][b, :S])
        m["p"] = f(inputs["p"][:, b, :S])
        maps.append(m)
    return maps


def kernel(**inputs):
    S = inputs["x"].shape[1]
    nb = inputs["x"].shape[0]
    nc, _ = build(S=S)
    in_maps = make_in_maps(inputs, nb, S)
    res = run_bass_kernel_spmd(nc, in_maps, core_ids=list(range(nb)))
    return np.stack([np.asarray(r["out"], dtype=np.float32) for r in res.results], axis=0)
```
